# Optimizing a Trainium2 kernel written in Bass

```python
import jax, jax.numpy as jnp
from jax import lax
import numpy as np

D_MODEL = 1024
BATCH = 16
SEQ = 2048
DEPTH = 4

N_BRANCH = 4
MIX_W = D_MODEL // 2
RW_N = 64
RW_H = MIX_W // RW_N
RW_RANK_W = 64
RW_RANK_A = 64
RW_RANK_G = 128
RW_GN_EPS = 64e-5
RW_COLS = 3 * MIX_W + RW_RANK_W + RW_RANK_A + RW_RANK_G
ML_H = 4
ML_DV = MIX_W // ML_H
ML_DK = ML_DV // 2
ML_CHUNK = 64
ML_COLS = 2 * ML_H * ML_DK + 2 * MIX_W + 4 * ML_H
LRU_BLOCKS = 8
LRU_BW = MIX_W // LRU_BLOCKS
LRU_CONV = 4
RG_C = 8.0
LRU_COLS = 2 * MIX_W
HG_D = 128
HG_H = MIX_W // HG_D
HG_CHUNK = 64
HG_COLS = 5 * MIX_W
D_FF = 2816
FFN_CONV = 3

NEG_BIG = -1e30

IN_SPLITS = (RW_COLS, ML_COLS, LRU_COLS, HG_COLS, N_BRANCH * D_MODEL)
N_IN = sum(IN_SPLITS)

kernel_name = "hybrid_rwkv7_mlstm_rglru_hgrn2_encoder"


def _split(t, sizes):
    idx = [int(s) for s in np.cumsum(sizes)[:-1]]
    return jnp.split(t, idx, axis=-1)


def _rmsnorm(x, g, eps=1e-6):
    xf = x.astype(jnp.float32)
    return xf * lax.rsqrt(jnp.mean(xf * xf, axis=-1, keepdims=True) + eps) * g


def _head_rmsnorm(y, g, eps=1e-6):
    yf = y.astype(jnp.float32)
    yf = yf * lax.rsqrt(jnp.mean(yf * yf, axis=-1, keepdims=True) + eps)
    return yf.reshape(*y.shape[:-2], -1) * g


def _dwconv(x, w, b):
    k = w.shape[0]
    left = (k - 1) // 2
    s = x.shape[1]
    xp = jnp.pad(x, ((0, 0), (left, k - 1 - left), (0, 0)))
    out = xp[:, 0:s] * w[0]
    for j in range(1, k):
        out = out + xp[:, j:j + s] * w[j]
    return out + b


def _rwkv7_scan(r, logw, a, b, k, v):
    bsz, _, nh, n = r.shape

    def step(state, inp):
        r_t, lw_t, a_t, b_t, k_t, v_t = inp
        sa = jnp.einsum("bhvk,bhk->bhv", state, a_t)
        state = (state * jnp.exp(lw_t)[:, :, None, :]
                 + sa[..., None] * b_t[:, :, None, :]
                 + v_t[..., None] * k_t[:, :, None, :])
        return state, jnp.einsum("bhvk,bhk->bhv", state, r_t)

    xs = tuple(jnp.moveaxis(t, 1, 0) for t in (r, logw, a, b, k, v))
    _, y = lax.scan(step, jnp.zeros((bsz, nh, n, n), jnp.float32), xs)
    return jnp.moveaxis(y, 0, 1)


def _rwkv7_branch(p, mu, w0, w2, a0, a2, g2, k_k, k_a, r_k, ln_w, ln_b):
    bsz, s, _ = p.shape
    pf = p.astype(jnp.float32)
    prev = jnp.pad(pf, ((0, 0), (1, 0), (0, 0)))[:, :-1]
    nxt = jnp.pad(pf, ((0, 0), (0, 1), (0, 0)))[:, 1:]
    pf = pf + mu * (0.5 * (prev + nxt) - pf)
    r, k, v, xw, xa, xg = _split(pf, [MIX_W, MIX_W, MIX_W, RW_RANK_W, RW_RANK_A, RW_RANK_G])
    heads = lambda t: t.reshape(bsz, s, RW_H, RW_N)
    kk = heads(k * k_k)
    kk = kk * lax.rsqrt(jnp.maximum(jnp.sum(kk * kk, axis=-1, keepdims=True), 1e-24))
    r_h, v_h = heads(r), heads(v)
    ys, kts = [], []
    for d in range(2):
        w = -jax.nn.softplus(-(w0[d] + jnp.tanh(xw) @ w2[d])) - 0.5
        a = jax.nn.sigmoid(a0[d] + xa @ a2[d])
        kt = heads(k * (1.0 + (a - 1.0) * k_a))
        args = [r_h, heads(-jnp.exp(w)), -kk, kk * heads(a), kt, v_h]
        if d == 1:
            args = [jnp.flip(t, 1) for t in args]
        yd = _rwkv7_scan(*args)
        ys.append(yd if d == 0 else jnp.flip(yd, 1))
        kts.append(kt)
    y = ys[0] + ys[1]
    mean = jnp.mean(y, axis=-1, keepdims=True)
    var = jnp.mean(jnp.square(y - mean), axis=-1, keepdims=True)
    y = ((y - mean) * lax.rsqrt(var + RW_GN_EPS)).reshape(bsz, s, MIX_W) * ln_w + ln_b
    k_bonus = 0.5 * (kts[0] + kts[1])
    bonus = jnp.sum(r_h * k_bonus * r_k, axis=-1, keepdims=True) * v_h
    y = y + bonus.reshape(bsz, s, MIX_W)
    g = jax.nn.sigmoid(xg) @ g2
    return y * g


def _mlstm_chunkwise(q, k, v, i_pre, f_pre):
    bsz, nh, s, dk = q.shape
    dv = v.shape[-1]
    lc = ML_CHUNK
    nc = s // lc
    q = (q * dk ** -0.5).reshape(bsz, nh, nc, lc, dk)
    k = k.reshape(bsz, nh, nc, lc, dk)
    v = v.reshape(bsz, nh, nc, lc, dv)
    ig = i_pre.reshape(bsz, nh, nc, lc)
    bcum = jnp.cumsum(jax.nn.log_sigmoid(f_pre).reshape(bsz, nh, nc, lc), axis=-1)
    g_end = bcum[..., -1]
    a_end = g_end[..., None] - bcum + ig

    def step(carry, inp):
        c_st, n_st, m_st = carry
        k_c, v_c, a_c, g_c = inp
        m_new = jnp.maximum(g_c + m_st, jnp.max(a_c, axis=-1))
        w_c = jnp.exp(a_c - m_new[..., None])
        dec = jnp.exp(g_c + m_st - m_new)
        c_new = dec[..., None, None] * c_st + jnp.einsum("bhl,bhlk,bhlv->bhkv", w_c, k_c, v_c)
        n_new = dec[..., None] * n_st + jnp.einsum("bhl,bhlk->bhk", w_c, k_c)
        return (c_new, n_new, m_new), (c_st, n_st, m_st)

    init = (jnp.zeros((bsz, nh, dk, dv), jnp.float32),
            jnp.zeros((bsz, nh, dk), jnp.float32),
            jnp.zeros((bsz, nh), jnp.float32))
    xs = (jnp.moveaxis(k, 2, 0), jnp.moveaxis(v, 2, 0),
          jnp.moveaxis(a_end, 2, 0), jnp.moveaxis(g_end, 2, 0))
    _, (c_prev, n_prev, m_prev) = lax.scan(step, init, xs)
    c_prev = jnp.moveaxis(c_prev, 0, 2)
    n_prev = jnp.moveaxis(n_prev, 0, 2)
    m_prev = jnp.moveaxis(m_prev, 0, 2)
    lower = jnp.tril(jnp.ones((lc, lc), dtype=bool))
    log_d = jnp.where(lower, bcum[..., :, None] - bcum[..., None, :] + ig[..., None, :], NEG_BIG)
    m_inter = bcum + m_prev[..., None]
    m_t = jnp.maximum(m_inter, jnp.max(log_d, axis=-1))
    pw = (jnp.where(lower, jnp.exp(log_d - m_t[..., None]), 0.0)
          * jnp.einsum("bhcld,bhcsd->bhcls", q, k))
    s_inter = jnp.exp(m_inter - m_t)
    num = (s_inter[..., None] * jnp.einsum("bhcld,bhcdv->bhclv", q, c_prev)
           + jnp.einsum("bhcls,bhcsv->bhclv", pw, v))
    den = s_inter * jnp.einsum("bhcld,bhcd->bhcl", q, n_prev) + jnp.sum(pw, axis=-1)
    h = num / jnp.maximum(jnp.abs(den), jnp.exp(-m_t))[..., None]
    return h.reshape(bsz, nh, s, dv)


def _mlstm_branch(p, i_bias, f_bias, norm_g):
    bsz, s, _ = p.shape
    q, k, v, o, ig, fg = _split(p.astype(jnp.float32),
                                [ML_H * ML_DK, ML_H * ML_DK, MIX_W, MIX_W, 2 * ML_H, 2 * ML_H])
    heads = lambda t, d: t.reshape(bsz, s, ML_H, d).transpose(0, 2, 1, 3)
    q, k, v = heads(q, ML_DK), heads(k, ML_DK), heads(v, ML_DV)
    ig = (ig.reshape(bsz, s, 2, ML_H) + i_bias).transpose(2, 0, 3, 1)
    fg = (fg.reshape(bsz, s, 2, ML_H) + f_bias).transpose(2, 0, 3, 1)
    fl = lambda t: jnp.flip(t, 2)
    h = (_mlstm_chunkwise(q, k, v, ig[0], fg[0])
         + fl(_mlstm_chunkwise(fl(q), fl(k), fl(v), fl(ig[1]), fl(fg[1]))))
    h = h.transpose(0, 2, 1, 3)
    return jax.nn.sigmoid(o) * _head_rmsnorm(h, norm_g)


def _lin_comb(e1, e2):
    a1, b1 = e1
    a2, b2 = e2
    return a1 * a2, a2 * b1 + b2


def _rglru_branch(p, conv_w, conv_b, wa, ba, wx, bx, lam):
    bsz, s, _ = p.shape
    xb, gb = _split(p.astype(jnp.float32), [MIX_W, MIX_W])
    xc = _dwconv(xb, conv_w, conv_b)
    xblk = xc.reshape(bsz, s, LRU_BLOCKS, LRU_BW)
    y = None
    for d in range(2):
        ra = jnp.einsum("bsgi,gij->bsgj", xblk, wa[d]).reshape(bsz, s, MIX_W) + ba[d]
        rx = jnp.einsum("bsgi,gij->bsgj", xblk, wx[d]).reshape(bsz, s, MIX_W) + bx[d]
        log_a = -RG_C * jax.nn.softplus(-lam[d]) * jax.nn.sigmoid(ra)
        a = jnp.exp(log_a)
        mult = jnp.sqrt(jnp.maximum(-jnp.expm1(2.0 * log_a), 0.0))
        bin_ = mult * (jax.nn.sigmoid(rx) * xc)
        _, hd = lax.associative_scan(_lin_comb, (a, bin_), axis=1, reverse=(d == 1))
        y = hd if y is None else y + hd
    return y * jax.nn.gelu(gb)


def _hgrn2_chunkwise(q, k, v, logf):
    bsz, nh, s, dk = q.shape
    dv = v.shape[-1]
    lc = HG_CHUNK
    nc = s // lc
    chunks = lambda t: jnp.moveaxis(t.reshape(bsz, nh, nc, lc, t.shape[-1]), 2, 0)
    lower = jnp.tril(jnp.ones((lc, lc), dtype=bool))[:, :, None]

    def step(state, inp):
        q_c, k_c, v_c, lf_c = inp
        gc = jnp.cumsum(lf_c, axis=-2)
        diff = gc[:, :, :, None, :] - gc[:, :, None, :, :]
        dec = jnp.where(lower, jnp.exp(jnp.minimum(diff, 0.0)), 0.0)
        att = jnp.einsum("bhtd,bhsd,bhtsd->bhts", q_c, k_c, dec)
        o = (jnp.einsum("bhts,bhsv->bhtv", att, v_c)
             + jnp.einsum("bhtd,bhdv->bhtv", q_c * jnp.exp(gc), state))
        g_end = gc[:, :, -1:, :]
        state = (jnp.exp(g_end[:, :, 0, :, None]) * state
                 + jnp.einsum("bhsd,bhsv->bhdv", k_c * jnp.exp(g_end - gc), v_c))
        return state, o

    _, o = lax.scan(step, jnp.zeros((bsz, nh, dk, dv), jnp.float32),
                    (chunks(q), chunks(k), chunks(v), chunks(logf)))
    return jnp.moveaxis(o, 0, 2).reshape(bsz, nh, s, dv)


def _hgrn2_branch(p, lb, norm_g):
    bsz, s, _ = p.shape
    q, f_fw, f_bw, i, g = _split(p.astype(jnp.float32), [MIX_W] * 5)
    heads = lambda t: t.reshape(bsz, s, HG_H, HG_D).transpose(0, 2, 1, 3)
    qh, ih = heads(jax.nn.silu(q)), heads(i)
    fl = lambda t: jnp.flip(t, 2)
    outs = []
    for d, fp in enumerate((f_fw, f_bw)):
        f = lb[d] + (1.0 - lb[d]) * jax.nn.sigmoid(fp)
        logf = jnp.log(f)
        kd = (1.0 - lb[d]) * jax.nn.sigmoid(-fp)
        args = [qh, heads(kd), ih, heads(logf)]
        if d == 1:
            args = [fl(t) for t in args]
        od = _hgrn2_chunkwise(*args)
        outs.append(od if d == 0 else fl(od))
    o = (outs[0] + outs[1]).transpose(0, 2, 1, 3)
    return jax.nn.silu(g) * _head_rmsnorm(o, norm_g)


def _conv_ffn(u, w_up, cw, cb, w_down):
    z = _dwconv(u @ w_up, cw, cb)
    val, gate = jnp.split(z, 2, axis=-1)
    return (val * jax.nn.silu(gate)) @ w_down


def setup_inputs(seed: int = 0) -> dict:
    key = jax.random.key(seed)
    ks = iter(jax.random.split(key, 48))
    f32 = jnp.float32

    def nrm(shape, std=1.0):
        return std * jax.random.normal(next(ks), shape, f32)

    def unif(shape, lo, hi):
        return jax.random.uniform(next(ks), shape, f32, lo, hi)

    L, D, W = DEPTH, D_MODEL, MIX_W
    lam_s = unif((L, 2, W), 0.9, 0.999) ** (1.0 / RG_C)
    return {
        "x": nrm((BATCH, SEQ, D)),
        "c": nrm((BATCH, D)),
        "ada_w": nrm((L, D, 6 * D), D ** -0.5),
        "ada_b": nrm((L, 6 * D), 0.02),
        "norm1_g": 1.0 + nrm((L, D), 0.02),
        "w_in": nrm((L, D, N_IN), D ** -0.5),
        "rw_mu": unif((L, RW_COLS), 0.0, 1.0),
        "rw_w0": unif((L, 2, W), -5.0, 0.0),
        "rw_w2": nrm((L, 2, RW_RANK_W, W), 0.1 * RW_RANK_W ** -0.5),
        "rw_a0": nrm((L, 2, W), 0.1),
        "rw_a2": nrm((L, 2, RW_RANK_A, W), 0.1 * RW_RANK_A ** -0.5),
        "rw_g2": nrm((L, RW_RANK_G, W), RW_RANK_G ** -0.5),
        "rw_kk": 0.85 + nrm((L, W), 0.05),
        "rw_ka": 1.0 + nrm((L, W), 0.05),
        "rw_rk": nrm((L, RW_H, RW_N), 0.1),
        "rw_lnw": 1.0 + nrm((L, W), 0.02),
        "rw_lnb": nrm((L, W), 0.02),
        "ml_ibias": nrm((L, 2, ML_H), 0.5),
        "ml_fbias": unif((L, 2, ML_H), 3.0, 6.0),
        "ml_norm": 1.0 + nrm((L, W), 0.02),
        "lru_conv_w": nrm((L, LRU_CONV, W), LRU_CONV ** -0.5),
        "lru_conv_b": nrm((L, W), 0.02),
        "lru_wa": nrm((L, 2, LRU_BLOCKS, LRU_BW, LRU_BW), LRU_BW ** -0.5),
        "lru_ba": nrm((L, 2, W), 0.02),
        "lru_wx": nrm((L, 2, LRU_BLOCKS, LRU_BW, LRU_BW), LRU_BW ** -0.5),
        "lru_bx": nrm((L, 2, W), 0.02),
        "lru_lam": jnp.log(lam_s) - jnp.log1p(-lam_s),
        "hg_lb": nrm((L, 2, W)),
        "hg_norm": 1.0 + nrm((L, W), 0.02),
        "w_branch": nrm((L, N_BRANCH, W, D), W ** -0.5),
        "w_out": nrm((L, D, D), D ** -0.5),
        "norm2_g": 1.0 + nrm((L, D), 0.02),
        "ffn_up": nrm((L, D, 2 * D_FF), D ** -0.5),
        "ffn_conv_w": nrm((L, FFN_CONV, 2 * D_FF), FFN_CONV ** -0.5),
        "ffn_conv_b": nrm((L, 2 * D_FF), 0.02),
        "ffn_down": nrm((L, D_FF, D), D_FF ** -0.5),
        "final_g": 1.0 + nrm((D,), 0.02),
    }


def reference(x, c, ada_w, ada_b, norm1_g, w_in, rw_mu, rw_w0, rw_w2, rw_a0, rw_a2, rw_g2,
              rw_kk, rw_ka, rw_rk, rw_lnw, rw_lnb, ml_ibias, ml_fbias, ml_norm,
              lru_conv_w, lru_conv_b, lru_wa, lru_ba, lru_wx, lru_bx, lru_lam,
              hg_lb, hg_norm, w_branch, w_out, norm2_g, ffn_up, ffn_conv_w, ffn_conv_b,
              ffn_down, final_g):
    lb_soft = jax.nn.softmax(hg_lb.astype(jnp.float32), axis=0)
    lb_all = jnp.clip(jnp.cumsum(lb_soft, axis=0) - lb_soft[0], 0.0, 1.0)
    cond = jax.nn.silu(c.astype(jnp.float32))
    h = x.astype(jnp.float32)
    for l in range(DEPTH):
        mod = cond @ ada_w[l] + ada_b[l]
        sh1, sc1, gt1, sh2, sc2, gt2 = jnp.split(mod[:, None, :], 6, axis=-1)
        u = _rmsnorm(h, norm1_g[l]) * (1.0 + sc1) + sh1
        p_rw, p_ml, p_lru, p_hg, p_gate = _split(u @ w_in[l], IN_SPLITS)
        ys = (
            _rwkv7_branch(p_rw, rw_mu[l], rw_w0[l], rw_w2[l], rw_a0[l], rw_a2[l], rw_g2[l],
                          rw_kk[l], rw_ka[l], rw_rk[l], rw_lnw[l], rw_lnb[l]),
            _mlstm_branch(p_ml, ml_ibias[l], ml_fbias[l], ml_norm[l]),
            _rglru_branch(p_lru, lru_conv_w[l], lru_conv_b[l], lru_wa[l], lru_ba[l],
                          lru_wx[l], lru_bx[l], lru_lam[l]),
            _hgrn2_branch(p_hg, lb_all[l], hg_norm[l]),
        )
        gates = jnp.split(jax.nn.sigmoid(p_gate), N_BRANCH, axis=-1)
        merged = gates[0] * (ys[0] @ w_branch[l, 0])
        for n in range(1, N_BRANCH):
            merged = merged + gates[n] * (ys[n] @ w_branch[l, n])
        h = h + gt1 * (merged @ w_out[l])
        u2 = _rmsnorm(h, norm2_g[l]) * (1.0 + sc2) + sh2
        h = h + gt2 * _conv_ffn(u2, ffn_up[l], ffn_conv_w[l], ffn_conv_b[l], ffn_down[l])
    return _rmsnorm(h, final_g)
```

```python
import numpy as np
import concourse.bass as bass
import concourse.mybir as mybir
from concourse.bass_utils import run_bass_kernel_spmd
from contextlib import ExitStack

F32 = mybir.dt.float32
BF16 = mybir.dt.bfloat16
ALU = mybir.AluOpType
AF = mybir.ActivationFunctionType

D = 1024
T = 2048
L = 4
NSEQ = 2
NCORE = 8
W = 512
N_IN = 11024
O_RW, O_ML, O_LRU, O_HG, O_GATE = 0, 1792, 3344, 4368, 6928
DFF = 2816
class Track:
    def __init__(self, sem, step):
        self.sem = sem
        self.val = 0
        self.step = step


class Dep:
    __slots__ = ("w", "r")

    def __init__(self):
        self.w = None
        self.r = {}


class Eng:
    def __init__(self, name, h, tr, is_pe=False):
        self.name = name
        self.h = h
        self.tr = tr
        self.is_pe = is_pe
        self.seen = {}
        self.ops = []
        self.pool = []
        self.dma_i = 0


class Prog:
    def __init__(self, nc, es):
        self.nc = nc
        self.es = es
        self.tracks = []

        def mk(name, step=1):
            t = Track(es.enter_context(nc.semaphore(name)), step)
            self.tracks.append(t)
            return t

        self.pe = Eng("pe", nc.tensor, mk("s_pe"), True)
        self.act = Eng("act", nc.scalar, mk("s_act"))
        self.dve = Eng("dve", nc.vector, mk("s_dve"))
        self.pool = Eng("pool", nc.gpsimd, mk("s_pool"))
        self.sp = Eng("sp", nc.sync, mk("s_sp"))
        self.engs = [self.pe, self.act, self.dve, self.pool, self.sp]
        for e, n in ((self.sp, 16), (self.pool, 16), (self.act, 4)):
            e.pool = [mk("d_%s%d" % (e.name, i), 16) for i in range(n)]
        self.nps = 0
        self.psums = []
        self.n_ops = 0

    def sb(self, name, shape, dt=F32):
        return self.es.enter_context(self.nc.sbuf_tensor(name, list(shape), dt))

    def make_psums(self):
        for i in range(8):
            t = self.es.enter_context(self.nc.psum_tensor("ps%d" % i, [128, 512], F32))
            self.psums.append((t, Dep()))

    def psum(self):
        t = self.psums[self.nps % 8]
        self.nps += 1
        return t

    def _waits(self, eng, reads, writes, extra=()):
        need = {}

        def add(ev):
            if ev is None:
                return
            tr, v = ev
            if need.get(tr, 0) < v:
                need[tr] = v

        for d in reads:
            add(d.w)
        for d in writes:
            add(d.w)
            for tr, v in d.r.items():
                add((tr, v))
        for ev in extra:
            add(ev)
        for tr, v in need.items():
            if tr is eng.tr and eng.is_pe:
                continue
            if eng.seen.get(tr, 0) >= v:
                continue
            eng.seen[tr] = v
            eng.ops.append(("wait", tr.sem, v))

    def op(self, eng, fn, reads=(), writes=(), inc=True):
        self._waits(eng, reads, writes)
        val = eng.tr.val + 1
        if inc:
            eng.tr.val = val
        eng.ops.append(("op", fn, inc))
        self.n_ops += 1
        for d in reads:
            if d.r.get(eng.tr, 0) < val:
                d.r[eng.tr] = val
        for d in writes:
            d.w = (eng.tr, val)
            d.r = {}

    def dma(self, eng, out, in_, reads=(), writes=(), **kw):
        tr = eng.pool[eng.dma_i % len(eng.pool)]
        eng.dma_i += 1
        extra = [(tr, tr.val)] if tr.val > 0 else []
        self._waits(eng, reads, writes, extra)
        tr.val += 16
        eng.ops.append(("dma", out, in_, tr.sem, kw))
        self.n_ops += 1
        for d in reads:
            d.r[tr] = tr.val
        for d in writes:
            d.w = (tr, tr.val)
            d.r = {}

    def barrier(self):
        for e in self.engs:
            for tr in self.tracks:
                if tr.val > e.seen.get(tr, 0):
                    if tr is e.tr and e.is_pe:
                        continue
                    e.seen[tr] = tr.val
                    e.ops.append(("wait", tr.sem, tr.val))

    def finish(self):
        self.barrier()
        nc = self.nc

        def replay(e, h):
            for o in e.ops:
                if o[0] == "wait":
                    h.wait_ge(o[1], o[2])
                elif o[0] == "op":
                    ins = o[1](h)
                    if o[2]:
                        ins.then_inc(e.tr.sem, 1)
                else:
                    h.dma_start(out=o[1], in_=o[2], **o[4]).then_inc(o[3], 16)

        with nc.Block() as block:
            @block.tensor
            def _(h):
                replay(self.pe, h)

            @block.scalar
            def _(h):
                replay(self.act, h)

            @block.vector
            def _(h):
                replay(self.dve, h)

            @block.gpsimd
            def _(h):
                replay(self.pool, h)

            @block.sync
            def _(h):
                replay(self.sp, h)

PV_SPEC = [("ada_b", 48), ("n1g", 8), ("n2g", 8), ("rw_mu", 14), ("rw_w0", 8), ("rw_a0", 8), ("rw_kk", 4),
           ("rw_ka", 4), ("rw_rk", 4), ("rw_lnw", 4), ("rw_lnb", 4), ("ml_norm", 4), ("lru_cw", 16),
           ("lru_cb", 4), ("lru_ba", 8), ("lru_bx", 8), ("lru_lam", 8), ("hg_lb", 8), ("hg_norm", 4),
           ("ffn_cw", 132), ("ffn_cb", 44), ("ml_gb", 16)]
PV_L = sum(n for _, n in PV_SPEC)
PV_OFF = {}
_o = 0
for _n, _c in PV_SPEC:
    PV_OFF[_n] = _o
    _o += _c
NPV = PV_L * L + 8


def _fm(v):
    return np.ascontiguousarray(np.asarray(v, np.float32).reshape(-1, 128).T)


def pack_pv(inp):
    pv = np.zeros((128, NPV), np.float32)
    for l in range(L):
        ent = {
            "ada_b": _fm(inp["ada_b"][l]), "n1g": _fm(inp["norm1_g"][l]), "n2g": _fm(inp["norm2_g"][l]),
            "rw_mu": _fm(inp["rw_mu"][l]), "rw_w0": _fm(inp["rw_w0"][l].reshape(-1)),
            "rw_a0": _fm(inp["rw_a0"][l].reshape(-1)), "rw_kk": _fm(inp["rw_kk"][l]),
            "rw_ka": _fm(inp["rw_ka"][l]), "rw_rk": _fm(inp["rw_rk"][l].reshape(-1)),
            "rw_lnw": _fm(inp["rw_lnw"][l]), "rw_lnb": _fm(inp["rw_lnb"][l]), "ml_norm": _fm(inp["ml_norm"][l]),
            "lru_cw": _fm(inp["lru_conv_w"][l].reshape(-1)), "lru_cb": _fm(inp["lru_conv_b"][l]),
            "lru_ba": _fm(inp["lru_ba"][l].reshape(-1)), "lru_bx": _fm(inp["lru_bx"][l].reshape(-1)),
            "lru_lam": _fm(inp["lru_lam"][l].reshape(-1)), "hg_lb": _fm(inp["hg_lb"][l].reshape(-1)),
            "hg_norm": _fm(inp["hg_norm"][l]), "ffn_cw": _fm(inp["ffn_conv_w"][l].reshape(-1)),
            "ffn_cb": _fm(inp["ffn_conv_b"][l]),
            "ml_gb": np.tile(np.concatenate([inp["ml_ibias"][l].reshape(-1), inp["ml_fbias"][l].reshape(-1)])[None, :],
                             (128, 1)).astype(np.float32),
        }
        for n, c in PV_SPEC:
            assert ent[n].shape == (128, c), (n, ent[n].shape)
            pv[:, l * PV_L + PV_OFF[n]: l * PV_L + PV_OFF[n] + c] = ent[n]
    pv[:, PV_L * L:] = _fm(inp["final_g"])
    return pv


NCST = 20


def pack_consts():
    c = np.zeros((128, NCST, 128), np.float32)
    i = np.arange(128)
    c[:, 0, :] = np.eye(128)
    c[:, 1, :] = 1.0
    c[:, 2, :] = (i[:, None] <= i[None, :])
    c[:, 3, :] = (i[:, None] < i[None, :])
    c[:, 4, :] = (i[:, None] // 64 == i[None, :] // 64)
    c[:, 5, :] = (i[:, None] > i[None, :])
    for k in range(7):
        bsz = 1 << k
        t, s_ = i[:, None], i[None, :]
        m = (t // (2 * bsz) == s_ // (2 * bsz)) & (t % (2 * bsz) >= bsz) & (s_ % (2 * bsz) < bsz)
        c[:, 6 + k, :] = m
        c[:, 13 + k, :] = m.T
    return c


def pack_lru_bd(w):
    out = np.zeros((L, 2, 4, 128, 128), np.float32)
    for j in range(4):
        out[:, :, j, 0:64, 0:64] = w[:, :, 2 * j]
        out[:, :, j, 64:128, 64:128] = w[:, :, 2 * j + 1]
    return out


def build(n_layers=L, n_seq=NSEQ, mixers=(0, 1, 2, 3), taps=(), rw_limit=(4, 2, 16)):
    nc = bass.Bass("TRN2", target_bir_lowering=False)
    dt = lambda name, shape, kind="ExternalInput": nc.dram_tensor(name, list(shape), F32, kind=kind).ap()
    xT = dt("xT", [NSEQ, D, T])
    cT = dt("cT", [128, 8, NSEQ])
    pvd = dt("pv", [128, NPV])
    cst = dt("cst", [128, NCST, 128])
    ada_w = dt("ada_w", [L, D, 6 * D])
    w_in = dt("w_in", [L, D, N_IN])
    rw_w2 = dt("rw_w2", [L, 2, 64, W])
    rw_a2 = dt("rw_a2", [L, 2, 64, W])
    rw_g2 = dt("rw_g2", [L, 128, W])
    lru_wa = dt("lru_wa", [L, 2, 4, 128, 128])
    lru_wx = dt("lru_wx", [L, 2, 4, 128, 128])
    w_branch = dt("w_branch", [L, 4, W, D])
    w_out = dt("w_out", [L, D, D])
    ffn_up = dt("ffn_up", [L, D, 2 * DFF])
    ffn_down = dt("ffn_down", [L, DFF, D])
    outT = dt("outT", [NSEQ, D, T], "ExternalOutput")
    hd = dt("hd", [NSEQ, D, T], "Internal")
    tapd = {}
    for name, shape in taps:
        tapd[name] = dt("tap_" + name, shape, "ExternalOutput")

    es = ExitStack()
    P = Prog(nc, es)
    P.make_psums()
    pe, act, dve, pool, sp = P.pe, P.act, P.dve, P.pool, P.sp

    def mm(out, lhsT, rhs, start, stop, r, w, inc=None):
        P.op(pe, lambda h: h.matmul(out, lhsT, rhs, start=start, stop=stop), r, w, inc=stop if inc is None else inc)

    def actf(out, in_, func, r, w, bias=None, scale=None):
        kw = {}
        if bias is not None:
            kw["bias"] = bias
        if scale is not None:
            kw["scale"] = scale
        P.op(act, lambda h: h.activation(out, in_, func, **kw), r, w)

    def tt(out, a, b, op, r, w, eng=None):
        P.op(eng or dve, lambda h: h.tensor_tensor(out, a, b, op), r, w)

    def ts(out, a, s1, s2, op0, op1, r, w, eng=None):
        if op1 is None:
            P.op(eng or dve, lambda h: h.tensor_scalar(out, a, s1, None, op0), r, w)
        else:
            P.op(eng or dve, lambda h: h.tensor_scalar(out, a, s1, s2, op0, op1), r, w)

    def stt(out, a, s, b, op0, op1, r, w, eng=None):
        P.op(eng or dve, lambda h: h.scalar_tensor_tensor(out, a, s, b, op0, op1), r, w)

    def cp(out, in_, r, w, eng=None):
        e = eng or dve
        if e is act:
            P.op(act, lambda h: h.copy(out, in_), r, w)
        else:
            P.op(e, lambda h: h.tensor_copy(out, in_), r, w)

    def memset(ap, v, w, eng=None):
        P.op(eng or dve, lambda h: h.memset(ap, v), (), w)

    def recip(out, in_, r, w):
        P.op(dve, lambda h: h.reciprocal(out, in_), r, w)

    def scan(out, d0, d1, init, r, w):
        P.op(dve, lambda h: h.tensor_tensor_scan(out, d0, d1, init, ALU.mult, ALU.add), r, w)

    def tap(name, src_ap, r, dst=None):
        if name in tapd:
            P.dma(pool, tapd[name] if dst is None else dst, src_ap, reads=r)

    U = P.sb("U", [128, 8, T], BF16)
    dU = Dep()
    RR = P.sb("RR", [128, 45056], BF16)
    ACC = RR[:, 0:16384].bitcast(F32).rearrange("p (c t) -> p c t", c=4)
    Y = RR[:, 16384:24576].rearrange("p (c t) -> p c t", c=4)
    M = RR[:, 24576:40960].rearrange("p (c t) -> p c t", c=8)
    AFF = RR[:, 0:45056].rearrange("p (c t) -> p c t", c=22)
    dACC, dY, dM, dAFF = Dep(), Dep(), Dep(), Dep()
    SCB = 40960
    SC = P.sb("SC", [128, SCB // 2], BF16)

    def scv(off, n, dtype):
        assert off % 4 == 0
        if dtype is F32:
            assert off + 4 * n <= SCB, (off, n)
            return SC[:, off // 2: off // 2 + 2 * n].bitcast(F32)
        assert off + 2 * n <= SCB, (off, n)
        return SC[:, off // 2: off // 2 + n]

    NSLAB = 3
    slabs = [(P.sb("slab%d" % i, [128, 8, 512], BF16), Dep()) for i in range(NSLAB)]
    slab_i = [0]
    PV = P.sb("PV", [128, NPV], F32)
    dPV = Dep()
    CF = P.sb("CF", [128, 6, 128], F32)
    CB = P.sb("CB", [128, NCST, 128], BF16)
    dC = Dep()
    MOD = P.sb("MOD", [128, L * 48 * NSEQ], F32)
    dMOD = Dep()
    SV = P.sb("SV", [128, L * 32], F32)
    dSV = Dep()
    CND = P.sb("CND", [128, 8, NSEQ], F32)
    CNDB = P.sb("CNDB", [128, 8, NSEQ], BF16)
    dCND = Dep()
    LW = P.sb("LW", [128, 3584], BF16)
    dLW = Dep()
    W2s = LW[0:64, 0:1024].rearrange("p (d n) -> p d n", d=2)
    A2s = LW[64:128, 0:1024].rearrange("p (d n) -> p d n", d=2)
    G2s = LW[:, 1024:1536]
    WAs = LW[:, 1536:2560].rearrange("p (d j n) -> p d j n", d=2, j=4)
    WXs = LW[:, 2560:3584].rearrange("p (d j n) -> p d j n", d=2, j=4)

    IDf, ONf, LEf, LTf, BKf = (CF[:, i, :] for i in range(5))
    IDb, ONb, LEb, LTb, BKb = (CB[:, i, :] for i in range(5))
    GTf = CF[:, 5, :]
    LMb = [CB[:, 6 + k, :] for k in range(7)]
    LMTb = [CB[:, 13 + k, :] for k in range(7)]

    def pv(name, l, c0=0, n=1):
        o = l * PV_L + PV_OFF[name] + c0
        return PV[:, o:o + n]

    def mod(l, k, c, sq):
        o = ((l * 6 + k) * 8 + c) * NSEQ + sq
        return MOD[:, o:o + 1]

    def load_slab(W2d, k0, nkc, col0, ncols, dst_col=0, new=True):
        if new:
            slab_i[0] += 1
        sl, sd = slabs[slab_i[0] % NSLAB]
        src = W2d[k0 * 128:(k0 + nkc) * 128, col0:col0 + ncols].rearrange("(kc p) n -> p kc n", p=128)
        P.dma(pool, sl[:, 0:nkc, dst_col:dst_col + ncols], src, writes=[sd])
        return sl, sd

    def linear_fm(W2d, nkc, col0, ncols, xfn, xdeps, ntok, evac):
        for cg in range(0, ncols, 512):
            n = min(512, ncols - cg)
            sl, sd = load_slab(W2d, 0, nkc, col0 + cg, n)
            for c in range(0, n, 128):
                cw = min(128, n - c)
                for t0 in range(0, ntok, 512):
                    tn = min(512, ntok - t0)
                    ps, pd = P.psum()
                    for kc in range(nkc):
                        mm(ps[0:cw, 0:tn], sl[:, kc, c:c + cw], xfn(kc, t0, tn), kc == 0, kc == nkc - 1,
                           [sd] + xdeps, [pd])
                    evac(ps, pd, cg + c, cw, t0, tn)

    ufn = lambda kc, t0, tn: U[:, kc, t0:t0 + tn]

    P.dma(sp, PV[:], pvd[:, :], writes=[dPV])
    P.dma(sp, CF[:], cst[:, 0:6, :], writes=[dC])
    P.dma(pool, CB[:], cst[:, :, :], writes=[dC])
    P.dma(sp, CND[:], cT[:, :, :], writes=[dCND])
    actf(CND[:], CND[:], AF.Silu, [dCND], [dCND])
    cp(CNDB[:], CND[:], [dCND], [dCND])
    for l in range(n_layers):
        def ev_mod(ps, pd, col, cw, t0, tn, l=l):
            cidx = col // 128
            o = (l * 48 + cidx) * NSEQ
            ts(MOD[:, o:o + NSEQ], ps[:, 0:NSEQ], pv("ada_b", l, cidx), None, ALU.add, None, [pd, dPV], [dMOD])
        linear_fm(ada_w[l], 8, 0, 6 * D, lambda kc, t0, tn: CNDB[:, kc, :], [dCND], NSEQ, ev_mod)
    def sv(l, o, n=1):
        return SV[:, l * 32 + o: l * 32 + o + n]
    TS = scv(0, 64, F32)
    dTS = Dep()
    for l in range(n_layers):
        lam = pv("lru_lam", l, 0, 8)
        actf(TS[:, 0:8], lam, AF.Abs, [dPV], [dTS])
        actf(TS[:, 0:8], TS[:, 0:8], AF.Exp, [dTS], [dTS], scale=-1.0)
        actf(TS[:, 0:8], TS[:, 0:8], AF.Ln, [dTS], [dTS], bias=1.0)
        ts(TS[:, 8:16], lam, -1.0, 0.0, ALU.mult, ALU.max, [dPV], [dTS])
        tt(TS[:, 0:8], TS[:, 0:8], TS[:, 8:16], ALU.add, [dTS], [dTS])
        ts(sv(l, 0, 8), TS[:, 0:8], -8.0, None, ALU.mult, None, [dTS], [dSV])
        ts(sv(l, 8, 8), TS[:, 0:8], -16.0, None, ALU.mult, None, [dTS], [dSV])
    EX = scv(256, 32, F32)
    SM = scv(384, 8, F32)
    dEX = Dep()
    for l in range(L):
        actf(EX[:, l * 8:(l + 1) * 8], pv("hg_lb", l, 0, 8), AF.Exp, [dPV], [dEX])
    tt(SM[:], EX[:, 0:8], EX[:, 8:16], ALU.add, [dEX], [dEX])
    tt(SM[:], SM[:], EX[:, 16:24], ALU.add, [dEX], [dEX])
    tt(SM[:], SM[:], EX[:, 24:32], ALU.add, [dEX], [dEX])
    recip(SM[:], SM[:], [dEX], [dEX])
    for l in range(L):
        tt(EX[:, l * 8:(l + 1) * 8], EX[:, l * 8:(l + 1) * 8], SM[:], ALU.mult, [dEX], [dEX])
    for l in range(n_layers):
        if l == 0:
            memset(sv(0, 16, 8), 0.0, [dSV])
        elif l == 1:
            cp(sv(1, 16, 8), EX[:, 8:16], [dEX], [dSV])
        else:
            tt(sv(l, 16, 8), sv(l - 1, 16, 8), EX[:, l * 8:(l + 1) * 8], ALU.add, [dEX, dSV], [dSV])
    for l in range(n_layers):
        ts(sv(l, 16, 8), sv(l, 16, 8), 0.0, 1.0, ALU.max, ALU.min, [dSV], [dSV])
        ts(sv(l, 24, 8), sv(l, 16, 8), -1.0, 1.0, ALU.mult, ALU.add, [dSV], [dSV])
    P.barrier()

    GS = P.sb("GS", [128, 64], F32)
    dGS = Dep()

    def norm_mod(src, l, sq, which):
        gname, ksh, ksc = ("n1g", 0, 1) if which == 0 else ("n2g", 3, 4)
        if l >= 0:
            for c in range(8):
                stt(GS[:, which * 8 + c: which * 8 + c + 1], mod(l, ksc, c, sq), 1.0, pv(gname, l, c),
                    ALU.add, ALU.mult, [dMOD, dPV], [dGS])
        HT = scv(0, 4096, F32).rearrange("p (c t) -> p c t", c=8)
        SQ = [scv(16384 + i * 2048, 512, F32) for i in range(2)]
        RS = scv(20480, 512, F32)
        TM = [scv(22528 + i * 2048, 512, F32) for i in range(2)]
        dHT, dSQ, dRS, dTM = Dep(), [Dep(), Dep()], Dep(), [Dep(), Dep()]
        srcv = src.rearrange("(c p) t -> p c t", p=128)
        for tq in range(4):
            P.dma(sp, HT[:, :, :], srcv[:, :, tq * 512:(tq + 1) * 512], writes=[dHT])
            ps, pd = P.psum()
            for c in range(8):
                actf(SQ[c % 2][:], HT[:, c, :], AF.Square, [dHT], [dSQ[c % 2]])
                mm(ps[:, :], ONf, SQ[c % 2][:], c == 0, c == 7, [dSQ[c % 2], dC], [pd], inc=True)
            actf(RS[:], ps[:, :], AF.Sqrt, [pd], [dRS], bias=1e-6, scale=1.0 / D)
            recip(RS[:], RS[:], [dRS], [dRS])
            for c in range(8):
                tt(TM[c % 2][:], HT[:, c, :], RS[:], ALU.mult, [dHT, dRS], [dTM[c % 2]])
                if l >= 0:
                    actf(U[:, c, tq * 512:(tq + 1) * 512], TM[c % 2][:], AF.Identity, [dTM[c % 2], dGS, dMOD], [dU],
                         bias=mod(l, ksh, c, sq), scale=GS[:, which * 8 + c: which * 8 + c + 1])
                else:
                    ts(HT[:, c, :], TM[c % 2][:], PV[:, PV_L * L + c: PV_L * L + c + 1], None, ALU.mult, None,
                       [dTM[c % 2], dPV], [dHT])
            if l < 0:
                P.dma(sp, outT[sq].rearrange("(c p) t -> p c t", p=128)[:, :, tq * 512:(tq + 1) * 512], HT[:, :, :],
                      reads=[dHT])

    def gate_merge(l, n, first):
        SG = [scv(i * 2048, 512, F32) for i in range(2)]
        TP = [scv(4096 + i * 2048, 512, F32) for i in range(2)]
        dSG, dTP = [Dep(), Dep()], [Dep(), Dep()]
        k = 0
        for cg in range(2):
            sg_, sgd = load_slab(w_in[l], 0, 8, O_GATE + n * D + cg * 512, 512)
            sb_, sbd = load_slab(w_branch[l, n], 0, 4, cg * 512, 512)
            for c in range(4):
                cc = cg * 4 + c
                for tq in range(4):
                    tsl = slice(tq * 512, (tq + 1) * 512)
                    pg, pgd = P.psum()
                    for kc in range(8):
                        mm(pg[:, :], sg_[:, kc, c * 128:(c + 1) * 128], U[:, kc, tsl], kc == 0, kc == 7, [sgd, dU], [pgd])
                    py, pyd = P.psum()
                    for kc in range(4):
                        mm(py[:, :], sb_[:, kc, c * 128:(c + 1) * 128], Y[:, kc, tsl], kc == 0, kc == 3, [sbd, dY], [pyd])
                    i = k % 2
                    k += 1
                    actf(SG[i][:], pg[:, :], AF.Sigmoid, [pgd], [dSG[i]])
                    if first:
                        tt(M[:, cc, tsl], py[:, :], SG[i][:], ALU.mult, [pyd, dSG[i]], [dM])
                    else:
                        tt(TP[i][:], py[:, :], SG[i][:], ALU.mult, [pyd, dSG[i]], [dTP[i]])
                        tt(M[:, cc, tsl], M[:, cc, tsl], TP[i][:], ALU.add, [dTP[i], dM], [dM])

    def out_proj_residual(l, sq, src, dst):
        HT = scv(0, 4096, F32).rearrange("p (c t) -> p c t", c=8)
        dHT = Dep()
        s0, s0d = load_slab(w_out[l], 0, 8, 0, 512)
        s1, s1d = load_slab(w_out[l], 0, 8, 512, 512)
        srcv = src.rearrange("(c p) t -> p c t", p=128)
        dstv = dst.rearrange("(c p) t -> p c t", p=128)
        for tq in range(4):
            tsl = slice(tq * 512, (tq + 1) * 512)
            P.dma(sp, HT[:, :, :], srcv[:, :, tsl], writes=[dHT])
            for c2 in range(8):
                sl, sd = (s0, s0d) if c2 < 4 else (s1, s1d)
                ps, pd = P.psum()
                for kc in range(8):
                    mm(ps[:, :], sl[:, kc, (c2 % 4) * 128:(c2 % 4 + 1) * 128], M[:, kc, tsl], kc == 0, kc == 7, [sd, dM], [pd])
                stt(HT[:, c2, :], ps[:, :], mod(l, 2, c2, sq), HT[:, c2, :], ALU.mult, ALU.add, [pd, dHT, dMOD], [dHT])
            P.dma(sp, dstv[:, :, tsl], HT[:, :, :], reads=[dHT])

    def ffn(l, sq, hsrc):
        ZV = scv(0, 2050, F32)
        ZG = scv(8208, 2050, F32)
        CV = scv(16416, 2048, F32)
        CG = scv(24608, 2048, F32)
        dZV, dZG, dCV, dCG = Dep(), Dep(), Dep(), Dep()
        memset(ZV[:, 0:1], 0.0, [dZV])
        memset(ZV[:, 2049:2050], 0.0, [dZV])
        memset(ZG[:, 0:1], 0.0, [dZG])
        memset(ZG[:, 2049:2050], 0.0, [dZG])
        up = ffn_up[l]
        for j in range(22):
            sl, sd = load_slab(up, 0, 8, j * 128, 128, 0)
            load_slab(up, 0, 8, DFF + j * 128, 128, 128, new=False)
            for half, Z, dZ in ((0, ZV, dZV), (1, ZG, dZG)):
                for tq in range(4):
                    ps, pd = P.psum()
                    for kc in range(8):
                        mm(ps[:, :], sl[:, kc, half * 128:(half + 1) * 128], U[:, kc, tq * 512:(tq + 1) * 512],
                           kc == 0, kc == 7, [sd, dU], [pd])
                    cp(Z[:, 1 + tq * 512: 1 + (tq + 1) * 512], ps[:, :], [pd], [dZ], eng=act)
            for half, Z, dZ, C, dCx in ((0, ZV, dZV, CV, dCV), (1, ZG, dZG, CG, dCG)):
                jj = half * 22 + j
                cw = lambda tp: pv("ffn_cw", l, tp * 44 + jj)
                ts(C[:], Z[:, 1:2049], cw(1), pv("ffn_cb", l, jj), ALU.mult, ALU.add, [dZ, dPV], [dCx])
                stt(C[:], Z[:, 0:2048], cw(0), C[:], ALU.mult, ALU.add, [dZ, dCx], [dCx])
                stt(C[:], Z[:, 2:2050], cw(2), C[:], ALU.mult, ALU.add, [dZ, dCx], [dCx])
            actf(CG[:], CG[:], AF.Silu, [dCG], [dCG])
            tt(AFF[:, j, :], CV[:], CG[:], ALU.mult, [dCV, dCG], [dAFF])
        P.barrier()
        HT = scv(0, 2048, F32).rearrange("p (c t) -> p c t", c=4)
        dHT = Dep()
        hv = hsrc.rearrange("(c p) t -> p c t", p=128)
        for cg in range(2):
            sls = []
            for pi, (k0, nk) in enumerate(((0, 8), (8, 8), (16, 6))):
                sls.append(load_slab(ffn_down[l], k0, nk, cg * 512, 512) + (k0, nk))
            for tq in range(4):
                tsl = slice(tq * 512, (tq + 1) * 512)
                P.dma(sp, HT[:, :, :], hv[:, cg * 4:(cg + 1) * 4, tsl], writes=[dHT])
                for c2 in range(4):
                    ps, pd = P.psum()
                    for sl, sd, k0, nk in sls:
                        for kc in range(nk):
                            kk = k0 + kc
                            mm(ps[:, :], sl[:, kc, c2 * 128:(c2 + 1) * 128], AFF[:, kk, tsl], kk == 0, kk == 21,
                               [sd, dAFF], [pd])
                    stt(HT[:, c2, :], ps[:, :], mod(l, 5, cg * 4 + c2, sq), HT[:, c2, :], ALU.mult, ALU.add,
                        [pd, dHT, dMOD], [dHT])
                P.dma(sp, hv[:, cg * 4:(cg + 1) * 4, tsl], HT[:, :, :], reads=[dHT])

    def tk(t0, n, d):
        if d == 0:
            return slice(t0, t0 + n)
        a = T - 1 - t0
        b = a - n
        return slice(a, None if b < 0 else b, -1)

    def load_layer_small(l):
        P.dma(pool, W2s, rw_w2[l].rearrange("d p n -> p d n"), writes=[dLW])
        P.dma(pool, A2s, rw_a2[l].rearrange("d p n -> p d n"), writes=[dLW])
        P.dma(pool, G2s, rw_g2[l], writes=[dLW])
        P.dma(pool, WAs, lru_wa[l].rearrange("d j p n -> p d j n"), writes=[dLW])
        P.dma(pool, WXs, lru_wx[l].rearrange("d j p n -> p d j n"), writes=[dLW])

    def mixer_lru(l):
        XP = scv(0, 2051, F32)
        XC = scv(8208, 2048, F32)
        XCB = scv(16400, 2048, BF16)
        Ba = scv(20496, 2048, F32)
        Bm = scv(28688, 2048, F32)
        Bx, Bh, Bg, By = (ACC[:, i, :] for i in range(4))
        dXP, dXC, dXCB, dBa, dBm, dBx, dBh, dBg, dBy = (Dep() for _ in range(9))
        memset(XP[:, 0:1], 0.0, [dXP])
        memset(XP[:, 2049:2051], 0.0, [dXP])
        for j in range(4):
            linear_fm(w_in[l], 8, O_LRU + j * 128, 128, ufn, [dU], T,
                      lambda ps, pd, col, cw, t0, tn: cp(XP[:, 1 + t0:1 + t0 + tn], ps[:, 0:tn], [pd], [dXP], eng=act))
            linear_fm(w_in[l], 8, O_LRU + 512 + j * 128, 128, ufn, [dU], T,
                      lambda ps, pd, col, cw, t0, tn: actf(Bg[:, t0:t0 + tn], ps[:, 0:tn], AF.Gelu, [pd], [dBg]))
            cwv = lambda tp: pv("lru_cw", l, tp * 4 + j)
            ts(XC[:], XP[:, 0:T], cwv(0), pv("lru_cb", l, j), ALU.mult, ALU.add, [dXP, dPV], [dXC])
            for tp in range(1, 4):
                stt(XC[:], XP[:, tp:tp + T], cwv(tp), XC[:], ALU.mult, ALU.add, [dXP, dXC], [dXC])
            cp(XCB[:], XC[:], [dXC], [dXCB])
            for d in range(2):
                for tq in range(4):
                    tsl = slice(tq * 512, (tq + 1) * 512)
                    ps, pd = P.psum()
                    mm(ps[:, :], WAs[:, d, j, :], XCB[:, tsl], True, True, [dLW, dXCB], [pd])
                    actf(Ba[:, tsl], ps[:, :], AF.Sigmoid, [pd, dPV], [dBa], bias=pv("lru_ba", l, d * 4 + j))
                    ps, pd = P.psum()
                    mm(ps[:, :], WXs[:, d, j, :], XCB[:, tsl], True, True, [dLW, dXCB], [pd])
                    actf(Bx[:, tsl], ps[:, :], AF.Sigmoid, [pd, dPV], [dBx], bias=pv("lru_bx", l, d * 4 + j))
                actf(Bm[:], Ba[:], AF.Exp, [dBa, dSV], [dBm], scale=sv(l, 8 + d * 4 + j))
                actf(Ba[:], Ba[:], AF.Exp, [dBa, dSV], [dBa], scale=sv(l, d * 4 + j))
                ts(Bm[:], Bm[:], 1.0, -1.0, ALU.min, ALU.mult, [dBm], [dBm])
                actf(Bm[:], Bm[:], AF.Sqrt, [dBm], [dBm], bias=1.0)
                tt(Bx[:], Bx[:], XC[:], ALU.mult, [dBx, dXC], [dBx])
                tt(Bm[:], Bm[:], Bx[:], ALU.mult, [dBm, dBx], [dBm])
                if d == 0:
                    scan(By[:], Ba[:], Bm[:], 0.0, [dBa, dBm], [dBy])
                else:
                    scan(Bh[:, ::-1], Ba[:, ::-1], Bm[:, ::-1], 0.0, [dBa, dBm], [dBh])
                    tt(By[:], By[:], Bh[:], ALU.add, [dBy, dBh], [dBy])
            tt(Y[:, j, :], By[:], Bg[:], ALU.mult, [dBy, dBg], [dY])

    PADB = RR[:, 40960:45056]

    HN_DEPS = (Dep(), Dep())

    def head_rmsnorm_gate(l, gname, h, gate_mul):
        SQ = PADB[:, 0:1024].bitcast(F32)
        RS = PADB[:, 1024:2048].bitcast(F32)
        dSQ, dRS = HN_DEPS
        for tq in range(4):
            tsl = slice(tq * 512, (tq + 1) * 512)
            actf(SQ[:], ACC[:, h, tsl], AF.Square, [dACC], [dSQ])
            ps, pd = P.psum()
            mm(ps[:, :], ONf, SQ[:], True, True, [dSQ, dC], [pd])
            actf(RS[:], ps[:, :], AF.Sqrt, [pd], [dRS], bias=1e-6, scale=1.0 / 128)
            recip(RS[:], RS[:], [dRS], [dRS])
            tt(SQ[:], ACC[:, h, tsl], RS[:], ALU.mult, [dACC, dRS, dSQ], [dSQ])
            stt(Y[:, h, tsl], SQ[:], pv(gname, l, h), Y[:, h, tsl], ALU.mult, ALU.mult, [dSQ, dPV, dY], [dY])

    def mixer_mlstm(l):
        QF = scv(0, 4096, BF16).rearrange("p (c t) -> p c t", c=2)
        KF = scv(8192, 4096, BF16).rearrange("p (c t) -> p c t", c=2)
        WKV = scv(16384, 6144, BF16).rearrange("p (k n) -> p k n", k=8)
        WGm = scv(28672, 128, BF16).rearrange("p (k n) -> p k n", k=8)
        o = [28928]

        def al(n, dtype):
            v = scv(o[0], n, dtype)
            o[0] += (n * (4 if dtype is F32 else 2) + 3) // 4 * 4
            return v
        KVT = al(768, BF16)
        IG, NLF, NEGC, KS = al(4, F32), al(4, F32), al(4, F32), al(4, F32)
        NLFB, DT, RD, HO = al(128, F32), al(128, F32), al(128, F32), al(128, F32)
        PW, QP = al(128, BF16), al(128, BF16)
        KTx = [al(128, BF16), al(128, BF16)]
        EBE = al(1, F32)
        Cst = al(258, F32).rearrange("p (c n) -> p c n", c=2)
        CBf = al(256, BF16).rearrange("p (c n) -> p c n", c=2)
        NBb = al(256, BF16).rearrange("p (c n) -> p c n", c=2)
        dQF, dKF, dWKV, dKVT, dG, dNLFB, dDT, dRD, dHO, dPW, dQP, dKT, dEBE, dCst, dCB = (Dep() for _ in range(15))
        linear_fm(w_in[l], 8, O_ML, 256, ufn, [dU], T,
                  lambda ps, pd, col, cw, t0, tn: ts(QF[:, col // 128, t0:t0 + tn], ps[:, 0:tn], 0.125, None, ALU.mult, None, [pd], [dQF]))
        linear_fm(w_in[l], 8, O_ML + 256, 256, ufn, [dU], T,
                  lambda ps, pd, col, cw, t0, tn: cp(KF[:, col // 128, t0:t0 + tn], ps[:, 0:tn], [pd], [dKF], eng=act))
        linear_fm(w_in[l], 8, O_ML + 1024, 512, ufn, [dU], T,
                  lambda ps, pd, col, cw, t0, tn: actf(Y[:, col // 128, t0:t0 + tn], ps[:, 0:tn], AF.Sigmoid, [pd], [dY]))
        wv = w_in[l]
        P.dma(pool, WKV[:, :, :], wv[:, O_ML + 256:O_ML + 1024].rearrange("(kc p) n -> p kc n", p=128), writes=[dWKV])
        P.dma(pool, WGm[:, :, :], wv[:, O_ML + 1536:O_ML + 1552].rearrange("(kc p) n -> p kc n", p=128), writes=[dWKV])
        memset(KTx[0][:], 0.0, [dKT])
        memset(KTx[1][:], 0.0, [dKT])
        GB = pv("ml_gb", l, 0, 16)
        URt = al(1024, BF16).rearrange("p (k n) -> p k n", k=8)
        dUR = Dep()
        TMPR = PADB[:, 0:4096].rearrange("p (c t) -> p c t", c=2)
        dTR = Dep()
        for d in range(2):
            if d == 1:
                for BUF, dB in ((QF, dQF), (KF, dKF)):
                    cp(TMPR[:, :, :], BUF[:, :, ::-1], [dB], [dTR])
                    cp(BUF[:, :, :], TMPR[:, :, :], [dTR], [dB])
            memset(Cst[:, :, :], 0.0, [dCst])
            memset(CBf[:, :, :], 0.0, [dCB])
            memset(NBb[:, :, :], 0.0, [dCB])
            for tau in range(16):
                tsl = tk(tau * 128, 128, d)
                pt = slice(tau * 128, tau * 128 + 128)
                if d == 1:
                    cp(URt[:, :, :], U[:, :, tsl], [dU], [dUR])
                    uf = lambda kc: URt[:, kc, :]
                else:
                    uf = lambda kc: U[:, kc, pt]
                for c0, cn in ((0, 512), (512, 256)):
                    ps, pd = P.psum()
                    for kc in range(8):
                        mm(ps[:, 0:cn], uf(kc), WKV[:, kc, c0:c0 + cn], kc == 0, kc == 7, [dU, dWKV, dUR], [pd])
                    cp(KVT[:, c0:c0 + cn], ps[:, 0:cn], [pd], [dKVT], eng=act)
                psg, pgd = P.psum()
                for kc in range(8):
                    mm(psg[:, 0:16], uf(kc), WGm[:, kc, :], kc == 0, kc == 7, [dU, dWKV, dUR], [pgd])
                tt(IG[:], psg[:, d * 4:d * 4 + 4], GB[:, d * 4:d * 4 + 4], ALU.add, [pgd, dPV], [dG])
                tt(NLF[:], psg[:, 8 + d * 4:12 + d * 4], GB[:, 8 + d * 4:12 + d * 4], ALU.add, [pgd, dPV], [dG])
                actf(NLF[:], NLF[:], AF.Exp, [dG], [dG], scale=-1.0)
                actf(NLF[:], NLF[:], AF.Ln, [dG], [dG], bias=1.0)
                psb, pbd = P.psum()
                mm(psb[:, 0:4], LEf, NLF[:], True, True, [dC, dG], [pbd])
                tt(NEGC[:], psb[:, 0:4], IG[:], ALU.add, [pbd, dG], [dG])
                for h in range(4):
                    hp = slice((h % 2) * 64, (h % 2) * 64 + 64)
                    hc = h // 2
                    ts(NLFB[:], ONf, NLF[:, h:h + 1], None, ALU.mult, None, [dC, dG], [dNLFB])
                    bb, bbd = P.psum()
                    mm(bb[:, 0:128], NLFB[:], LEf, True, True, [dNLFB, dC], [bbd])
                    actf(EBE[:], bb[:, 127:128], AF.Exp, [bbd], [dEBE], scale=-1.0)
                    actf(KS[:, h:h + 1], bb[:, 127:128], AF.Exp, [bbd, dG], [dG], scale=-1.0, bias=NEGC[:, h:h + 1])
                    actf(DT[:], bb[:, 0:128], AF.Exp, [bbd, dG], [dDT], scale=-1.0, bias=NEGC[:, h:h + 1])
                    tt(DT[:], DT[:], LEf, ALU.mult, [dDT, dC], [dDT])
                    st, std = P.psum()
                    mm(st[:, 0:128], KF[hp, hc, pt], QF[hp, hc, pt], True, True, [dKF, dQF], [std])
                    tt(PW[:], st[:, 0:128], DT[:], ALU.mult, [std, dDT], [dPW])
                    actf(RD[hp, :], bb[hp, 0:128], AF.Exp, [bbd], [dRD], scale=-1.0)
                    tt(QP[hp, :], QF[hp, hc, pt], RD[hp, :], ALU.mult, [dQF, dRD], [dQP])
                    nu, nud = P.psum()
                    mm(nu[:, 0:128], KVT[:, 256 + h * 128:384 + h * 128], PW[:], True, False, [dKVT, dPW], [nud])
                    mm(nu[:, 0:128], CBf[hp, hc, :], QP[hp, :], False, True, [dCB, dQP], [nud])
                    de, ded = P.psum()
                    mm(de[:, 0:128], ONb, PW[:], True, False, [dC, dPW], [ded])
                    mm(de[:, 0:128], NBb[hp, hc, :], QP[hp, :], False, True, [dCB, dQP], [ded])
                    actf(RD[:], de[:, 0:128], AF.Abs, [ded], [dRD])
                    ts(RD[:], RD[:], 1.0, None, ALU.max, None, [dRD], [dRD])
                    recip(RD[:], RD[:], [dRD], [dRD])
                    if d == 0:
                        tt(ACC[:, h, tsl], nu[:, 0:128], RD[:], ALU.mult, [nud, dRD], [dACC])
                    else:
                        tt(HO[:], nu[:, 0:128], RD[:], ALU.mult, [nud, dRD], [dHO])
                        tt(ACC[:, h, tsl], ACC[:, h, tsl], HO[:], ALU.add, [dHO, dACC], [dACC])
                    kt = KTx[h % 2]
                    ts(kt[:, hp], KVT[:, h * 64:h * 64 + 64], KS[:, h:h + 1], None, ALU.mult, None, [dKVT, dG], [dKT])
                    pc, pcd = P.psum()
                    mm(pc[:, 0:128], kt[:], KVT[:, 256 + h * 128:384 + h * 128], True, True, [dKT, dKVT], [pcd], inc=False)
                    mm(pc[:, 128:129], kt[:], ONb[:, 0:1], True, True, [dKT, dC], [pcd], inc=True)
                    stt(Cst[hp, hc, :], Cst[hp, hc, :], EBE[hp, :], pc[hp, 0:129], ALU.mult, ALU.add, [dCst, dEBE, pcd], [dCst])
                    cp(CBf[hp, hc, :], Cst[hp, hc, 0:128], [dCst], [dCB], eng=act)
                    ts(NBb[hp, hc, :], ONf[hp, :], Cst[hp, hc, 128:129], None, ALU.mult, None, [dCst, dC], [dCB])
        for h in range(4):
            head_rmsnorm_gate(l, "ml_norm", h, True)

    def mixer_hgrn2(l):
        LF = scv(0, 2048, F32)
        G = scv(8192, 2048, F32)
        Kb = scv(16384, 2048, BF16)
        QS = scv(20480, 2048, BF16)
        WI = scv(24576, 1024, BF16).rearrange("p (k n) -> p k n", k=8)
        o = [26624]

        def al(n, dtype):
            v = scv(o[0], n, dtype)
            o[0] += (n * (4 if dtype is F32 else 2) + 3) // 4 * 4
            return v
        GLt, EXt, Sst = (al(128, F32) for _ in range(3))
        QTt, QHt, ATT, VT, KHT, SBb = (al(128, BF16) for _ in range(6))
        TM = al(9 * 128, F32).rearrange("p (k n) -> p k n", k=9)
        KTLa = al(9 * 128, BF16).rearrange("p (k n) -> p k n", k=9)
        EGE = al(1, F32)
        URt = al(1024, BF16).rearrange("p (k n) -> p k n", k=8)
        dUR = Dep()
        dLF, dG, dKb, dQS, dWI, dGL, dEX, dTMP, dEXk, dS, dQT, dQH, dKH, dATT, dVT, dKHT, dSB, dKTL, dEGE = (Dep() for _ in range(19))
        dKTLs = [Dep() for _ in range(8)]
        for h in range(4):
            linear_fm(w_in[l], 8, O_HG + h * 128, 128, ufn, [dU], T,
                      lambda ps, pd, col, cw, t0, tn: actf(QS[:, t0:t0 + tn], ps[:, 0:tn], AF.Silu, [pd], [dQS]))
            linear_fm(w_in[l], 8, O_HG + 2048 + h * 128, 128, ufn, [dU], T,
                      lambda ps, pd, col, cw, t0, tn: actf(Y[:, h, t0:t0 + tn], ps[:, 0:tn], AF.Silu, [pd], [dY]))
            P.dma(pool, WI[:, :, :], w_in[l][:, O_HG + 1536 + h * 128:O_HG + 1664 + h * 128].rearrange("(kc p) n -> p kc n", p=128),
                  writes=[dWI])
            for d in range(2):
                linear_fm(w_in[l], 8, O_HG + 512 + d * 512 + h * 128, 128, ufn, [dU], T,
                          lambda ps, pd, col, cw, t0, tn: actf(LF[:, tk(t0, tn, d)], ps[:, 0:tn], AF.Sigmoid, [pd], [dLF]))
                ts(LF[:], LF[:], sv(l, 24 + d * 4 + h), sv(l, 16 + d * 4 + h), ALU.mult, ALU.add, [dLF, dSV], [dLF])
                ts(Kb[:], LF[:], -1.0, 1.0, ALU.mult, ALU.add, [dLF], [dKb])
                actf(LF[:], LF[:], AF.Ln, [dLF], [dLF])
                for tau in range(16):
                    pt = slice(tau * 128, tau * 128 + 128)
                    scan(G[:, pt], ONf, LF[:, pt], 0.0, [dLF, dC], [dG])
                memset(Sst[:], 0.0, [dS])
                memset(SBb[:], 0.0, [dSB])
                for tau in range(16):
                    t0 = tau * 128
                    pt = slice(t0, t0 + 128)
                    nt = tk(t0, 128, d)
                    Gt = G[:, pt]
                    Gt3 = Gt.rearrange("p (b i) -> p b i", b=8)
                    GL3 = GLt.rearrange("p (b i) -> p b i", b=8)
                    cp(GL3[:, 0, :], Gt3[:, 0, :], [dG], [dGL], eng=pool)
                    tt(GL3[:, 1:8, :], Gt3[:, 1:8, :], Gt3[:, 0:7, 15:16].to_broadcast([128, 7, 16]), ALU.subtract, [dG], [dGL], eng=pool)
                    actf(EXt[:], GLt[:], AF.Exp, [dGL], [dEX])
                    tt(QTt[:], QS[:, nt], EXt[:], ALU.mult, [dQS, dEX], [dQT])
                    actf(EXt[:], Gt, AF.Exp, [dG, dQT], [dEX])
                    tt(QHt[:], QS[:, nt], EXt[:], ALU.mult, [dQS, dEX], [dQH])
                    cp(TM[:, 0, :], Gt, [dG], [dTMP], eng=pool)
                    tt(TM[:, 1:9, :], Gt.unsqueeze(1).to_broadcast([128, 8, 128]),
                       Gt3[:, 0:8, 15:16].to_broadcast([128, 8, 128]), ALU.subtract, [dG], [dTMP])
                    actf(TM[:, :, :], TM[:, :, :], AF.Exp, [dTMP], [dTMP], scale=-1.0)
                    stt(KTLa[:, :, :], TM[:, :, :], 1e26, Kb[:, pt].unsqueeze(1).to_broadcast([128, 9, 128]),
                        ALU.min, ALU.mult, [dTMP, dKb], [dKTL])
                    at, atd = P.psum()
                    for I in range(8):
                        mm(at[:, 16 * I:16 * I + 16], KTLa[:, I, :], QTt[:, 16 * I:16 * I + 16], True, True, [dKTL, dQT], [atd],
                           inc=(I == 7))
                    tt(ATT[:], at[:, 0:128], LEf, ALU.mult, [atd, dC], [dATT])
                    KH = KTLa[:, 8, :]
                    actf(EGE[:], Gt[:, 127:128], AF.Exp, [dG], [dEGE])
                    vp, vpd = P.psum()
                    if d == 1:
                        cp(URt[:, :, :], U[:, :, nt], [dU], [dUR])
                    for kc in range(8):
                        mm(vp[:, 0:128], URt[:, kc, :] if d == 1 else U[:, kc, pt], WI[:, kc, :], kc == 0, kc == 7,
                           [dU, dWI, dUR], [vpd])
                    cp(VT[:], vp[:, 0:128], [vpd], [dVT], eng=act)
                    op_, opd = P.psum()
                    mm(op_[:, 0:128], VT[:], ATT[:], True, False, [dVT, dATT], [opd])
                    mm(op_[:, 0:128], SBb[:], QHt[:], False, True, [dSB, dQH], [opd])
                    if d == 0:
                        cp(ACC[:, h, nt], op_[:, 0:128], [opd], [dACC], eng=act)
                    else:
                        tt(ACC[:, h, nt], ACC[:, h, nt], op_[:, 0:128], ALU.add, [opd, dACC], [dACC])
                    tp_, tpd = P.psum()
                    mm(tp_[:, 0:128], KH, IDb, True, True, [dKTL, dC], [tpd])
                    cp(KHT[:], tp_[:, 0:128], [tpd], [dKHT], eng=act)
                    sp_, spd = P.psum()
                    mm(sp_[:, 0:128], KHT[:], VT[:], True, True, [dKHT, dVT], [spd])
                    stt(Sst[:], Sst[:], EGE[:], sp_[:, 0:128], ALU.mult, ALU.add, [dS, dEGE, spd], [dS])
                    cp(SBb[:], Sst[:], [dS], [dSB], eng=pool)
            head_rmsnorm_gate(l, "hg_norm", h, True)

    MUV = P.sb("MUV", [128, 40], F32)
    dMUV = Dep()

    def mixer_rwkv(l):
        deps = {}

        def dp(n):
            if n not in deps:
                deps[n] = Dep()
            return deps[n]
        P0 = scv(0, 2050, F32)
        PF = scv(8208, 2048, F32)
        G = scv(16400, 2048, F32)
        Bb = scv(24592, 2048, BF16)
        KT = [scv(28688, 2048, BF16), scv(32784, 2048, BF16)]
        TXW, XA, SXG, Rr, Vv, KK, Kraw, GG = (M[:, i, :] for i in range(8))
        ts(MUV[:, 0:14], pv("rw_mu", l, 0, 14), -1.0, 1.0, ALU.mult, ALU.add, [dPV], [dMUV])
        ts(MUV[:, 14:28], pv("rw_mu", l, 0, 14), 0.5, None, ALU.mult, None, [dPV], [dMUV])
        ts(MUV[:, 28:32], pv("rw_ka", l, 0, 4), -1.0, 1.0, ALU.mult, ALU.add, [dPV], [dMUV])
        ts(MUV[:, 32:36], pv("rw_rk", l, 0, 4), 0.5, None, ALU.mult, None, [dPV], [dMUV])

        def shifted(ci):
            memset(P0[:, 0:1], 0.0, [dp("P0")])
            memset(P0[:, 2049:2050], 0.0, [dp("P0")])
            linear_fm(w_in[l], 8, O_RW + ci * 128, 128, ufn, [dU], T,
                      lambda ps, pd, col, cw, t0, tn: cp(P0[:, 1 + t0:1 + t0 + tn], ps[:, 0:tn], [pd], [dp("P0")], eng=act))
            tt(PF[:], P0[:, 0:T], P0[:, 2:T + 2], ALU.add, [dp("P0")], [dp("PF")])
            ts(PF[:], PF[:], MUV[:, 14 + ci:15 + ci], None, ALU.mult, None, [dp("PF"), dMUV], [dp("PF")])
            stt(PF[:], P0[:, 1:T + 1], MUV[:, ci:ci + 1], PF[:], ALU.mult, ALU.add, [dp("P0"), dp("PF"), dMUV], [dp("PF")])

        shifted(12)
        actf(TXW[0:64, :], PF[0:64, :], AF.Tanh, [dp("PF")], [dp("TXW")])
        cp(XA[64:128, :], PF[64:128, :], [dp("PF")], [dp("XA")])
        shifted(13)
        actf(SXG[:, :], PF[:], AF.Sigmoid, [dp("PF")], [dp("SXG")])
        SQt = P0[:, 0:512]
        RSt = P0[:, 512:1024]
        T1 = P0[:, 1024:1536]
        HP = [slice(0, 64), slice(64, 128)]
        for j in range(rw_limit[0]):
            P.barrier()
            regions = [(SC, 18440, 20480), (RR, 40960, 45056)] + [(RR, k * 4096, (k + 1) * 4096) for k in range(4) if k != j]
            ri, ro = [0], [regions[0][1]]

            def al(n, dtype=BF16):
                ne = n * (2 if dtype is F32 else 1)
                ne = (ne + 1) // 2 * 2
                while ro[0] + ne > regions[ri[0]][2]:
                    ri[0] += 1
                    ro[0] = regions[ri[0]][1]
                t_, a_ = regions[ri[0]][0], ro[0]
                ro[0] += ne
                v = t_[:, a_:a_ + ne]
                return v.bitcast(F32) if dtype is F32 else v

            def mkset(sid):
                B = {"sid": sid}
                for nm in ("AT", "RT", "BT", "KTt", "BH", "KHh", "VTf", "ATk", "VTk", "BHk", "KHk", "PT"):
                    B[nm] = al(128)
                for nm in ("EXa", "EXb", "EXc", "EXd"):
                    B[nm] = al(128, F32)
                for nm in ("NT0", "WS", "AAK", "ARB", "ARK", "Xb", "Tb", "TTb", "UT"):
                    B[nm] = [al(128), al(128)]
                B["NK"] = [al(768).rearrange("p (k n) -> p k n", k=6) for _ in range(2)]
                B["MT"] = [al(128, F32), al(128, F32)]
                B["GC"] = al(64, F32)
                return B
            sets = [mkset(0), mkset(1)]
            Sf = al(128, F32)
            Sb = al(128)
            for B in sets:
                for b_ in (B["UT"][0], B["UT"][1], B["MT"][0], B["MT"][1]):
                    memset(b_[:], 0.0, [dp("misc%d" % B["sid"])], eng=pool)
            shifted(j)
            cp(Rr[:, :], PF[:], [dp("PF")], [dp("R")], eng=act)
            shifted(8 + j)
            cp(Vv[:, :], PF[:], [dp("PF")], [dp("V")], eng=act)
            shifted(4 + j)
            cp(Kraw[:, :], PF[:], [dp("PF")], [dp("Kraw")], eng=act)
            ts(PF[:], PF[:], pv("rw_kk", l, j), None, ALU.mult, None, [dp("PF"), dPV], [dp("PF")])
            for tq in range(4):
                tsl = slice(tq * 512, (tq + 1) * 512)
                actf(SQt, PF[:, tsl], AF.Square, [dp("PF")], [dp("P0")])
                ps, pd = P.psum()
                mm(ps[:, :], BKf, SQt, True, True, [dC, dp("P0")], [pd])
                ts(RSt, ps[:, :], 1e-24, None, ALU.max, None, [pd], [dp("P0")])
                actf(RSt, RSt, AF.Sqrt, [dp("P0")], [dp("P0")])
                recip(RSt, RSt, [dp("P0")], [dp("P0")])
                tt(KK[:, tsl], PF[:, tsl], RSt, ALU.mult, [dp("PF"), dp("P0")], [dp("KK")])
                ps, pd = P.psum()
                mm(ps[:, :], G2s[:, j * 128:(j + 1) * 128], SXG[:, tsl], True, True, [dLW, dp("SXG")], [pd])
                cp(GG[:, tsl], ps[:, :], [pd], [dp("GG")], eng=act)
            AS = P0[:, 0:2048]
            for d in range(rw_limit[1]):
                for tq in range(4):
                    tsl = slice(tq * 512, (tq + 1) * 512)
                    ps, pd = P.psum()
                    mm(ps[:, :], W2s[:, d, j * 128:(j + 1) * 128], TXW[0:64, tsl], True, True, [dLW, dp("TXW")], [pd])
                    actf(PF[:, tk(tq * 512, 512, d)], ps[:, :], AF.Sigmoid, [pd, dPV], [dp("PF")], bias=pv("rw_w0", l, d * 4 + j))
                    ps, pd = P.psum()
                    mm(ps[:, :], A2s[:, d, j * 128:(j + 1) * 128], XA[64:128, tsl], True, True, [dLW, dp("XA")], [pd])
                    actf(AS[:, tk(tq * 512, 512, d)], ps[:, :], AF.Sigmoid, [pd, dPV], [dp("P0")], bias=pv("rw_a0", l, d * 4 + j))
                ts(PF[:], PF[:], -0.6065306597, None, ALU.mult, None, [dp("PF")], [dp("PF")])
                for tau in range(16):
                    pt = slice(tau * 128, tau * 128 + 128)
                    scan(G[:, pt], ONf, PF[:, pt], 0.0, [dp("PF"), dC], [dp("G")])
                rv = slice(None, None, -1) if d == 1 else slice(None)
                dKT = dp("KT%d" % d)
                ts(KT[d][:], AS, pv("rw_ka", l, j), MUV[:, 28 + j:29 + j], ALU.mult, ALU.add, [dp("P0"), dPV, dMUV], [dKT])
                tt(KT[d][:], KT[d][:], Kraw[:, rv], ALU.mult, [dKT, dp("Kraw")], [dKT])
                tt(Bb[:], KK[:, rv], AS, ALU.mult, [dp("KK"), dp("P0")], [dp("Bb")])
                memset(Sf[:], 0.0, [dp("Sf0"), dp("Sf1")], eng=pool)
                memset(Sb[:], 0.0, [dp("Sb0"), dp("Sb1")], eng=pool)

                def D(B, nm, e=None):
                    return dp("%s%s_%d" % (nm, "" if e is None else str(e), B["sid"]))

                def prep(tau, B):
                    pt = slice(tau * 128, tau * 128 + 128)
                    nt = tk(tau * 128, 128, d)
                    Gt = G[:, pt]
                    actf(B["EXa"][:], Gt, AF.Exp, [dp("G")], [D(B, "EXa")])
                    tt(B["RT"][:], Rr[:, nt], B["EXa"][:], ALU.mult, [dp("R"), D(B, "EXa")], [D(B, "RT")])
                    cp(B["EXb"][:, 1:128], B["EXa"][:, 0:127], [D(B, "EXa")], [D(B, "EXb")], eng=pool)
                    memset(B["EXb"][:, 0:1], 1.0, [D(B, "EXb")], eng=pool)
                    stt(B["AT"][:], KK[:, nt], -1.0, B["EXb"][:], ALU.mult, ALU.mult, [dp("KK"), D(B, "EXb")], [D(B, "AT")])
                    actf(B["EXc"][:], Gt, AF.Exp, [dp("G")], [D(B, "EXc")], scale=-1.0)
                    tt(B["BT"][:], Bb[:, pt], B["EXc"][:], ALU.mult, [dp("Bb"), D(B, "EXc")], [D(B, "BT")])
                    tt(B["KTt"][:], KT[d][:, pt], B["EXc"][:], ALU.mult, [dKT, D(B, "EXc")], [D(B, "KTt")], eng=pool)
                    actf(B["EXd"][:], Gt, AF.Exp, [dp("G")], [D(B, "EXd")], scale=-1.0, bias=Gt[:, 127:128])
                    tt(B["BH"][:], Bb[:, pt], B["EXd"][:], ALU.mult, [dp("Bb"), D(B, "EXd")], [D(B, "BH")])
                    tt(B["KHh"][:], KT[d][:, pt], B["EXd"][:], ALU.mult, [dKT, D(B, "EXd")], [D(B, "KHh")], eng=pool)
                    cp(B["VTf"][:], Vv[:, nt], [dp("V")], [D(B, "VTf")], eng=pool)
                    for sn, dn_ in (("AT", "ATk"), ("VTf", "VTk"), ("BH", "BHk"), ("KHh", "KHk")):
                        ps, pd = P.psum()
                        mm(ps[:, 0:128], B[sn][:], IDb, True, True, [D(B, sn), dC], [pd])
                        cp(B[dn_][:], ps[:, 0:128], [pd], [D(B, dn_)], eng=act)

                def score(B, lh, ln, rh, rn, mask, dst, dn_):
                    ps, pd = P.psum()
                    mm(ps[:, 0:128], lh, rh, True, True, [D(B, ln), D(B, rn)], [pd])
                    tt(dst[:], ps[:, 0:128], mask, ALU.mult, [pd, dC], [dn_])

                def st_scores(tau, B, e):
                    hp = HP[e]
                    AT, BT, KTt, RT = B["AT"], B["BT"], B["KTt"], B["RT"]
                    score(B, BT[hp, :], "BT", AT[hp, :], "AT", LTf, B["NT0"][e], D(B, "NT", e))
                    score(B, AT[hp, :], "AT", BT[hp, :], "BT", GTf, B["WS"][e], D(B, "WS", e))
                    score(B, KTt[hp, :], "KTt", AT[hp, :], "AT", LTf, B["AAK"][e], D(B, "AAK", e))
                    score(B, BT[hp, :], "BT", RT[hp, :], "RT", LEf, B["ARB"][e], D(B, "ARB", e))
                    score(B, KTt[hp, :], "KTt", RT[hp, :], "RT", LEf, B["ARK"][e], D(B, "ARK", e))

                def st_masks(tau, B, e):
                    nk, nt0 = B["NK"][e], B["NT0"][e]
                    src = nt0[:, :].unsqueeze(1).to_broadcast([128, 6, 128])
                    P.op(pool, lambda h: h.tensor_tensor(nk[:, :, :], src, CB[:, 14:20, :], ALU.mult),
                         [D(B, "NT", e), dC], [D(B, "NK", e)])

                def st_x0(tau, B, e):
                    hp, oc = HP[e], HP[1 - e]
                    ps, pd = P.psum()
                    mm(ps[:, 0:64], B["AAK"][e][:], B["VTk"][:, hp], True, True, [D(B, "AAK", e), D(B, "VTk")], [pd])
                    cp(B["Xb"][e][:, oc], ps[:, 0:64], [pd], [D(B, "Xb", e)], eng=act)
                    cp(B["Xb"][e][:, hp], B["ATk"][:, hp], [D(B, "ATk")], [D(B, "Xb", e)], eng=pool)

                def st_lvl0(tau, B, e):
                    Tb, TTb = B["Tb"][e], B["TTb"][e]
                    tt(Tb[:], B["WS"][e][:], LMb[0], ALU.mult, [D(B, "WS", e), dC], [D(B, "T", e)])
                    tt(Tb[:], Tb[:], IDb, ALU.add, [D(B, "T", e), dC], [D(B, "T", e)])
                    tt(TTb[:], B["NT0"][e][:], LMTb[0], ALU.mult, [D(B, "NT", e), dC], [D(B, "TT", e)], eng=pool)
                    tt(TTb[:], TTb[:], IDb, ALU.add, [D(B, "TT", e), dC], [D(B, "TT", e)], eng=pool)

                def mk_lvl(k):
                    def st(tau, B, e):
                        Tb, TTb, WS = B["Tb"][e], B["TTb"][e], B["WS"][e]
                        dT, dTT, dWS = D(B, "T", e), D(B, "TT", e), D(B, "WS", e)
                        ps, pd = P.psum()
                        mm(ps[:, 0:128], B["NK"][e][:, k - 1, :], Tb[:], True, True, [D(B, "NK", e), dT], [pd])
                        cp(WS[:], ps[:, 0:128], [pd], [dWS])
                        pz, pzd = P.psum()
                        mm(pz[:, 0:128], IDb, Tb[:], True, False, [dC, dT], [pzd])
                        mm(pz[:, 0:128], TTb[:], WS[:], False, True, [dTT, dWS], [pzd])
                        pt_, ptd = P.psum()
                        mm(pt_[:, 0:128], IDb, TTb[:], True, False, [dC, dTT], [ptd])
                        mm(pt_[:, 0:128], WS[:], TTb[:], False, True, [dTT, dWS], [ptd])
                        cp(Tb[:], pz[:, 0:128], [pzd], [dT], eng=act)
                        cp(TTb[:], pt_[:, 0:128], [ptd], [dTT], eng=act)
                    return st

                def st_apply(tau, B, e):
                    ps, pd = P.psum()
                    mm(ps[:, 0:128], B["TTb"][e][:], B["Xb"][e][:], True, True, [D(B, "TT", e), D(B, "Xb", e)], [pd])
                    cp(B["Xb"][e][:], ps[:, 0:128], [pd], [D(B, "Xb", e)], eng=act)

                def st_mt(tau, B, e):
                    hp = HP[e]
                    ps, pd = P.psum()
                    mm(ps[:, 0:64], B["Xb"][e][:], B["BHk"][:, hp], True, True, [D(B, "Xb", e), D(B, "BHk")], [pd])
                    stt(B["MT"][e][hp, hp], IDf[hp, hp], B["EXa"][hp, 127:128], ps[hp, 0:64], ALU.mult, ALU.add,
                        [pd, dC, D(B, "EXa")], [D(B, "MT", e)])

                def st_gc(tau, B, e):
                    hp, oc = HP[e], HP[1 - e]
                    ps, pd = P.psum()
                    mm(ps[:, 0:64], B["BHk"][:], B["Xb"][e][:, oc], True, False, [D(B, "Xb", e), D(B, "BHk")], [pd])
                    mm(ps[:, 0:64], B["KHk"][:], B["VTk"][:, hp], False, True, [D(B, "KHk"), D(B, "VTk")], [pd])
                    cp(B["GC"][hp, :], ps[hp, 0:64], [pd], [D(B, "GC", e)], eng=act)

                def st_pt(tau, B, e):
                    hp = HP[e]
                    ps, pd = P.psum()
                    mm(ps[:, 0:128], B["Xb"][e][:], IDb, True, True, [D(B, "Xb", e), dC], [pd])
                    cp(B["PT"][hp, :], ps[hp, 0:128], [pd], [D(B, "PT", e)], eng=act)

                def st_u(tau, B, e):
                    hp, oc = HP[e], HP[1 - e]
                    ps, pd = P.psum()
                    mm(ps[:, 0:64], B["PT"][hp, :], Sb[hp, hp], True, True, [D(B, "PT", e), dp("Sb%d" % e)], [pd])
                    tt(B["UT"][e][:, hp], ps[:, 0:64], B["Xb"][e][:, oc], ALU.add, [pd, D(B, "Xb", e)], [D(B, "UT", e)])

                def st_y(tau, B, e):
                    hp = HP[e]
                    nt = tk(tau * 128, 128, d)
                    ps, pd = P.psum()
                    mm(ps[:, 0:128], Sb[hp, :], B["RT"][hp, :], True, False, [dp("Sb%d" % e), D(B, "RT")], [pd])
                    mm(ps[:, 0:128], B["UT"][e][:], B["ARB"][e][:], False, False, [D(B, "UT", e), D(B, "ARB", e)], [pd])
                    mm(ps[:, 0:128], B["VTk"][:], B["ARK"][e][:], False, True, [D(B, "VTk"), D(B, "ARK", e)], [pd])
                    if d == 0:
                        cp(ACC[hp, j, nt], ps[hp, 0:128], [pd], [dACC], eng=act)
                    else:
                        tt(ACC[hp, j, nt], ACC[hp, j, nt], ps[hp, 0:128], ALU.add, [pd, dACC], [dACC])

                def st_chain(tau, B, e):
                    hp = HP[e]
                    ps, pd = P.psum()
                    mm(ps[:, 0:64], B["MT"][e][hp, :], Sf[hp, hp], True, True, [D(B, "MT", e), dp("Sf%d" % e)], [pd])
                    tt(Sf[hp, hp], ps[hp, 0:64], B["GC"][hp, :], ALU.add, [pd, D(B, "GC", e), dp("Sf%d" % e)], [dp("Sf%d" % e)])
                    cp(Sb[hp, hp], Sf[hp, hp], [dp("Sf%d" % e)], [dp("Sb%d" % e)], eng=pool)

                indep = [st_scores, st_masks, st_x0, st_lvl0] + [mk_lvl(k) for k in range(1, 7)] + \
                        [st_apply, st_mt, st_gc, st_pt]
                ntile = rw_limit[2]
                for tau0 in range(0, ntile, 2):
                    ctx = [(tau0 + i, sets[i]) for i in range(min(2, ntile - tau0))]
                    for tau, B in ctx:
                        prep(tau, B)
                    for stg in indep:
                        for tau, B in ctx:
                            for e in range(2):
                                stg(tau, B, e)
                    for tau, B in ctx:
                        for stg in (st_u, st_y, st_chain):
                            for e in range(2):
                                stg(tau, B, e)
            P.barrier()
            for tq in range(4):
                tsl = slice(tq * 512, (tq + 1) * 512)
                ps, pd = P.psum()
                mm(ps[:, :], BKf, ACC[:, j, tsl], True, True, [dC, dACC], [pd])
                stt(SQt, ps[:, :], -1.0 / 64, ACC[:, j, tsl], ALU.mult, ALU.add, [pd, dACC], [dp("P0")])
                actf(RSt, SQt, AF.Square, [dp("P0")], [dp("P0")])
                ps, pd = P.psum()
                mm(ps[:, :], BKf, RSt, True, True, [dC, dp("P0")], [pd])
                actf(RSt, ps[:, :], AF.Sqrt, [pd], [dp("P0")], bias=64e-5, scale=1.0 / 64)
                recip(RSt, RSt, [dp("P0")], [dp("P0")])
                tt(SQt, SQt, RSt, ALU.mult, [dp("P0")], [dp("P0")])
                ts(SQt, SQt, pv("rw_lnw", l, j), pv("rw_lnb", l, j), ALU.mult, ALU.add, [dp("P0"), dPV], [dp("P0")])
                tt(T1, KT[0][:, tsl], KT[1][:, tk(tq * 512, 512, 1)], ALU.add, [dp("KT0"), dp("KT1")], [dp("P0")])
                tt(T1, T1, Rr[:, tsl], ALU.mult, [dp("P0"), dp("R")], [dp("P0")])
                ts(T1, T1, MUV[:, 32 + j:33 + j], None, ALU.mult, None, [dp("P0"), dMUV], [dp("P0")])
                ps, pd = P.psum()
                mm(ps[:, :], BKf, T1, True, True, [dC, dp("P0")], [pd])
                tt(T1, ps[:, :], Vv[:, tsl], ALU.mult, [pd, dp("V")], [dp("P0")])
                tt(SQt, SQt, T1, ALU.add, [dp("P0")], [dp("P0")])
                tt(Y[:, j, tsl], SQt, GG[:, tsl], ALU.mult, [dp("P0"), dp("GG")], [dY])

    def mixer_stub(l):
        pass

    mix_fns = {0: globals().get("_mx_rwkv"), 1: mixer_mlstm, 2: mixer_lru, 3: globals().get("_mx_hg")}
    mix_fns[0] = locals().get("mixer_rwkv", mixer_stub)
    mix_fns[3] = locals().get("mixer_hgrn2", mixer_stub)

    def layer(l, sq):
        src = xT[sq] if l == 0 else hd[sq]
        load_layer_small(l)
        norm_mod(src, l, sq, 0)
        P.barrier()
        if l == 0 and sq == 0:
            tap("U", U[:, :, :], [dU], None)
        first = True
        for n in range(4):
            if n in mixers:
                mix_fns[n](l)
                P.barrier()
                if l == 0 and sq == 0:
                    tap("Y%d" % n, Y[:, :, :], [dY], None)
                gate_merge(l, n, first)
                first = False
                P.barrier()
        out_proj_residual(l, sq, src, hd[sq])
        P.barrier()
        norm_mod(hd[sq], l, sq, 1)
        P.barrier()
        ffn(l, sq, hd[sq])
        P.barrier()

    for sq in range(n_seq):
        for l in range(n_layers):
            layer(l, sq)
        norm_mod(hd[sq], -1, sq, 0)
        P.barrier()
    P.finish()
    es.close()
    return nc, P


def make_in_maps(inputs, n_cores=NCORE):
    inp = {k: np.asarray(v) for k, v in inputs.items()}
    pv = pack_pv(inp)
    cst = pack_consts()
    shared = {
        "pv": pv, "cst": cst,
        "ada_w": np.ascontiguousarray(inp["ada_w"], np.float32), "w_in": np.ascontiguousarray(inp["w_in"], np.float32),
        "rw_w2": np.ascontiguousarray(inp["rw_w2"], np.float32), "rw_a2": np.ascontiguousarray(inp["rw_a2"], np.float32),
        "rw_g2": np.ascontiguousarray(inp["rw_g2"], np.float32),
        "lru_wa": pack_lru_bd(inp["lru_wa"]), "lru_wx": pack_lru_bd(inp["lru_wx"]),
        "w_branch": np.ascontiguousarray(inp["w_branch"], np.float32), "w_out": np.ascontiguousarray(inp["w_out"], np.float32),
        "ffn_up": np.ascontiguousarray(inp["ffn_up"], np.float32), "ffn_down": np.ascontiguousarray(inp["ffn_down"], np.float32),
    }
    maps = []
    for i in range(n_cores):
        xs = inp["x"][i * NSEQ:(i + 1) * NSEQ]
        m = dict(shared)
        m["xT"] = np.ascontiguousarray(np.transpose(xs, (0, 2, 1)), np.float32)
        cs = inp["c"][i * NSEQ:(i + 1) * NSEQ].astype(np.float32)
        m["cT"] = np.ascontiguousarray(cs.T.reshape(8, 128, NSEQ).transpose(1, 0, 2))
        maps.append(m)
    return maps


def kernel(**inputs):
    nc, _ = build()
    maps = make_in_maps(inputs)
    res = run_bass_kernel_spmd(nc, maps, core_ids=list(range(NCORE)))
    outs = [np.transpose(r["outT"], (0, 2, 1)) for r in res.results]
    return np.ascontiguousarray(np.concatenate(outs, axis=0), dtype=np.float32)
```

```python
import numpy as np
import concourse.bass as bass
import concourse.mybir as mybir
from concourse.bass_utils import run_bass_kernel_spmd
from contextlib import ExitStack

F32 = mybir.dt.float32
BF16 = mybir.dt.bfloat16
ALU = mybir.AluOpType
AF = mybir.ActivationFunctionType

D = 1024
T = 2048
L = 4
NSEQ = 2
NCORE = 8
W = 512
N_IN = 11024
O_RW, O_ML, O_LRU, O_HG, O_GATE = 0, 1792, 3344, 4368, 6928
DFF = 2816
class Track:
    def __init__(self, sem, step):
        self.sem = sem
        self.val = 0
        self.step = step


class Dep:
    __slots__ = ("w", "r")

    def __init__(self):
        self.w = None
        self.r = {}


class Eng:
    def __init__(self, name, h, tr, is_pe=False):
        self.name = name
        self.h = h
        self.tr = tr
        self.is_pe = is_pe
        self.seen = {}
        self.ops = []
        self.pool = []
        self.dma_i = 0


class Prog:
    def __init__(self, nc, es):
        self.nc = nc
        self.es = es
        self.tracks = []

        def mk(name, step=1):
            t = Track(es.enter_context(nc.semaphore(name)), step)
            self.tracks.append(t)
            return t

        self.pe = Eng("pe", nc.tensor, mk("s_pe"), True)
        self.act = Eng("act", nc.scalar, mk("s_act"))
        self.dve = Eng("dve", nc.vector, mk("s_dve"))
        self.pool = Eng("pool", nc.gpsimd, mk("s_pool"))
        self.sp = Eng("sp", nc.sync, mk("s_sp"))
        self.engs = [self.pe, self.act, self.dve, self.pool, self.sp]
        for e, n in ((self.sp, 16), (self.pool, 16), (self.act, 4)):
            e.pool = [mk("d_%s%d" % (e.name, i), 16) for i in range(n)]
        self.nps = 0
        self.psums = []
        self.n_ops = 0

    def sb(self, name, shape, dt=F32):
        return self.es.enter_context(self.nc.sbuf_tensor(name, list(shape), dt))

    def make_psums(self):
        for i in range(8):
            t = self.es.enter_context(self.nc.psum_tensor("ps%d" % i, [128, 512], F32))
            self.psums.append((t, Dep()))

    def psum(self):
        t = self.psums[self.nps % 8]
        self.nps += 1
        return t

    def _waits(self, eng, reads, writes, extra=()):
        need = {}

        def add(ev):
            if ev is None:
                return
            tr, v = ev
            if need.get(tr, 0) < v:
                need[tr] = v

        for d in reads:
            add(d.w)
        for d in writes:
            add(d.w)
            for tr, v in d.r.items():
                add((tr, v))
        for ev in extra:
            add(ev)
        for tr, v in need.items():
            if tr is eng.tr and eng.is_pe:
                continue
            if eng.seen.get(tr, 0) >= v:
                continue
            eng.seen[tr] = v
            eng.ops.append(("wait", tr.sem, v))

    def op(self, eng, fn, reads=(), writes=(), inc=True):
        self._waits(eng, reads, writes)
        val = eng.tr.val + 1
        if inc:
            eng.tr.val = val
        eng.ops.append(("op", fn, inc))
        self.n_ops += 1
        for d in reads:
            if d.r.get(eng.tr, 0) < val:
                d.r[eng.tr] = val
        for d in writes:
            d.w = (eng.tr, val)
            d.r = {}

    def dma(self, eng, out, in_, reads=(), writes=(), **kw):
        tr = eng.pool[eng.dma_i % len(eng.pool)]
        eng.dma_i += 1
        extra = [(tr, tr.val)] if tr.val > 0 else []
        self._waits(eng, reads, writes, extra)
        tr.val += 16
        eng.ops.append(("dma", out, in_, tr.sem, kw))
        self.n_ops += 1
        for d in reads:
            d.r[tr] = tr.val
        for d in writes:
            d.w = (tr, tr.val)
            d.r = {}

    def barrier(self):
        for e in self.engs:
            for tr in self.tracks:
                if tr.val > e.seen.get(tr, 0):
                    if tr is e.tr and e.is_pe:
                        continue
                    e.seen[tr] = tr.val
                    e.ops.append(("wait", tr.sem, tr.val))

    def finish(self):
        self.barrier()
        nc = self.nc

        def replay(e, h):
            for o in e.ops:
                if o[0] == "wait":
                    h.wait_ge(o[1], o[2])
                elif o[0] == "op":
                    ins = o[1](h)
                    if o[2]:
                        ins.then_inc(e.tr.sem, 1)
                else:
                    h.dma_start(out=o[1], in_=o[2], **o[4]).then_inc(o[3], 16)

        with nc.Block() as block:
            @block.tensor
            def _(h):
                replay(self.pe, h)

            @block.scalar
            def _(h):
                replay(self.act, h)

            @block.vector
            def _(h):
                replay(self.dve, h)

            @block.gpsimd
            def _(h):
                replay(self.pool, h)

            @block.sync
            def _(h):
                replay(self.sp, h)

PV_SPEC = [("ada_b", 48), ("n1g", 8), ("n2g", 8), ("rw_mu", 14), ("rw_w0", 8), ("rw_a0", 8), ("rw_kk", 4),
           ("rw_ka", 4), ("rw_rk", 4), ("rw_lnw", 4), ("rw_lnb", 4), ("ml_norm", 4), ("lru_cw", 16),
           ("lru_cb", 4), ("lru_ba", 8), ("lru_bx", 8), ("lru_lam", 8), ("hg_lb", 8), ("hg_norm", 4),
           ("ffn_cw", 132), ("ffn_cb", 44), ("ml_gb", 16)]
PV_L = sum(n for _, n in PV_SPEC)
PV_OFF = {}
_o = 0
for _n, _c in PV_SPEC:
    PV_OFF[_n] = _o
    _o += _c
NPV = PV_L * L + 8


def _fm(v):
    return np.ascontiguousarray(np.asarray(v, np.float32).reshape(-1, 128).T)


def pack_pv(inp):
    pv = np.zeros((128, NPV), np.float32)
    for l in range(L):
        ent = {
            "ada_b": _fm(inp["ada_b"][l]), "n1g": _fm(inp["norm1_g"][l]), "n2g": _fm(inp["norm2_g"][l]),
            "rw_mu": _fm(inp["rw_mu"][l]), "rw_w0": _fm(inp["rw_w0"][l].reshape(-1)),
            "rw_a0": _fm(inp["rw_a0"][l].reshape(-1)), "rw_kk": _fm(inp["rw_kk"][l]),
            "rw_ka": _fm(inp["rw_ka"][l]), "rw_rk": _fm(inp["rw_rk"][l].reshape(-1)),
            "rw_lnw": _fm(inp["rw_lnw"][l]), "rw_lnb": _fm(inp["rw_lnb"][l]), "ml_norm": _fm(inp["ml_norm"][l]),
            "lru_cw": _fm(inp["lru_conv_w"][l].reshape(-1)), "lru_cb": _fm(inp["lru_conv_b"][l]),
            "lru_ba": _fm(inp["lru_ba"][l].reshape(-1)), "lru_bx": _fm(inp["lru_bx"][l].reshape(-1)),
            "lru_lam": _fm(inp["lru_lam"][l].reshape(-1)), "hg_lb": _fm(inp["hg_lb"][l].reshape(-1)),
            "hg_norm": _fm(inp["hg_norm"][l]), "ffn_cw": _fm(inp["ffn_conv_w"][l].reshape(-1)),
            "ffn_cb": _fm(inp["ffn_conv_b"][l]),
            "ml_gb": np.tile(np.concatenate([inp["ml_ibias"][l].reshape(-1), inp["ml_fbias"][l].reshape(-1)])[None, :],
                             (128, 1)).astype(np.float32),
        }
        for n, c in PV_SPEC:
            assert ent[n].shape == (128, c), (n, ent[n].shape)
            pv[:, l * PV_L + PV_OFF[n]: l * PV_L + PV_OFF[n] + c] = ent[n]
    pv[:, PV_L * L:] = _fm(inp["final_g"])
    return pv


NCST = 20


def pack_consts():
    c = np.zeros((128, NCST, 128), np.float32)
    i = np.arange(128)
    c[:, 0, :] = np.eye(128)
    c[:, 1, :] = 1.0
    c[:, 2, :] = (i[:, None] <= i[None, :])
    c[:, 3, :] = (i[:, None] < i[None, :])
    c[:, 4, :] = (i[:, None] // 64 == i[None, :] // 64)
    c[:, 5, :] = (i[:, None] > i[None, :])
    for k in range(7):
        bsz = 1 << k
        t, s_ = i[:, None], i[None, :]
        m = (t // (2 * bsz) == s_ // (2 * bsz)) & (t % (2 * bsz) >= bsz) & (s_ % (2 * bsz) < bsz)
        c[:, 6 + k, :] = m
        c[:, 13 + k, :] = m.T
    return c


def pack_lru_bd(w):
    out = np.zeros((L, 2, 4, 128, 128), np.float32)
    for j in range(4):
        out[:, :, j, 0:64, 0:64] = w[:, :, 2 * j]
        out[:, :, j, 64:128, 64:128] = w[:, :, 2 * j + 1]
    return out


def build(n_layers=L, n_seq=NSEQ, mixers=(0, 1, 2, 3), taps=(), rw_limit=(4, 2, 16)):
    nc = bass.Bass("TRN2", target_bir_lowering=False)
    dt = lambda name, shape, kind="ExternalInput": nc.dram_tensor(name, list(shape), F32, kind=kind).ap()
    xT = dt("xT", [NSEQ, D, T])
    cT = dt("cT", [128, 8, NSEQ])
    pvd = dt("pv", [128, NPV])
    cst = dt("cst", [128, NCST, 128])
    ada_w = dt("ada_w", [L, D, 6 * D])
    w_in = dt("w_in", [L, D, N_IN])
    rw_w2 = dt("rw_w2", [L, 2, 64, W])
    rw_a2 = dt("rw_a2", [L, 2, 64, W])
    rw_g2 = dt("rw_g2", [L, 128, W])
    lru_wa = dt("lru_wa", [L, 2, 4, 128, 128])
    lru_wx = dt("lru_wx", [L, 2, 4, 128, 128])
    w_branch = dt("w_branch", [L, 4, W, D])
    w_out = dt("w_out", [L, D, D])
    ffn_up = dt("ffn_up", [L, D, 2 * DFF])
    ffn_down = dt("ffn_down", [L, DFF, D])
    outT = dt("outT", [NSEQ, D, T], "ExternalOutput")
    hd = dt("hd", [NSEQ, D, T], "Internal")
    tapd = {}
    for name, shape in taps:
        tapd[name] = dt("tap_" + name, shape, "ExternalOutput")

    es = ExitStack()
    P = Prog(nc, es)
    P.make_psums()
    pe, act, dve, pool, sp = P.pe, P.act, P.dve, P.pool, P.sp

    def mm(out, lhsT, rhs, start, stop, r, w, inc=None):
        P.op(pe, lambda h: h.matmul(out, lhsT, rhs, start=start, stop=stop), r, w, inc=stop if inc is None else inc)

    def actf(out, in_, func, r, w, bias=None, scale=None):
        kw = {}
        if bias is not None:
            kw["bias"] = bias
        if scale is not None:
            kw["scale"] = scale
        P.op(act, lambda h: h.activation(out, in_, func, **kw), r, w)

    def tt(out, a, b, op, r, w, eng=None):
        P.op(eng or dve, lambda h: h.tensor_tensor(out, a, b, op), r, w)

    def ts(out, a, s1, s2, op0, op1, r, w, eng=None):
        if op1 is None:
            P.op(eng or dve, lambda h: h.tensor_scalar(out, a, s1, None, op0), r, w)
        else:
            P.op(eng or dve, lambda h: h.tensor_scalar(out, a, s1, s2, op0, op1), r, w)

    def stt(out, a, s, b, op0, op1, r, w, eng=None):
        P.op(eng or dve, lambda h: h.scalar_tensor_tensor(out, a, s, b, op0, op1), r, w)

    def cp(out, in_, r, w, eng=None):
        e = eng or dve
        if e is act:
            P.op(act, lambda h: h.copy(out, in_), r, w)
        else:
            P.op(e, lambda h: h.tensor_copy(out, in_), r, w)

    def memset(ap, v, w, eng=None):
        P.op(eng or dve, lambda h: h.memset(ap, v), (), w)

    def recip(out, in_, r, w):
        P.op(dve, lambda h: h.reciprocal(out, in_), r, w)

    def scan(out, d0, d1, init, r, w):
        P.op(dve, lambda h: h.tensor_tensor_scan(out, d0, d1, init, ALU.mult, ALU.add), r, w)

    def tap(name, src_ap, r, dst=None):
        if name in tapd:
            P.dma(pool, tapd[name] if dst is None else dst, src_ap, reads=r)

    U = P.sb("U", [128, 8, T], BF16)
    dU = Dep()
    RR = P.sb("RR", [128, 45056], BF16)
    ACC = RR[:, 0:16384].bitcast(F32).rearrange("p (c t) -> p c t", c=4)
    Y = RR[:, 16384:24576].rearrange("p (c t) -> p c t", c=4)
    M = RR[:, 24576:40960].rearrange("p (c t) -> p c t", c=8)
    AFF = RR[:, 0:45056].rearrange("p (c t) -> p c t", c=22)
    dACC, dY, dM, dAFF = Dep(), Dep(), Dep(), Dep()
    SCB = 40960
    SC = P.sb("SC", [128, SCB // 2], BF16)

    def scv(off, n, dtype):
        assert off % 4 == 0
        if dtype is F32:
            assert off + 4 * n <= SCB, (off, n)
            return SC[:, off // 2: off // 2 + 2 * n].bitcast(F32)
        assert off + 2 * n <= SCB, (off, n)
        return SC[:, off // 2: off // 2 + n]

    NSLAB = 3
    slabs = [(P.sb("slab%d" % i, [128, 8, 512], BF16), Dep()) for i in range(NSLAB)]
    slab_i = [0]
    PV = P.sb("PV", [128, NPV], F32)
    dPV = Dep()
    CF = P.sb("CF", [128, 6, 128], F32)
    CB = P.sb("CB", [128, NCST, 128], BF16)
    dC = Dep()
    MOD = P.sb("MOD", [128, L * 48 * NSEQ], F32)
    dMOD = Dep()
    SV = P.sb("SV", [128, L * 32], F32)
    dSV = Dep()
    CND = P.sb("CND", [128, 8, NSEQ], F32)
    CNDB = P.sb("CNDB", [128, 8, NSEQ], BF16)
    dCND = Dep()
    LW = P.sb("LW", [128, 3584], BF16)
    dLW = Dep()
    W2s = LW[0:64, 0:1024].rearrange("p (d n) -> p d n", d=2)
    A2s = LW[64:128, 0:1024].rearrange("p (d n) -> p d n", d=2)
    G2s = LW[:, 1024:1536]
    WAs = LW[:, 1536:2560].rearrange("p (d j n) -> p d j n", d=2, j=4)
    WXs = LW[:, 2560:3584].rearrange("p (d j n) -> p d j n", d=2, j=4)

    IDf, ONf, LEf, LTf, BKf = (CF[:, i, :] for i in range(5))
    IDb, ONb, LEb, LTb, BKb = (CB[:, i, :] for i in range(5))
    GTf = CF[:, 5, :]
    LMb = [CB[:, 6 + k, :] for k in range(7)]
    LMTb = [CB[:, 13 + k, :] for k in range(7)]

    def pv(name, l, c0=0, n=1):
        o = l * PV_L + PV_OFF[name] + c0
        return PV[:, o:o + n]

    def mod(l, k, c, sq):
        o = ((l * 6 + k) * 8 + c) * NSEQ + sq
        return MOD[:, o:o + 1]

    def load_slab(W2d, k0, nkc, col0, ncols, dst_col=0, new=True):
        if new:
            slab_i[0] += 1
        sl, sd = slabs[slab_i[0] % NSLAB]
        src = W2d[k0 * 128:(k0 + nkc) * 128, col0:col0 + ncols].rearrange("(kc p) n -> p kc n", p=128)
        P.dma(pool, sl[:, 0:nkc, dst_col:dst_col + ncols], src, writes=[sd])
        return sl, sd

    def linear_fm(W2d, nkc, col0, ncols, xfn, xdeps, ntok, evac):
        for cg in range(0, ncols, 512):
            n = min(512, ncols - cg)
            sl, sd = load_slab(W2d, 0, nkc, col0 + cg, n)
            for c in range(0, n, 128):
                cw = min(128, n - c)
                for t0 in range(0, ntok, 512):
                    tn = min(512, ntok - t0)
                    ps, pd = P.psum()
                    for kc in range(nkc):
                        mm(ps[0:cw, 0:tn], sl[:, kc, c:c + cw], xfn(kc, t0, tn), kc == 0, kc == nkc - 1,
                           [sd] + xdeps, [pd])
                    evac(ps, pd, cg + c, cw, t0, tn)

    ufn = lambda kc, t0, tn: U[:, kc, t0:t0 + tn]

    P.dma(sp, PV[:], pvd[:, :], writes=[dPV])
    P.dma(sp, CF[:], cst[:, 0:6, :], writes=[dC])
    P.dma(pool, CB[:], cst[:, :, :], writes=[dC])
    P.dma(sp, CND[:], cT[:, :, :], writes=[dCND])
    actf(CND[:], CND[:], AF.Silu, [dCND], [dCND])
    cp(CNDB[:], CND[:], [dCND], [dCND])
    for l in range(n_layers):
        def ev_mod(ps, pd, col, cw, t0, tn, l=l):
            cidx = col // 128
            o = (l * 48 + cidx) * NSEQ
            ts(MOD[:, o:o + NSEQ], ps[:, 0:NSEQ], pv("ada_b", l, cidx), None, ALU.add, None, [pd, dPV], [dMOD])
        linear_fm(ada_w[l], 8, 0, 6 * D, lambda kc, t0, tn: CNDB[:, kc, :], [dCND], NSEQ, ev_mod)
    def sv(l, o, n=1):
        return SV[:, l * 32 + o: l * 32 + o + n]
    TS = scv(0, 64, F32)
    dTS = Dep()
    for l in range(n_layers):
        lam = pv("lru_lam", l, 0, 8)
        actf(TS[:, 0:8], lam, AF.Abs, [dPV], [dTS])
        actf(TS[:, 0:8], TS[:, 0:8], AF.Exp, [dTS], [dTS], scale=-1.0)
        actf(TS[:, 0:8], TS[:, 0:8], AF.Ln, [dTS], [dTS], bias=1.0)
        ts(TS[:, 8:16], lam, -1.0, 0.0, ALU.mult, ALU.max, [dPV], [dTS])
        tt(TS[:, 0:8], TS[:, 0:8], TS[:, 8:16], ALU.add, [dTS], [dTS])
        ts(sv(l, 0, 8), TS[:, 0:8], -8.0, None, ALU.mult, None, [dTS], [dSV])
        ts(sv(l, 8, 8), TS[:, 0:8], -16.0, None, ALU.mult, None, [dTS], [dSV])
    EX = scv(256, 32, F32)
    SM = scv(384, 8, F32)
    dEX = Dep()
    for l in range(L):
        actf(EX[:, l * 8:(l + 1) * 8], pv("hg_lb", l, 0, 8), AF.Exp, [dPV], [dEX])
    tt(SM[:], EX[:, 0:8], EX[:, 8:16], ALU.add, [dEX], [dEX])
    tt(SM[:], SM[:], EX[:, 16:24], ALU.add, [dEX], [dEX])
    tt(SM[:], SM[:], EX[:, 24:32], ALU.add, [dEX], [dEX])
    recip(SM[:], SM[:], [dEX], [dEX])
    for l in range(L):
        tt(EX[:, l * 8:(l + 1) * 8], EX[:, l * 8:(l + 1) * 8], SM[:], ALU.mult, [dEX], [dEX])
    for l in range(n_layers):
        if l == 0:
            memset(sv(0, 16, 8), 0.0, [dSV])
        elif l == 1:
            cp(sv(1, 16, 8), EX[:, 8:16], [dEX], [dSV])
        else:
            tt(sv(l, 16, 8), sv(l - 1, 16, 8), EX[:, l * 8:(l + 1) * 8], ALU.add, [dEX, dSV], [dSV])
    for l in range(n_layers):
        ts(sv(l, 16, 8), sv(l, 16, 8), 0.0, 1.0, ALU.max, ALU.min, [dSV], [dSV])
        ts(sv(l, 24, 8), sv(l, 16, 8), -1.0, 1.0, ALU.mult, ALU.add, [dSV], [dSV])
    P.barrier()

    GS = P.sb("GS", [128, 64], F32)
    dGS = Dep()

    def norm_mod(src, l, sq, which):
        gname, ksh, ksc = ("n1g", 0, 1) if which == 0 else ("n2g", 3, 4)
        if l >= 0:
            for c in range(8):
                stt(GS[:, which * 8 + c: which * 8 + c + 1], mod(l, ksc, c, sq), 1.0, pv(gname, l, c),
                    ALU.add, ALU.mult, [dMOD, dPV], [dGS])
        HT = scv(0, 4096, F32).rearrange("p (c t) -> p c t", c=8)
        SQ = [scv(16384 + i * 2048, 512, F32) for i in range(2)]
        RS = scv(20480, 512, F32)
        TM = [scv(22528 + i * 2048, 512, F32) for i in range(2)]
        dHT, dSQ, dRS, dTM = Dep(), [Dep(), Dep()], Dep(), [Dep(), Dep()]
        srcv = src.rearrange("(c p) t -> p c t", p=128)
        for tq in range(4):
            P.dma(sp, HT[:, :, :], srcv[:, :, tq * 512:(tq + 1) * 512], writes=[dHT])
            ps, pd = P.psum()
            for c in range(8):
                actf(SQ[c % 2][:], HT[:, c, :], AF.Square, [dHT], [dSQ[c % 2]])
                mm(ps[:, :], ONf, SQ[c % 2][:], c == 0, c == 7, [dSQ[c % 2], dC], [pd], inc=True)
            actf(RS[:], ps[:, :], AF.Sqrt, [pd], [dRS], bias=1e-6, scale=1.0 / D)
            recip(RS[:], RS[:], [dRS], [dRS])
            for c in range(8):
                tt(TM[c % 2][:], HT[:, c, :], RS[:], ALU.mult, [dHT, dRS], [dTM[c % 2]])
                if l >= 0:
                    actf(U[:, c, tq * 512:(tq + 1) * 512], TM[c % 2][:], AF.Identity, [dTM[c % 2], dGS, dMOD], [dU],
                         bias=mod(l, ksh, c, sq), scale=GS[:, which * 8 + c: which * 8 + c + 1])
                else:
                    ts(HT[:, c, :], TM[c % 2][:], PV[:, PV_L * L + c: PV_L * L + c + 1], None, ALU.mult, None,
                       [dTM[c % 2], dPV], [dHT])
            if l < 0:
                P.dma(sp, outT[sq].rearrange("(c p) t -> p c t", p=128)[:, :, tq * 512:(tq + 1) * 512], HT[:, :, :],
                      reads=[dHT])

    def gate_merge(l, n, first):
        SG = [scv(i * 2048, 512, F32) for i in range(2)]
        TP = [scv(4096 + i * 2048, 512, F32) for i in range(2)]
        dSG, dTP = [Dep(), Dep()], [Dep(), Dep()]
        k = 0
        for cg in range(2):
            sg_, sgd = load_slab(w_in[l], 0, 8, O_GATE + n * D + cg * 512, 512)
            sb_, sbd = load_slab(w_branch[l, n], 0, 4, cg * 512, 512)
            for c in range(4):
                cc = cg * 4 + c
                for tq in range(4):
                    tsl = slice(tq * 512, (tq + 1) * 512)
                    pg, pgd = P.psum()
                    for kc in range(8):
                        mm(pg[:, :], sg_[:, kc, c * 128:(c + 1) * 128], U[:, kc, tsl], kc == 0, kc == 7, [sgd, dU], [pgd])
                    py, pyd = P.psum()
                    for kc in range(4):
                        mm(py[:, :], sb_[:, kc, c * 128:(c + 1) * 128], Y[:, kc, tsl], kc == 0, kc == 3, [sbd, dY], [pyd])
                    i = k % 2
                    k += 1
                    actf(SG[i][:], pg[:, :], AF.Sigmoid, [pgd], [dSG[i]])
                    if first:
                        tt(M[:, cc, tsl], py[:, :], SG[i][:], ALU.mult, [pyd, dSG[i]], [dM])
                    else:
                        tt(TP[i][:], py[:, :], SG[i][:], ALU.mult, [pyd, dSG[i]], [dTP[i]])
                        tt(M[:, cc, tsl], M[:, cc, tsl], TP[i][:], ALU.add, [dTP[i], dM], [dM])

    def out_proj_residual(l, sq, src, dst):
        HT = scv(0, 4096, F32).rearrange("p (c t) -> p c t", c=8)
        dHT = Dep()
        s0, s0d = load_slab(w_out[l], 0, 8, 0, 512)
        s1, s1d = load_slab(w_out[l], 0, 8, 512, 512)
        srcv = src.rearrange("(c p) t -> p c t", p=128)
        dstv = dst.rearrange("(c p) t -> p c t", p=128)
        for tq in range(4):
            tsl = slice(tq * 512, (tq + 1) * 512)
            P.dma(sp, HT[:, :, :], srcv[:, :, tsl], writes=[dHT])
            for c2 in range(8):
                sl, sd = (s0, s0d) if c2 < 4 else (s1, s1d)
                ps, pd = P.psum()
                for kc in range(8):
                    mm(ps[:, :], sl[:, kc, (c2 % 4) * 128:(c2 % 4 + 1) * 128], M[:, kc, tsl], kc == 0, kc == 7, [sd, dM], [pd])
                stt(HT[:, c2, :], ps[:, :], mod(l, 2, c2, sq), HT[:, c2, :], ALU.mult, ALU.add, [pd, dHT, dMOD], [dHT])
            P.dma(sp, dstv[:, :, tsl], HT[:, :, :], reads=[dHT])

    def ffn(l, sq, hsrc):
        ZV = scv(0, 2050, F32)
        ZG = scv(8208, 2050, F32)
        CV = scv(16416, 2048, F32)
        CG = scv(24608, 2048, F32)
        dZV, dZG, dCV, dCG = Dep(), Dep(), Dep(), Dep()
        memset(ZV[:, 0:1], 0.0, [dZV])
        memset(ZV[:, 2049:2050], 0.0, [dZV])
        memset(ZG[:, 0:1], 0.0, [dZG])
        memset(ZG[:, 2049:2050], 0.0, [dZG])
        up = ffn_up[l]
        for j in range(22):
            sl, sd = load_slab(up, 0, 8, j * 128, 128, 0)
            load_slab(up, 0, 8, DFF + j * 128, 128, 128, new=False)
            for half, Z, dZ in ((0, ZV, dZV), (1, ZG, dZG)):
                for tq in range(4):
                    ps, pd = P.psum()
                    for kc in range(8):
                        mm(ps[:, :], sl[:, kc, half * 128:(half + 1) * 128], U[:, kc, tq * 512:(tq + 1) * 512],
                           kc == 0, kc == 7, [sd, dU], [pd])
                    cp(Z[:, 1 + tq * 512: 1 + (tq + 1) * 512], ps[:, :], [pd], [dZ], eng=act)
            for half, Z, dZ, C, dCx in ((0, ZV, dZV, CV, dCV), (1, ZG, dZG, CG, dCG)):
                jj = half * 22 + j
                cw = lambda tp: pv("ffn_cw", l, tp * 44 + jj)
                ts(C[:], Z[:, 1:2049], cw(1), pv("ffn_cb", l, jj), ALU.mult, ALU.add, [dZ, dPV], [dCx])
                stt(C[:], Z[:, 0:2048], cw(0), C[:], ALU.mult, ALU.add, [dZ, dCx], [dCx])
                stt(C[:], Z[:, 2:2050], cw(2), C[:], ALU.mult, ALU.add, [dZ, dCx], [dCx])
            actf(CG[:], CG[:], AF.Silu, [dCG], [dCG])
            tt(AFF[:, j, :], CV[:], CG[:], ALU.mult, [dCV, dCG], [dAFF])
        P.barrier()
        HT = scv(0, 2048, F32).rearrange("p (c t) -> p c t", c=4)
        dHT = Dep()
        hv = hsrc.rearrange("(c p) t -> p c t", p=128)
        for cg in range(2):
            sls = []
            for pi, (k0, nk) in enumerate(((0, 8), (8, 8), (16, 6))):
                sls.append(load_slab(ffn_down[l], k0, nk, cg * 512, 512) + (k0, nk))
            for tq in range(4):
                tsl = slice(tq * 512, (tq + 1) * 512)
                P.dma(sp, HT[:, :, :], hv[:, cg * 4:(cg + 1) * 4, tsl], writes=[dHT])
                for c2 in range(4):
                    ps, pd = P.psum()
                    for sl, sd, k0, nk in sls:
                        for kc in range(nk):
                            kk = k0 + kc
                            mm(ps[:, :], sl[:, kc, c2 * 128:(c2 + 1) * 128], AFF[:, kk, tsl], kk == 0, kk == 21,
                               [sd, dAFF], [pd])
                    stt(HT[:, c2, :], ps[:, :], mod(l, 5, cg * 4 + c2, sq), HT[:, c2, :], ALU.mult, ALU.add,
                        [pd, dHT, dMOD], [dHT])
                P.dma(sp, hv[:, cg * 4:(cg + 1) * 4, tsl], HT[:, :, :], reads=[dHT])

    def tk(t0, n, d):
        if d == 0:
            return slice(t0, t0 + n)
        a = T - 1 - t0
        b = a - n
        return slice(a, None if b < 0 else b, -1)

    def load_layer_small(l):
        P.dma(pool, W2s, rw_w2[l].rearrange("d p n -> p d n"), writes=[dLW])
        P.dma(pool, A2s, rw_a2[l].rearrange("d p n -> p d n"), writes=[dLW])
        P.dma(pool, G2s, rw_g2[l], writes=[dLW])
        P.dma(pool, WAs, lru_wa[l].rearrange("d j p n -> p d j n"), writes=[dLW])
        P.dma(pool, WXs, lru_wx[l].rearrange("d j p n -> p d j n"), writes=[dLW])

    def mixer_lru(l):
        XP = scv(0, 2051, F32)
        XC = scv(8208, 2048, F32)
        XCB = scv(16400, 2048, BF16)
        Ba = scv(20496, 2048, F32)
        Bm = scv(28688, 2048, F32)
        Bx, Bh, Bg, By = (ACC[:, i, :] for i in range(4))
        dXP, dXC, dXCB, dBa, dBm, dBx, dBh, dBg, dBy = (Dep() for _ in range(9))
        memset(XP[:, 0:1], 0.0, [dXP])
        memset(XP[:, 2049:2051], 0.0, [dXP])
        for j in range(4):
            linear_fm(w_in[l], 8, O_LRU + j * 128, 128, ufn, [dU], T,
                      lambda ps, pd, col, cw, t0, tn: cp(XP[:, 1 + t0:1 + t0 + tn], ps[:, 0:tn], [pd], [dXP], eng=act))
            linear_fm(w_in[l], 8, O_LRU + 512 + j * 128, 128, ufn, [dU], T,
                      lambda ps, pd, col, cw, t0, tn: actf(Bg[:, t0:t0 + tn], ps[:, 0:tn], AF.Gelu, [pd], [dBg]))
            cwv = lambda tp: pv("lru_cw", l, tp * 4 + j)
            ts(XC[:], XP[:, 0:T], cwv(0), pv("lru_cb", l, j), ALU.mult, ALU.add, [dXP, dPV], [dXC])
            for tp in range(1, 4):
                stt(XC[:], XP[:, tp:tp + T], cwv(tp), XC[:], ALU.mult, ALU.add, [dXP, dXC], [dXC])
            cp(XCB[:], XC[:], [dXC], [dXCB])
            for d in range(2):
                for tq in range(4):
                    tsl = slice(tq * 512, (tq + 1) * 512)
                    ps, pd = P.psum()
                    mm(ps[:, :], WAs[:, d, j, :], XCB[:, tsl], True, True, [dLW, dXCB], [pd])
                    actf(Ba[:, tsl], ps[:, :], AF.Sigmoid, [pd, dPV], [dBa], bias=pv("lru_ba", l, d * 4 + j))
                    ps, pd = P.psum()
                    mm(ps[:, :], WXs[:, d, j, :], XCB[:, tsl], True, True, [dLW, dXCB], [pd])
                    actf(Bx[:, tsl], ps[:, :], AF.Sigmoid, [pd, dPV], [dBx], bias=pv("lru_bx", l, d * 4 + j))
                actf(Bm[:], Ba[:], AF.Exp, [dBa, dSV], [dBm], scale=sv(l, 8 + d * 4 + j))
                actf(Ba[:], Ba[:], AF.Exp, [dBa, dSV], [dBa], scale=sv(l, d * 4 + j))
                ts(Bm[:], Bm[:], 1.0, -1.0, ALU.min, ALU.mult, [dBm], [dBm])
                actf(Bm[:], Bm[:], AF.Sqrt, [dBm], [dBm], bias=1.0)
                tt(Bx[:], Bx[:], XC[:], ALU.mult, [dBx, dXC], [dBx])
                tt(Bm[:], Bm[:], Bx[:], ALU.mult, [dBm, dBx], [dBm])
                if d == 0:
                    scan(By[:], Ba[:], Bm[:], 0.0, [dBa, dBm], [dBy])
                else:
                    scan(Bh[:, ::-1], Ba[:, ::-1], Bm[:, ::-1], 0.0, [dBa, dBm], [dBh])
                    tt(By[:], By[:], Bh[:], ALU.add, [dBy, dBh], [dBy])
            tt(Y[:, j, :], By[:], Bg[:], ALU.mult, [dBy, dBg], [dY])

    PADB = RR[:, 40960:45056]

    HN_DEPS = (Dep(), Dep())

    def head_rmsnorm_gate(l, gname, h, gate_mul):
        SQ = PADB[:, 0:1024].bitcast(F32)
        RS = PADB[:, 1024:2048].bitcast(F32)
        dSQ, dRS = HN_DEPS
        for tq in range(4):
            tsl = slice(tq * 512, (tq + 1) * 512)
            actf(SQ[:], ACC[:, h, tsl], AF.Square, [dACC], [dSQ])
            ps, pd = P.psum()
            mm(ps[:, :], ONf, SQ[:], True, True, [dSQ, dC], [pd])
            actf(RS[:], ps[:, :], AF.Sqrt, [pd], [dRS], bias=1e-6, scale=1.0 / 128)
            recip(RS[:], RS[:], [dRS], [dRS])
            tt(SQ[:], ACC[:, h, tsl], RS[:], ALU.mult, [dACC, dRS, dSQ], [dSQ])
            stt(Y[:, h, tsl], SQ[:], pv(gname, l, h), Y[:, h, tsl], ALU.mult, ALU.mult, [dSQ, dPV, dY], [dY])

    def mixer_mlstm(l):
        QF = scv(0, 4096, BF16).rearrange("p (c t) -> p c t", c=2)
        KF = scv(8192, 4096, BF16).rearrange("p (c t) -> p c t", c=2)
        WKV = scv(16384, 6144, BF16).rearrange("p (k n) -> p k n", k=8)
        WGm = scv(28672, 128, BF16).rearrange("p (k n) -> p k n", k=8)
        o = [28928]

        def al(n, dtype):
            v = scv(o[0], n, dtype)
            o[0] += (n * (4 if dtype is F32 else 2) + 3) // 4 * 4
            return v
        KVT = al(768, BF16)
        IG, NLF, NEGC, KS = al(4, F32), al(4, F32), al(4, F32), al(4, F32)
        NLFB, DT, RD, HO = al(128, F32), al(128, F32), al(128, F32), al(128, F32)
        PW, QP = al(128, BF16), al(128, BF16)
        KTx = [al(128, BF16), al(128, BF16)]
        EBE = al(1, F32)
        Cst = al(258, F32).rearrange("p (c n) -> p c n", c=2)
        CBf = al(256, BF16).rearrange("p (c n) -> p c n", c=2)
        NBb = al(256, BF16).rearrange("p (c n) -> p c n", c=2)
        dQF, dKF, dWKV, dKVT, dG, dNLFB, dDT, dRD, dHO, dPW, dQP, dKT, dEBE, dCst, dCB = (Dep() for _ in range(15))
        linear_fm(w_in[l], 8, O_ML, 256, ufn, [dU], T,
                  lambda ps, pd, col, cw, t0, tn: ts(QF[:, col // 128, t0:t0 + tn], ps[:, 0:tn], 0.125, None, ALU.mult, None, [pd], [dQF]))
        linear_fm(w_in[l], 8, O_ML + 256, 256, ufn, [dU], T,
                  lambda ps, pd, col, cw, t0, tn: cp(KF[:, col // 128, t0:t0 + tn], ps[:, 0:tn], [pd], [dKF], eng=act))
        linear_fm(w_in[l], 8, O_ML + 1024, 512, ufn, [dU], T,
                  lambda ps, pd, col, cw, t0, tn: actf(Y[:, col // 128, t0:t0 + tn], ps[:, 0:tn], AF.Sigmoid, [pd], [dY]))
        wv = w_in[l]
        P.dma(pool, WKV[:, :, :], wv[:, O_ML + 256:O_ML + 1024].rearrange("(kc p) n -> p kc n", p=128), writes=[dWKV])
        P.dma(pool, WGm[:, :, :], wv[:, O_ML + 1536:O_ML + 1552].rearrange("(kc p) n -> p kc n", p=128), writes=[dWKV])
        memset(KTx[0][:], 0.0, [dKT])
        memset(KTx[1][:], 0.0, [dKT])
        GB = pv("ml_gb", l, 0, 16)
        URt = al(1024, BF16).rearrange("p (k n) -> p k n", k=8)
        dUR = Dep()
        TMPR = PADB[:, 0:4096].rearrange("p (c t) -> p c t", c=2)
        dTR = Dep()
        for d in range(2):
            if d == 1:
                for BUF, dB in ((QF, dQF), (KF, dKF)):
                    cp(TMPR[:, :, :], BUF[:, :, ::-1], [dB], [dTR])
                    cp(BUF[:, :, :], TMPR[:, :, :], [dTR], [dB])
            memset(Cst[:, :, :], 0.0, [dCst])
            memset(CBf[:, :, :], 0.0, [dCB])
            memset(NBb[:, :, :], 0.0, [dCB])
            for tau in range(16):
                tsl = tk(tau * 128, 128, d)
                pt = slice(tau * 128, tau * 128 + 128)
                if d == 1:
                    cp(URt[:, :, :], U[:, :, tsl], [dU], [dUR])
                    uf = lambda kc: URt[:, kc, :]
                else:
                    uf = lambda kc: U[:, kc, pt]
                for c0, cn in ((0, 512), (512, 256)):
                    ps, pd = P.psum()
                    for kc in range(8):
                        mm(ps[:, 0:cn], uf(kc), WKV[:, kc, c0:c0 + cn], kc == 0, kc == 7, [dU, dWKV, dUR], [pd])
                    cp(KVT[:, c0:c0 + cn], ps[:, 0:cn], [pd], [dKVT], eng=act)
                psg, pgd = P.psum()
                for kc in range(8):
                    mm(psg[:, 0:16], uf(kc), WGm[:, kc, :], kc == 0, kc == 7, [dU, dWKV, dUR], [pgd])
                tt(IG[:], psg[:, d * 4:d * 4 + 4], GB[:, d * 4:d * 4 + 4], ALU.add, [pgd, dPV], [dG])
                tt(NLF[:], psg[:, 8 + d * 4:12 + d * 4], GB[:, 8 + d * 4:12 + d * 4], ALU.add, [pgd, dPV], [dG])
                actf(NLF[:], NLF[:], AF.Exp, [dG], [dG], scale=-1.0)
                actf(NLF[:], NLF[:], AF.Ln, [dG], [dG], bias=1.0)
                psb, pbd = P.psum()
                mm(psb[:, 0:4], LEf, NLF[:], True, True, [dC, dG], [pbd])
                tt(NEGC[:], psb[:, 0:4], IG[:], ALU.add, [pbd, dG], [dG])
                for h in range(4):
                    hp = slice((h % 2) * 64, (h % 2) * 64 + 64)
                    hc = h // 2
                    ts(NLFB[:], ONf, NLF[:, h:h + 1], None, ALU.mult, None, [dC, dG], [dNLFB])
                    bb, bbd = P.psum()
                    mm(bb[:, 0:128], NLFB[:], LEf, True, True, [dNLFB, dC], [bbd])
                    actf(EBE[:], bb[:, 127:128], AF.Exp, [bbd], [dEBE], scale=-1.0)
                    actf(KS[:, h:h + 1], bb[:, 127:128], AF.Exp, [bbd, dG], [dG], scale=-1.0, bias=NEGC[:, h:h + 1])
                    actf(DT[:], bb[:, 0:128], AF.Exp, [bbd, dG], [dDT], scale=-1.0, bias=NEGC[:, h:h + 1])
                    tt(DT[:], DT[:], LEf, ALU.mult, [dDT, dC], [dDT])
                    st, std = P.psum()
                    mm(st[:, 0:128], KF[hp, hc, pt], QF[hp, hc, pt], True, True, [dKF, dQF], [std])
                    tt(PW[:], st[:, 0:128], DT[:], ALU.mult, [std, dDT], [dPW])
                    actf(RD[hp, :], bb[hp, 0:128], AF.Exp, [bbd], [dRD], scale=-1.0)
                    tt(QP[hp, :], QF[hp, hc, pt], RD[hp, :], ALU.mult, [dQF, dRD], [dQP])
                    nu, nud = P.psum()
                    mm(nu[:, 0:128], KVT[:, 256 + h * 128:384 + h * 128], PW[:], True, False, [dKVT, dPW], [nud])
                    mm(nu[:, 0:128], CBf[hp, hc, :], QP[hp, :], False, True, [dCB, dQP], [nud])
                    de, ded = P.psum()
                    mm(de[:, 0:128], ONb, PW[:], True, False, [dC, dPW], [ded])
                    mm(de[:, 0:128], NBb[hp, hc, :], QP[hp, :], False, True, [dCB, dQP], [ded])
                    actf(RD[:], de[:, 0:128], AF.Abs, [ded], [dRD])
                    ts(RD[:], RD[:], 1.0, None, ALU.max, None, [dRD], [dRD])
                    recip(RD[:], RD[:], [dRD], [dRD])
                    if d == 0:
                        tt(ACC[:, h, tsl], nu[:, 0:128], RD[:], ALU.mult, [nud, dRD], [dACC])
                    else:
                        tt(HO[:], nu[:, 0:128], RD[:], ALU.mult, [nud, dRD], [dHO])
                        tt(ACC[:, h, tsl], ACC[:, h, tsl], HO[:], ALU.add, [dHO, dACC], [dACC])
                    kt = KTx[h % 2]
                    ts(kt[:, hp], KVT[:, h * 64:h * 64 + 64], KS[:, h:h + 1], None, ALU.mult, None, [dKVT, dG], [dKT])
                    pc, pcd = P.psum()
                    mm(pc[:, 0:128], kt[:], KVT[:, 256 + h * 128:384 + h * 128], True, True, [dKT, dKVT], [pcd], inc=False)
                    mm(pc[:, 128:129], kt[:], ONb[:, 0:1], True, True, [dKT, dC], [pcd], inc=True)
                    stt(Cst[hp, hc, :], Cst[hp, hc, :], EBE[hp, :], pc[hp, 0:129], ALU.mult, ALU.add, [dCst, dEBE, pcd], [dCst])
                    cp(CBf[hp, hc, :], Cst[hp, hc, 0:128], [dCst], [dCB], eng=act)
                    ts(NBb[hp, hc, :], ONf[hp, :], Cst[hp, hc, 128:129], None, ALU.mult, None, [dCst, dC], [dCB])
        for h in range(4):
            head_rmsnorm_gate(l, "ml_norm", h, True)

    def mixer_hgrn2(l):
        LF = scv(0, 2048, F32)
        G = scv(8192, 2048, F32)
        Kb = scv(16384, 2048, BF16)
        QS = scv(20480, 2048, BF16)
        WI = scv(24576, 1024, BF16).rearrange("p (k n) -> p k n", k=8)
        o = [26624]

        def al(n, dtype):
            v = scv(o[0], n, dtype)
            o[0] += (n * (4 if dtype is F32 else 2) + 3) // 4 * 4
            return v
        GLt, EXt, Sst = (al(128, F32) for _ in range(3))
        QTt, QHt, ATT, VT, KHT, SBb = (al(128, BF16) for _ in range(6))
        TM = al(9 * 128, F32).rearrange("p (k n) -> p k n", k=9)
        KTLa = al(9 * 128, BF16).rearrange("p (k n) -> p k n", k=9)
        EGE = al(1, F32)
        URt = al(1024, BF16).rearrange("p (k n) -> p k n", k=8)
        dUR = Dep()
        dLF, dG, dKb, dQS, dWI, dGL, dEX, dTMP, dEXk, dS, dQT, dQH, dKH, dATT, dVT, dKHT, dSB, dKTL, dEGE = (Dep() for _ in range(19))
        dKTLs = [Dep() for _ in range(8)]
        for h in range(4):
            linear_fm(w_in[l], 8, O_HG + h * 128, 128, ufn, [dU], T,
                      lambda ps, pd, col, cw, t0, tn: actf(QS[:, t0:t0 + tn], ps[:, 0:tn], AF.Silu, [pd], [dQS]))
            linear_fm(w_in[l], 8, O_HG + 2048 + h * 128, 128, ufn, [dU], T,
                      lambda ps, pd, col, cw, t0, tn: actf(Y[:, h, t0:t0 + tn], ps[:, 0:tn], AF.Silu, [pd], [dY]))
            P.dma(pool, WI[:, :, :], w_in[l][:, O_HG + 1536 + h * 128:O_HG + 1664 + h * 128].rearrange("(kc p) n -> p kc n", p=128),
                  writes=[dWI])
            for d in range(2):
                linear_fm(w_in[l], 8, O_HG + 512 + d * 512 + h * 128, 128, ufn, [dU], T,
                          lambda ps, pd, col, cw, t0, tn: actf(LF[:, tk(t0, tn, d)], ps[:, 0:tn], AF.Sigmoid, [pd], [dLF]))
                ts(LF[:], LF[:], sv(l, 24 + d * 4 + h), sv(l, 16 + d * 4 + h), ALU.mult, ALU.add, [dLF, dSV], [dLF])
                ts(Kb[:], LF[:], -1.0, 1.0, ALU.mult, ALU.add, [dLF], [dKb])
                actf(LF[:], LF[:], AF.Ln, [dLF], [dLF])
                for tau in range(16):
                    pt = slice(tau * 128, tau * 128 + 128)
                    scan(G[:, pt], ONf, LF[:, pt], 0.0, [dLF, dC], [dG])
                memset(Sst[:], 0.0, [dS])
                memset(SBb[:], 0.0, [dSB])
                for tau in range(16):
                    t0 = tau * 128
                    pt = slice(t0, t0 + 128)
                    nt = tk(t0, 128, d)
                    Gt = G[:, pt]
                    Gt3 = Gt.rearrange("p (b i) -> p b i", b=8)
                    GL3 = GLt.rearrange("p (b i) -> p b i", b=8)
                    cp(GL3[:, 0, :], Gt3[:, 0, :], [dG], [dGL], eng=pool)
                    tt(GL3[:, 1:8, :], Gt3[:, 1:8, :], Gt3[:, 0:7, 15:16].to_broadcast([128, 7, 16]), ALU.subtract, [dG], [dGL], eng=pool)
                    actf(EXt[:], GLt[:], AF.Exp, [dGL], [dEX])
                    tt(QTt[:], QS[:, nt], EXt[:], ALU.mult, [dQS, dEX], [dQT])
                    actf(EXt[:], Gt, AF.Exp, [dG, dQT], [dEX])
                    tt(QHt[:], QS[:, nt], EXt[:], ALU.mult, [dQS, dEX], [dQH])
                    cp(TM[:, 0, :], Gt, [dG], [dTMP], eng=pool)
                    tt(TM[:, 1:9, :], Gt.unsqueeze(1).to_broadcast([128, 8, 128]),
                       Gt3[:, 0:8, 15:16].to_broadcast([128, 8, 128]), ALU.subtract, [dG], [dTMP])
                    actf(TM[:, :, :], TM[:, :, :], AF.Exp, [dTMP], [dTMP], scale=-1.0)
                    stt(KTLa[:, :, :], TM[:, :, :], 1e26, Kb[:, pt].unsqueeze(1).to_broadcast([128, 9, 128]),
                        ALU.min, ALU.mult, [dTMP, dKb], [dKTL])
                    at, atd = P.psum()
                    for I in range(8):
                        mm(at[:, 16 * I:16 * I + 16], KTLa[:, I, :], QTt[:, 16 * I:16 * I + 16], True, True, [dKTL, dQT], [atd],
                           inc=(I == 7))
                    tt(ATT[:], at[:, 0:128], LEf, ALU.mult, [atd, dC], [dATT])
                    KH = KTLa[:, 8, :]
                    actf(EGE[:], Gt[:, 127:128], AF.Exp, [dG], [dEGE])
                    vp, vpd = P.psum()
                    if d == 1:
                        cp(URt[:, :, :], U[:, :, nt], [dU], [dUR])
                    for kc in range(8):
                        mm(vp[:, 0:128], URt[:, kc, :] if d == 1 else U[:, kc, pt], WI[:, kc, :], kc == 0, kc == 7,
                           [dU, dWI, dUR], [vpd])
                    cp(VT[:], vp[:, 0:128], [vpd], [dVT], eng=act)
                    op_, opd = P.psum()
                    mm(op_[:, 0:128], VT[:], ATT[:], True, False, [dVT, dATT], [opd])
                    mm(op_[:, 0:128], SBb[:], QHt[:], False, True, [dSB, dQH], [opd])
                    if d == 0:
                        cp(ACC[:, h, nt], op_[:, 0:128], [opd], [dACC], eng=act)
                    else:
                        tt(ACC[:, h, nt], ACC[:, h, nt], op_[:, 0:128], ALU.add, [opd, dACC], [dACC])
                    tp_, tpd = P.psum()
                    mm(tp_[:, 0:128], KH, IDb, True, True, [dKTL, dC], [tpd])
                    cp(KHT[:], tp_[:, 0:128], [tpd], [dKHT], eng=act)
                    sp_, spd = P.psum()
                    mm(sp_[:, 0:128], KHT[:], VT[:], True, True, [dKHT, dVT], [spd])
                    stt(Sst[:], Sst[:], EGE[:], sp_[:, 0:128], ALU.mult, ALU.add, [dS, dEGE, spd], [dS])
                    cp(SBb[:], Sst[:], [dS], [dSB], eng=pool)
            head_rmsnorm_gate(l, "hg_norm", h, True)

    MUV = P.sb("MUV", [128, 40], F32)
    dMUV = Dep()

    def mixer_rwkv(l):
        deps = {}

        def dp(n):
            if n not in deps:
                deps[n] = Dep()
            return deps[n]
        P0 = scv(0, 2050, F32)
        PF = scv(8208, 2048, F32)
        G = scv(16400, 2048, F32)
        Bb = scv(24592, 2048, BF16)
        KT = [scv(28688, 2048, BF16), scv(32784, 2048, BF16)]
        TXW, XA, SXG, Rr, Vv, KK, Kraw, GG = (M[:, i, :] for i in range(8))
        ts(MUV[:, 0:14], pv("rw_mu", l, 0, 14), -1.0, 1.0, ALU.mult, ALU.add, [dPV], [dMUV])
        ts(MUV[:, 14:28], pv("rw_mu", l, 0, 14), 0.5, None, ALU.mult, None, [dPV], [dMUV])
        ts(MUV[:, 28:32], pv("rw_ka", l, 0, 4), -1.0, 1.0, ALU.mult, ALU.add, [dPV], [dMUV])
        ts(MUV[:, 32:36], pv("rw_rk", l, 0, 4), 0.5, None, ALU.mult, None, [dPV], [dMUV])

        def shifted(ci):
            memset(P0[:, 0:1], 0.0, [dp("P0")])
            memset(P0[:, 2049:2050], 0.0, [dp("P0")])
            linear_fm(w_in[l], 8, O_RW + ci * 128, 128, ufn, [dU], T,
                      lambda ps, pd, col, cw, t0, tn: cp(P0[:, 1 + t0:1 + t0 + tn], ps[:, 0:tn], [pd], [dp("P0")], eng=act))
            tt(PF[:], P0[:, 0:T], P0[:, 2:T + 2], ALU.add, [dp("P0")], [dp("PF")])
            ts(PF[:], PF[:], MUV[:, 14 + ci:15 + ci], None, ALU.mult, None, [dp("PF"), dMUV], [dp("PF")])
            stt(PF[:], P0[:, 1:T + 1], MUV[:, ci:ci + 1], PF[:], ALU.mult, ALU.add, [dp("P0"), dp("PF"), dMUV], [dp("PF")])

        shifted(12)
        actf(TXW[0:64, :], PF[0:64, :], AF.Tanh, [dp("PF")], [dp("TXW")])
        cp(XA[64:128, :], PF[64:128, :], [dp("PF")], [dp("XA")])
        shifted(13)
        actf(SXG[:, :], PF[:], AF.Sigmoid, [dp("PF")], [dp("SXG")])
        SQt = P0[:, 0:512]
        RSt = P0[:, 512:1024]
        T1 = P0[:, 1024:1536]
        HP = [slice(0, 64), slice(64, 128)]
        for j in range(rw_limit[0]):
            P.barrier()
            regions = [(SC, 18440, 20480), (RR, 40960, 45056)] + [(RR, k * 4096, (k + 1) * 4096) for k in range(4) if k != j]
            ri, ro = [0], [regions[0][1]]

            def al(n, dtype=BF16):
                ne = n * (2 if dtype is F32 else 1)
                ne = (ne + 1) // 2 * 2
                while ro[0] + ne > regions[ri[0]][2]:
                    ri[0] += 1
                    ro[0] = regions[ri[0]][1]
                t_, a_ = regions[ri[0]][0], ro[0]
                ro[0] += ne
                v = t_[:, a_:a_ + ne]
                return v.bitcast(F32) if dtype is F32 else v

            def mkset(sid):
                B = {"sid": sid}
                for nm in ("AT", "RT", "BT", "KTt", "BH", "KHh", "VTf", "ATk", "VTk", "BHk", "KHk", "PT"):
                    B[nm] = al(128)
                for nm in ("EXa", "EXb", "EXc", "EXd"):
                    B[nm] = al(128, F32)
                for nm in ("NT0", "WS", "AAK", "ARB", "ARK", "Xb", "Tb", "TTb", "UT"):
                    B[nm] = [al(128), al(128)]
                B["MT"] = [al(128, F32), al(128, F32)]
                B["GC"] = al(64, F32)
                return B
            NSET = 3
            sets = [mkset(i) for i in range(NSET)]
            Sf = al(128, F32)
            Sb = al(128)
            for B in sets:
                for b_ in (B["UT"][0], B["UT"][1], B["MT"][0], B["MT"][1]):
                    memset(b_[:], 0.0, [dp("misc%d" % B["sid"])], eng=pool)
            shifted(j)
            cp(Rr[:, :], PF[:], [dp("PF")], [dp("R")], eng=act)
            shifted(8 + j)
            cp(Vv[:, :], PF[:], [dp("PF")], [dp("V")], eng=act)
            shifted(4 + j)
            cp(Kraw[:, :], PF[:], [dp("PF")], [dp("Kraw")], eng=act)
            ts(PF[:], PF[:], pv("rw_kk", l, j), None, ALU.mult, None, [dp("PF"), dPV], [dp("PF")])
            for tq in range(4):
                tsl = slice(tq * 512, (tq + 1) * 512)
                actf(SQt, PF[:, tsl], AF.Square, [dp("PF")], [dp("P0")])
                ps, pd = P.psum()
                mm(ps[:, :], BKf, SQt, True, True, [dC, dp("P0")], [pd])
                ts(RSt, ps[:, :], 1e-24, None, ALU.max, None, [pd], [dp("P0")])
                actf(RSt, RSt, AF.Sqrt, [dp("P0")], [dp("P0")])
                recip(RSt, RSt, [dp("P0")], [dp("P0")])
                tt(KK[:, tsl], PF[:, tsl], RSt, ALU.mult, [dp("PF"), dp("P0")], [dp("KK")])
                ps, pd = P.psum()
                mm(ps[:, :], G2s[:, j * 128:(j + 1) * 128], SXG[:, tsl], True, True, [dLW, dp("SXG")], [pd])
                cp(GG[:, tsl], ps[:, :], [pd], [dp("GG")], eng=act)
            AS = P0[:, 0:2048]
            for d in range(rw_limit[1]):
                for tq in range(4):
                    tsl = slice(tq * 512, (tq + 1) * 512)
                    ps, pd = P.psum()
                    mm(ps[:, :], W2s[:, d, j * 128:(j + 1) * 128], TXW[0:64, tsl], True, True, [dLW, dp("TXW")], [pd])
                    actf(PF[:, tk(tq * 512, 512, d)], ps[:, :], AF.Sigmoid, [pd, dPV], [dp("PF")], bias=pv("rw_w0", l, d * 4 + j))
                    ps, pd = P.psum()
                    mm(ps[:, :], A2s[:, d, j * 128:(j + 1) * 128], XA[64:128, tsl], True, True, [dLW, dp("XA")], [pd])
                    actf(AS[:, tk(tq * 512, 512, d)], ps[:, :], AF.Sigmoid, [pd, dPV], [dp("P0")], bias=pv("rw_a0", l, d * 4 + j))
                ts(PF[:], PF[:], -0.6065306597, None, ALU.mult, None, [dp("PF")], [dp("PF")])
                for tau in range(16):
                    pt = slice(tau * 128, tau * 128 + 128)
                    scan(G[:, pt], ONf, PF[:, pt], 0.0, [dp("PF"), dC], [dp("G")])
                rv = slice(None, None, -1) if d == 1 else slice(None)
                dKT = dp("KT%d" % d)
                ts(KT[d][:], AS, pv("rw_ka", l, j), MUV[:, 28 + j:29 + j], ALU.mult, ALU.add, [dp("P0"), dPV, dMUV], [dKT])
                tt(KT[d][:], KT[d][:], Kraw[:, rv], ALU.mult, [dKT, dp("Kraw")], [dKT])
                tt(Bb[:], KK[:, rv], AS, ALU.mult, [dp("KK"), dp("P0")], [dp("Bb")])
                memset(Sf[:], 0.0, [dp("Sf0"), dp("Sf1")], eng=pool)
                memset(Sb[:], 0.0, [dp("Sb0"), dp("Sb1")], eng=pool)

                def D(B, nm, e=None):
                    return dp("%s%s_%d" % (nm, "" if e is None else str(e), B["sid"]))

                def prep(tau, B):
                    pt = slice(tau * 128, tau * 128 + 128)
                    nt = tk(tau * 128, 128, d)
                    Gt = G[:, pt]
                    actf(B["EXa"][:], Gt, AF.Exp, [dp("G")], [D(B, "EXa")])
                    tt(B["RT"][:], Rr[:, nt], B["EXa"][:], ALU.mult, [dp("R"), D(B, "EXa")], [D(B, "RT")])
                    cp(B["EXb"][:, 1:128], B["EXa"][:, 0:127], [D(B, "EXa")], [D(B, "EXb")], eng=pool)
                    memset(B["EXb"][:, 0:1], 1.0, [D(B, "EXb")], eng=pool)
                    stt(B["AT"][:], KK[:, nt], -1.0, B["EXb"][:], ALU.mult, ALU.mult, [dp("KK"), D(B, "EXb")], [D(B, "AT")])
                    actf(B["EXc"][:], Gt, AF.Exp, [dp("G")], [D(B, "EXc")], scale=-1.0)
                    tt(B["BT"][:], Bb[:, pt], B["EXc"][:], ALU.mult, [dp("Bb"), D(B, "EXc")], [D(B, "BT")])
                    tt(B["KTt"][:], KT[d][:, pt], B["EXc"][:], ALU.mult, [dKT, D(B, "EXc")], [D(B, "KTt")], eng=pool)
                    actf(B["EXd"][:], Gt, AF.Exp, [dp("G")], [D(B, "EXd")], scale=-1.0, bias=Gt[:, 127:128])
                    tt(B["BH"][:], Bb[:, pt], B["EXd"][:], ALU.mult, [dp("Bb"), D(B, "EXd")], [D(B, "BH")])
                    tt(B["KHh"][:], KT[d][:, pt], B["EXd"][:], ALU.mult, [dKT, D(B, "EXd")], [D(B, "KHh")], eng=pool)
                    cp(B["VTf"][:], Vv[:, nt], [dp("V")], [D(B, "VTf")], eng=pool)
                    for sn, dn_ in (("AT", "ATk"), ("VTf", "VTk"), ("BH", "BHk"), ("KHh", "KHk")):
                        ps, pd = P.psum()
                        mm(ps[:, 0:128], B[sn][:], IDb, True, True, [D(B, sn), dC], [pd])
                        cp(B[dn_][:], ps[:, 0:128], [pd], [D(B, dn_)], eng=act)

                def score(B, lh, ln, rh, rn, mask, dst, dn_):
                    ps, pd = P.psum()
                    mm(ps[:, 0:128], lh, rh, True, True, [D(B, ln), D(B, rn)], [pd])
                    tt(dst[:], ps[:, 0:128], mask, ALU.mult, [pd, dC], [dn_])

                def st_scores(tau, B, e):
                    hp = HP[e]
                    AT, BT, KTt, RT = B["AT"], B["BT"], B["KTt"], B["RT"]
                    score(B, BT[hp, :], "BT", AT[hp, :], "AT", LTf, B["NT0"][e], D(B, "NT", e))
                    score(B, AT[hp, :], "AT", BT[hp, :], "BT", GTf, B["WS"][e], D(B, "WS", e))
                    score(B, KTt[hp, :], "KTt", AT[hp, :], "AT", LTf, B["AAK"][e], D(B, "AAK", e))
                    score(B, BT[hp, :], "BT", RT[hp, :], "RT", LEf, B["ARB"][e], D(B, "ARB", e))
                    score(B, KTt[hp, :], "KTt", RT[hp, :], "RT", LEf, B["ARK"][e], D(B, "ARK", e))

                def st_x0(tau, B, e):
                    hp, oc = HP[e], HP[1 - e]
                    ps, pd = P.psum()
                    mm(ps[:, 0:64], B["AAK"][e][:], B["VTk"][:, hp], True, True, [D(B, "AAK", e), D(B, "VTk")], [pd])
                    cp(B["Xb"][e][:, oc], ps[:, 0:64], [pd], [D(B, "Xb", e)], eng=act)
                    cp(B["Xb"][e][:, hp], B["ATk"][:, hp], [D(B, "ATk")], [D(B, "Xb", e)], eng=pool)

                def st_lvl0(tau, B, e):
                    Tb, TTb = B["Tb"][e], B["TTb"][e]
                    tt(Tb[:], B["WS"][e][:], LMb[0], ALU.mult, [D(B, "WS", e), dC], [D(B, "T", e)])
                    tt(Tb[:], Tb[:], IDb, ALU.add, [D(B, "T", e), dC], [D(B, "T", e)])
                    tt(TTb[:], B["NT0"][e][:], LMTb[0], ALU.mult, [D(B, "NT", e), dC], [D(B, "TT", e)], eng=pool)
                    tt(TTb[:], TTb[:], IDb, ALU.add, [D(B, "TT", e), dC], [D(B, "TT", e)], eng=pool)

                def mk_lvl(k):
                    def st(tau, B, e):
                        Tb, TTb, WS = B["Tb"][e], B["TTb"][e], B["WS"][e]
                        dT, dTT, dWS = D(B, "T", e), D(B, "TT", e), D(B, "WS", e)
                        ps, pd = P.psum()
                        mm(ps[:, 0:128], B["NT0"][e][:], Tb[:], True, True, [D(B, "NT", e), dT], [pd])
                        tt(WS[:], ps[:, 0:128], LMb[k], ALU.mult, [pd, dC], [dWS])
                        pz, pzd = P.psum()
                        mm(pz[:, 0:128], IDb, Tb[:], True, False, [dC, dT], [pzd])
                        mm(pz[:, 0:128], TTb[:], WS[:], False, True, [dTT, dWS], [pzd])
                        pt_, ptd = P.psum()
                        mm(pt_[:, 0:128], WS[:], TTb[:], True, True, [dTT, dWS], [ptd])
                        cp(Tb[:], pz[:, 0:128], [pzd], [dT], eng=act)
                        tt(TTb[:], TTb[:], pt_[:, 0:128], ALU.add, [ptd, dTT], [dTT])
                    return st

                def st_apply(tau, B, e):
                    ps, pd = P.psum()
                    mm(ps[:, 0:128], B["TTb"][e][:], B["Xb"][e][:], True, True, [D(B, "TT", e), D(B, "Xb", e)], [pd])
                    cp(B["Xb"][e][:], ps[:, 0:128], [pd], [D(B, "Xb", e)], eng=act)

                def st_mt(tau, B, e):
                    hp = HP[e]
                    ps, pd = P.psum()
                    mm(ps[:, 0:64], B["Xb"][e][:], B["BHk"][:, hp], True, True, [D(B, "Xb", e), D(B, "BHk")], [pd])
                    stt(B["MT"][e][hp, hp], IDf[hp, hp], B["EXa"][hp, 127:128], ps[hp, 0:64], ALU.mult, ALU.add,
                        [pd, dC, D(B, "EXa")], [D(B, "MT", e)])

                def st_gc(tau, B, e):
                    hp, oc = HP[e], HP[1 - e]
                    ps, pd = P.psum()
                    mm(ps[:, 0:64], B["BHk"][:], B["Xb"][e][:, oc], True, False, [D(B, "Xb", e), D(B, "BHk")], [pd])
                    mm(ps[:, 0:64], B["KHk"][:], B["VTk"][:, hp], False, True, [D(B, "KHk"), D(B, "VTk")], [pd])
                    cp(B["GC"][hp, :], ps[hp, 0:64], [pd], [D(B, "GC", e)], eng=act)

                def st_pt(tau, B, e):
                    hp = HP[e]
                    ps, pd = P.psum()
                    mm(ps[:, 0:128], B["Xb"][e][:], IDb, True, True, [D(B, "Xb", e), dC], [pd])
                    cp(B["PT"][hp, :], ps[hp, 0:128], [pd], [D(B, "PT", e)], eng=act)

                def st_u(tau, B, e):
                    hp, oc = HP[e], HP[1 - e]
                    ps, pd = P.psum()
                    mm(ps[:, 0:64], B["PT"][hp, :], Sb[hp, hp], True, True, [D(B, "PT", e), dp("Sb%d" % e)], [pd])
                    tt(B["UT"][e][:, hp], ps[:, 0:64], B["Xb"][e][:, oc], ALU.add, [pd, D(B, "Xb", e)], [D(B, "UT", e)])

                def st_y(tau, B, e):
                    hp = HP[e]
                    nt = tk(tau * 128, 128, d)
                    ps, pd = P.psum()
                    mm(ps[:, 0:128], Sb[hp, :], B["RT"][hp, :], True, False, [dp("Sb%d" % e), D(B, "RT")], [pd])
                    mm(ps[:, 0:128], B["UT"][e][:], B["ARB"][e][:], False, False, [D(B, "UT", e), D(B, "ARB", e)], [pd])
                    mm(ps[:, 0:128], B["VTk"][:], B["ARK"][e][:], False, True, [D(B, "VTk"), D(B, "ARK", e)], [pd])
                    if d == 0:
                        cp(ACC[hp, j, nt], ps[hp, 0:128], [pd], [dACC], eng=act)
                    else:
                        tt(ACC[hp, j, nt], ACC[hp, j, nt], ps[hp, 0:128], ALU.add, [pd, dACC], [dACC])

                def st_chain(tau, B, e):
                    hp = HP[e]
                    ps, pd = P.psum()
                    mm(ps[:, 0:64], B["MT"][e][hp, :], Sf[hp, hp], True, True, [D(B, "MT", e), dp("Sf%d" % e)], [pd])
                    tt(Sf[hp, hp], ps[hp, 0:64], B["GC"][hp, :], ALU.add, [pd, D(B, "GC", e), dp("Sf%d" % e)], [dp("Sf%d" % e)])
                    cp(Sb[hp, hp], Sf[hp, hp], [dp("Sf%d" % e)], [dp("Sb%d" % e)], eng=pool)

                indep = [st_scores, st_x0, st_lvl0] + [mk_lvl(k) for k in range(1, 7)] + \
                        [st_apply, st_mt, st_gc, st_pt]
                ntile = rw_limit[2]
                for tau0 in range(0, ntile, NSET):
                    ctx = [(tau0 + i, sets[i]) for i in range(min(NSET, ntile - tau0))]
                    for tau, B in ctx:
                        prep(tau, B)
                    for stg in indep:
                        for tau, B in ctx:
                            for e in range(2):
                                stg(tau, B, e)
                    for tau, B in ctx:
                        for stg in (st_u, st_y, st_chain):
                            for e in range(2):
                                stg(tau, B, e)
            P.barrier()
            for tq in range(4):
                tsl = slice(tq * 512, (tq + 1) * 512)
                ps, pd = P.psum()
                mm(ps[:, :], BKf, ACC[:, j, tsl], True, True, [dC, dACC], [pd])
                stt(SQt, ps[:, :], -1.0 / 64, ACC[:, j, tsl], ALU.mult, ALU.add, [pd, dACC], [dp("P0")])
                actf(RSt, SQt, AF.Square, [dp("P0")], [dp("P0")])
                ps, pd = P.psum()
                mm(ps[:, :], BKf, RSt, True, True, [dC, dp("P0")], [pd])
                actf(RSt, ps[:, :], AF.Sqrt, [pd], [dp("P0")], bias=64e-5, scale=1.0 / 64)
                recip(RSt, RSt, [dp("P0")], [dp("P0")])
                tt(SQt, SQt, RSt, ALU.mult, [dp("P0")], [dp("P0")])
                ts(SQt, SQt, pv("rw_lnw", l, j), pv("rw_lnb", l, j), ALU.mult, ALU.add, [dp("P0"), dPV], [dp("P0")])
                tt(T1, KT[0][:, tsl], KT[1][:, tk(tq * 512, 512, 1)], ALU.add, [dp("KT0"), dp("KT1")], [dp("P0")])
                tt(T1, T1, Rr[:, tsl], ALU.mult, [dp("P0"), dp("R")], [dp("P0")])
                ts(T1, T1, MUV[:, 32 + j:33 + j], None, ALU.mult, None, [dp("P0"), dMUV], [dp("P0")])
                ps, pd = P.psum()
                mm(ps[:, :], BKf, T1, True, True, [dC, dp("P0")], [pd])
                tt(T1, ps[:, :], Vv[:, tsl], ALU.mult, [pd, dp("V")], [dp("P0")])
                tt(SQt, SQt, T1, ALU.add, [dp("P0")], [dp("P0")])
                tt(Y[:, j, tsl], SQt, GG[:, tsl], ALU.mult, [dp("P0"), dp("GG")], [dY])

    def mixer_stub(l):
        pass

    mix_fns = {0: globals().get("_mx_rwkv"), 1: mixer_mlstm, 2: mixer_lru, 3: globals().get("_mx_hg")}
    mix_fns[0] = locals().get("mixer_rwkv", mixer_stub)
    mix_fns[3] = locals().get("mixer_hgrn2", mixer_stub)

    def layer(l, sq):
        src = xT[sq] if l == 0 else hd[sq]
        load_layer_small(l)
        norm_mod(src, l, sq, 0)
        P.barrier()
        if l == 0 and sq == 0:
            tap("U", U[:, :, :], [dU], None)
        first = True
        for n in range(4):
            if n in mixers:
                mix_fns[n](l)
                P.barrier()
                if l == 0 and sq == 0:
                    tap("Y%d" % n, Y[:, :, :], [dY], None)
                gate_merge(l, n, first)
                first = False
                P.barrier()
        out_proj_residual(l, sq, src, hd[sq])
        P.barrier()
        norm_mod(hd[sq], l, sq, 1)
        P.barrier()
        ffn(l, sq, hd[sq])
        P.barrier()

    for sq in range(n_seq):
        for l in range(n_layers):
            layer(l, sq)
        norm_mod(hd[sq], -1, sq, 0)
        P.barrier()
    P.finish()
    es.close()
    return nc, P


def make_in_maps(inputs, n_cores=NCORE):
    inp = {k: np.asarray(v) for k, v in inputs.items()}
    pv = pack_pv(inp)
    cst = pack_consts()
    shared = {
        "pv": pv, "cst": cst,
        "ada_w": np.ascontiguousarray(inp["ada_w"], np.float32), "w_in": np.ascontiguousarray(inp["w_in"], np.float32),
        "rw_w2": np.ascontiguousarray(inp["rw_w2"], np.float32), "rw_a2": np.ascontiguousarray(inp["rw_a2"], np.float32),
        "rw_g2": np.ascontiguousarray(inp["rw_g2"], np.float32),
        "lru_wa": pack_lru_bd(inp["lru_wa"]), "lru_wx": pack_lru_bd(inp["lru_wx"]),
        "w_branch": np.ascontiguousarray(inp["w_branch"], np.float32), "w_out": np.ascontiguousarray(inp["w_out"], np.float32),
        "ffn_up": np.ascontiguousarray(inp["ffn_up"], np.float32), "ffn_down": np.ascontiguousarray(inp["ffn_down"], np.float32),
    }
    maps = []
    for i in range(n_cores):
        xs = inp["x"][i * NSEQ:(i + 1) * NSEQ]
        m = dict(shared)
        m["xT"] = np.ascontiguousarray(np.transpose(xs, (0, 2, 1)), np.float32)
        cs = inp["c"][i * NSEQ:(i + 1) * NSEQ].astype(np.float32)
        m["cT"] = np.ascontiguousarray(cs.T.reshape(8, 128, NSEQ).transpose(1, 0, 2))
        maps.append(m)
    return maps


def kernel(**inputs):
    nc, _ = build()
    maps = make_in_maps(inputs)
    res = run_bass_kernel_spmd(nc, maps, core_ids=list(range(NCORE)))
    outs = [np.transpose(r["outT"], (0, 2, 1)) for r in res.results]
    return np.ascontiguousarray(np.concatenate(outs, axis=0), dtype=np.float32)
```

```python
import numpy as np
import concourse.bass as bass
import concourse.mybir as mybir
from concourse.bass_utils import run_bass_kernel_spmd
from contextlib import ExitStack

F32 = mybir.dt.float32
BF16 = mybir.dt.bfloat16
ALU = mybir.AluOpType
AF = mybir.ActivationFunctionType

D = 1024
T = 2048
L = 4
NSEQ = 2
NCORE = 8
W = 512
N_IN = 11024
O_RW, O_ML, O_LRU, O_HG, O_GATE = 0, 1792, 3344, 4368, 6928
DFF = 2816
class Track:
    def __init__(self, sem, step):
        self.sem = sem
        self.val = 0
        self.step = step


class Dep:
    __slots__ = ("w", "r")

    def __init__(self):
        self.w = None
        self.r = {}


class Eng:
    def __init__(self, name, h, tr, is_pe=False):
        self.name = name
        self.h = h
        self.tr = tr
        self.is_pe = is_pe
        self.seen = {}
        self.ops = []
        self.pool = []
        self.dma_i = 0


class Prog:
    def __init__(self, nc, es):
        self.nc = nc
        self.es = es
        self.tracks = []

        def mk(name, step=1):
            t = Track(es.enter_context(nc.semaphore(name)), step)
            self.tracks.append(t)
            return t

        self.pe = Eng("pe", nc.tensor, mk("s_pe"), True)
        self.act = Eng("act", nc.scalar, mk("s_act"))
        self.dve = Eng("dve", nc.vector, mk("s_dve"))
        self.pool = Eng("pool", nc.gpsimd, mk("s_pool"))
        self.sp = Eng("sp", nc.sync, mk("s_sp"))
        self.engs = [self.pe, self.act, self.dve, self.pool, self.sp]
        for e, n in ((self.sp, 16), (self.pool, 16), (self.act, 4)):
            e.pool = [mk("d_%s%d" % (e.name, i), 16) for i in range(n)]
        self.nps = 0
        self.psums = []
        self.n_ops = 0

    def sb(self, name, shape, dt=F32):
        return self.es.enter_context(self.nc.sbuf_tensor(name, list(shape), dt))

    def make_psums(self):
        for i in range(8):
            t = self.es.enter_context(self.nc.psum_tensor("ps%d" % i, [128, 512], F32))
            self.psums.append((t, Dep()))

    def psum(self):
        t = self.psums[self.nps % 8]
        self.nps += 1
        return t

    def _waits(self, eng, reads, writes, extra=()):
        need = {}

        def add(ev):
            if ev is None:
                return
            tr, v = ev
            if need.get(tr, 0) < v:
                need[tr] = v

        for d in reads:
            add(d.w)
        for d in writes:
            add(d.w)
            for tr, v in d.r.items():
                add((tr, v))
        for ev in extra:
            add(ev)
        for tr, v in need.items():
            if tr is eng.tr and eng.is_pe:
                continue
            if eng.seen.get(tr, 0) >= v:
                continue
            eng.seen[tr] = v
            eng.ops.append(("wait", tr.sem, v))

    def op(self, eng, fn, reads=(), writes=(), inc=True):
        self._waits(eng, reads, writes)
        val = eng.tr.val + 1
        if inc:
            eng.tr.val = val
        eng.ops.append(("op", fn, inc))
        self.n_ops += 1
        for d in reads:
            if d.r.get(eng.tr, 0) < val:
                d.r[eng.tr] = val
        for d in writes:
            d.w = (eng.tr, val)
            d.r = {}

    def dma(self, eng, out, in_, reads=(), writes=(), **kw):
        tr = eng.pool[eng.dma_i % len(eng.pool)]
        eng.dma_i += 1
        extra = [(tr, tr.val)] if tr.val > 0 else []
        self._waits(eng, reads, writes, extra)
        tr.val += 16
        eng.ops.append(("dma", out, in_, tr.sem, kw))
        self.n_ops += 1
        for d in reads:
            d.r[tr] = tr.val
        for d in writes:
            d.w = (tr, tr.val)
            d.r = {}

    def barrier(self):
        for e in self.engs:
            for tr in self.tracks:
                if tr.val > e.seen.get(tr, 0):
                    if tr is e.tr and e.is_pe:
                        continue
                    e.seen[tr] = tr.val
                    e.ops.append(("wait", tr.sem, tr.val))

    def finish(self):
        self.barrier()
        nc = self.nc

        def replay(e, h):
            for o in e.ops:
                if o[0] == "wait":
                    h.wait_ge(o[1], o[2])
                elif o[0] == "op":
                    ins = o[1](h)
                    if o[2]:
                        ins.then_inc(e.tr.sem, 1)
                else:
                    h.dma_start(out=o[1], in_=o[2], **o[4]).then_inc(o[3], 16)

        with nc.Block() as block:
            @block.tensor
            def _(h):
                replay(self.pe, h)

            @block.scalar
            def _(h):
                replay(self.act, h)

            @block.vector
            def _(h):
                replay(self.dve, h)

            @block.gpsimd
            def _(h):
                replay(self.pool, h)

            @block.sync
            def _(h):
                replay(self.sp, h)

PV_SPEC = [("ada_b", 48), ("n1g", 8), ("n2g", 8), ("rw_mu", 14), ("rw_w0", 8), ("rw_a0", 8), ("rw_kk", 4),
           ("rw_ka", 4), ("rw_rk", 4), ("rw_lnw", 4), ("rw_lnb", 4), ("ml_norm", 4), ("lru_cw", 16),
           ("lru_cb", 4), ("lru_ba", 8), ("lru_bx", 8), ("lru_lam", 8), ("hg_lb", 8), ("hg_norm", 4),
           ("ffn_cw", 132), ("ffn_cb", 44), ("ml_gb", 16)]
PV_L = sum(n for _, n in PV_SPEC)
PV_OFF = {}
_o = 0
for _n, _c in PV_SPEC:
    PV_OFF[_n] = _o
    _o += _c
NPV = PV_L * L + 8


def _fm(v):
    return np.ascontiguousarray(np.asarray(v, np.float32).reshape(-1, 128).T)


def pack_pv(inp):
    pv = np.zeros((128, NPV), np.float32)
    for l in range(L):
        ent = {
            "ada_b": _fm(inp["ada_b"][l]), "n1g": _fm(inp["norm1_g"][l]), "n2g": _fm(inp["norm2_g"][l]),
            "rw_mu": _fm(inp["rw_mu"][l]), "rw_w0": _fm(inp["rw_w0"][l].reshape(-1)),
            "rw_a0": _fm(inp["rw_a0"][l].reshape(-1)), "rw_kk": _fm(inp["rw_kk"][l]),
            "rw_ka": _fm(inp["rw_ka"][l]), "rw_rk": _fm(inp["rw_rk"][l].reshape(-1)),
            "rw_lnw": _fm(inp["rw_lnw"][l]), "rw_lnb": _fm(inp["rw_lnb"][l]), "ml_norm": _fm(inp["ml_norm"][l]),
            "lru_cw": _fm(inp["lru_conv_w"][l].reshape(-1)), "lru_cb": _fm(inp["lru_conv_b"][l]),
            "lru_ba": _fm(inp["lru_ba"][l].reshape(-1)), "lru_bx": _fm(inp["lru_bx"][l].reshape(-1)),
            "lru_lam": _fm(inp["lru_lam"][l].reshape(-1)), "hg_lb": _fm(inp["hg_lb"][l].reshape(-1)),
            "hg_norm": _fm(inp["hg_norm"][l]), "ffn_cw": _fm(inp["ffn_conv_w"][l].reshape(-1)),
            "ffn_cb": _fm(inp["ffn_conv_b"][l]),
            "ml_gb": np.tile(np.concatenate([inp["ml_ibias"][l].reshape(-1), inp["ml_fbias"][l].reshape(-1)])[None, :],
                             (128, 1)).astype(np.float32),
        }
        for n, c in PV_SPEC:
            assert ent[n].shape == (128, c), (n, ent[n].shape)
            pv[:, l * PV_L + PV_OFF[n]: l * PV_L + PV_OFF[n] + c] = ent[n]
    pv[:, PV_L * L:] = _fm(inp["final_g"])
    return pv


NCST = 20


def pack_consts():
    c = np.zeros((128, NCST, 128), np.float32)
    i = np.arange(128)
    c[:, 0, :] = np.eye(128)
    c[:, 1, :] = 1.0
    c[:, 2, :] = (i[:, None] <= i[None, :])
    c[:, 3, :] = (i[:, None] < i[None, :])
    c[:, 4, :] = (i[:, None] // 64 == i[None, :] // 64)
    c[:, 5, :] = (i[:, None] > i[None, :])
    for k in range(7):
        bsz = 1 << k
        t, s_ = i[:, None], i[None, :]
        m = (t // (2 * bsz) == s_ // (2 * bsz)) & (t % (2 * bsz) >= bsz) & (s_ % (2 * bsz) < bsz)
        c[:, 6 + k, :] = m
        c[:, 13 + k, :] = m.T
    return c


def pack_lru_bd(w):
    out = np.zeros((L, 2, 4, 128, 128), np.float32)
    for j in range(4):
        out[:, :, j, 0:64, 0:64] = w[:, :, 2 * j]
        out[:, :, j, 64:128, 64:128] = w[:, :, 2 * j + 1]
    return out


def build(n_layers=L, n_seq=NSEQ, mixers=(0, 1, 2, 3), taps=(), rw_limit=(4, 2, 16)):
    nc = bass.Bass("TRN2", target_bir_lowering=False)
    dt = lambda name, shape, kind="ExternalInput": nc.dram_tensor(name, list(shape), F32, kind=kind).ap()
    xT = dt("xT", [NSEQ, D, T])
    cT = dt("cT", [128, 8, NSEQ])
    pvd = dt("pv", [128, NPV])
    cst = dt("cst", [128, NCST, 128])
    ada_w = dt("ada_w", [L, D, 6 * D])
    w_in = dt("w_in", [L, D, N_IN])
    rw_w2 = dt("rw_w2", [L, 2, 64, W])
    rw_a2 = dt("rw_a2", [L, 2, 64, W])
    rw_g2 = dt("rw_g2", [L, 128, W])
    lru_wa = dt("lru_wa", [L, 2, 4, 128, 128])
    lru_wx = dt("lru_wx", [L, 2, 4, 128, 128])
    w_branch = dt("w_branch", [L, 4, W, D])
    w_out = dt("w_out", [L, D, D])
    ffn_up = dt("ffn_up", [L, D, 2 * DFF])
    ffn_down = dt("ffn_down", [L, DFF, D])
    outT = dt("outT", [NSEQ, D, T], "ExternalOutput")
    hd = dt("hd", [NSEQ, D, T], "Internal")
    tapd = {}
    for name, shape in taps:
        tapd[name] = dt("tap_" + name, shape, "ExternalOutput")

    es = ExitStack()
    P = Prog(nc, es)
    P.make_psums()
    pe, act, dve, pool, sp = P.pe, P.act, P.dve, P.pool, P.sp

    def mm(out, lhsT, rhs, start, stop, r, w, inc=None):
        P.op(pe, lambda h: h.matmul(out, lhsT, rhs, start=start, stop=stop), r, w, inc=stop if inc is None else inc)

    def actf(out, in_, func, r, w, bias=None, scale=None):
        kw = {}
        if bias is not None:
            kw["bias"] = bias
        if scale is not None:
            kw["scale"] = scale
        P.op(act, lambda h: h.activation(out, in_, func, **kw), r, w)

    def tt(out, a, b, op, r, w, eng=None):
        P.op(eng or dve, lambda h: h.tensor_tensor(out, a, b, op), r, w)

    def ts(out, a, s1, s2, op0, op1, r, w, eng=None):
        if op1 is None:
            P.op(eng or dve, lambda h: h.tensor_scalar(out, a, s1, None, op0), r, w)
        else:
            P.op(eng or dve, lambda h: h.tensor_scalar(out, a, s1, s2, op0, op1), r, w)

    def stt(out, a, s, b, op0, op1, r, w, eng=None):
        P.op(eng or dve, lambda h: h.scalar_tensor_tensor(out, a, s, b, op0, op1), r, w)

    def cp(out, in_, r, w, eng=None):
        e = eng or dve
        if e is act:
            P.op(act, lambda h: h.copy(out, in_), r, w)
        else:
            P.op(e, lambda h: h.tensor_copy(out, in_), r, w)

    def memset(ap, v, w, eng=None):
        P.op(eng or dve, lambda h: h.memset(ap, v), (), w)

    def recip(out, in_, r, w):
        P.op(dve, lambda h: h.reciprocal(out, in_), r, w)

    def scan(out, d0, d1, init, r, w):
        P.op(dve, lambda h: h.tensor_tensor_scan(out, d0, d1, init, ALU.mult, ALU.add), r, w)

    def tap(name, src_ap, r, dst=None):
        if name in tapd:
            P.dma(pool, tapd[name] if dst is None else dst, src_ap, reads=r)

    U = P.sb("U", [128, 8, T], BF16)
    dU = Dep()
    RR = P.sb("RR", [128, 45056], BF16)
    ACC = RR[:, 0:16384].bitcast(F32).rearrange("p (c t) -> p c t", c=4)
    Y = RR[:, 16384:24576].rearrange("p (c t) -> p c t", c=4)
    M = RR[:, 24576:40960].rearrange("p (c t) -> p c t", c=8)
    AFF = RR[:, 0:45056].rearrange("p (c t) -> p c t", c=22)
    dACC, dY, dM, dAFF = Dep(), Dep(), Dep(), Dep()
    SCB = 40960
    SC = P.sb("SC", [128, SCB // 2], BF16)

    def scv(off, n, dtype):
        assert off % 4 == 0
        if dtype is F32:
            assert off + 4 * n <= SCB, (off, n)
            return SC[:, off // 2: off // 2 + 2 * n].bitcast(F32)
        assert off + 2 * n <= SCB, (off, n)
        return SC[:, off // 2: off // 2 + n]

    NSLAB = 3
    slabs = [(P.sb("slab%d" % i, [128, 8, 512], BF16), Dep()) for i in range(NSLAB)]
    slab_i = [0]
    PV = P.sb("PV", [128, NPV], F32)
    dPV = Dep()
    CF = P.sb("CF", [128, 6, 128], F32)
    CB = P.sb("CB", [128, NCST, 128], BF16)
    dC = Dep()
    MOD = P.sb("MOD", [128, L * 48 * NSEQ], F32)
    dMOD = Dep()
    SV = P.sb("SV", [128, L * 32], F32)
    dSV = Dep()
    CND = P.sb("CND", [128, 8, NSEQ], F32)
    CNDB = P.sb("CNDB", [128, 8, NSEQ], BF16)
    dCND = Dep()
    LW = P.sb("LW", [128, 3584], BF16)
    dLW = Dep()
    W2s = LW[0:64, 0:1024].rearrange("p (d n) -> p d n", d=2)
    A2s = LW[64:128, 0:1024].rearrange("p (d n) -> p d n", d=2)
    G2s = LW[:, 1024:1536]
    WAs = LW[:, 1536:2560].rearrange("p (d j n) -> p d j n", d=2, j=4)
    WXs = LW[:, 2560:3584].rearrange("p (d j n) -> p d j n", d=2, j=4)

    IDf, ONf, LEf, LTf, BKf = (CF[:, i, :] for i in range(5))
    IDb, ONb, LEb, LTb, BKb = (CB[:, i, :] for i in range(5))
    GTf = CF[:, 5, :]
    LMb = [CB[:, 6 + k, :] for k in range(7)]
    LMTb = [CB[:, 13 + k, :] for k in range(7)]

    def pv(name, l, c0=0, n=1):
        o = l * PV_L + PV_OFF[name] + c0
        return PV[:, o:o + n]

    def mod(l, k, c, sq):
        o = ((l * 6 + k) * 8 + c) * NSEQ + sq
        return MOD[:, o:o + 1]

    def load_slab(W2d, k0, nkc, col0, ncols, dst_col=0, new=True):
        if new:
            slab_i[0] += 1
        sl, sd = slabs[slab_i[0] % NSLAB]
        src = W2d[k0 * 128:(k0 + nkc) * 128, col0:col0 + ncols].rearrange("(kc p) n -> p kc n", p=128)
        P.dma(pool, sl[:, 0:nkc, dst_col:dst_col + ncols], src, writes=[sd])
        return sl, sd

    def linear_fm(W2d, nkc, col0, ncols, xfn, xdeps, ntok, evac):
        for cg in range(0, ncols, 512):
            n = min(512, ncols - cg)
            sl, sd = load_slab(W2d, 0, nkc, col0 + cg, n)
            for c in range(0, n, 128):
                cw = min(128, n - c)
                for t0 in range(0, ntok, 512):
                    tn = min(512, ntok - t0)
                    ps, pd = P.psum()
                    for kc in range(nkc):
                        mm(ps[0:cw, 0:tn], sl[:, kc, c:c + cw], xfn(kc, t0, tn), kc == 0, kc == nkc - 1,
                           [sd] + xdeps, [pd])
                    evac(ps, pd, cg + c, cw, t0, tn)

    ufn = lambda kc, t0, tn: U[:, kc, t0:t0 + tn]

    P.dma(sp, PV[:], pvd[:, :], writes=[dPV])
    P.dma(sp, CF[:], cst[:, 0:6, :], writes=[dC])
    P.dma(pool, CB[:], cst[:, :, :], writes=[dC])
    P.dma(sp, CND[:], cT[:, :, :], writes=[dCND])
    actf(CND[:], CND[:], AF.Silu, [dCND], [dCND])
    cp(CNDB[:], CND[:], [dCND], [dCND])
    for l in range(n_layers):
        def ev_mod(ps, pd, col, cw, t0, tn, l=l):
            cidx = col // 128
            o = (l * 48 + cidx) * NSEQ
            ts(MOD[:, o:o + NSEQ], ps[:, 0:NSEQ], pv("ada_b", l, cidx), None, ALU.add, None, [pd, dPV], [dMOD])
        linear_fm(ada_w[l], 8, 0, 6 * D, lambda kc, t0, tn: CNDB[:, kc, :], [dCND], NSEQ, ev_mod)
    def sv(l, o, n=1):
        return SV[:, l * 32 + o: l * 32 + o + n]
    TS = scv(0, 64, F32)
    dTS = Dep()
    for l in range(n_layers):
        lam = pv("lru_lam", l, 0, 8)
        actf(TS[:, 0:8], lam, AF.Abs, [dPV], [dTS])
        actf(TS[:, 0:8], TS[:, 0:8], AF.Exp, [dTS], [dTS], scale=-1.0)
        actf(TS[:, 0:8], TS[:, 0:8], AF.Ln, [dTS], [dTS], bias=1.0)
        ts(TS[:, 8:16], lam, -1.0, 0.0, ALU.mult, ALU.max, [dPV], [dTS])
        tt(TS[:, 0:8], TS[:, 0:8], TS[:, 8:16], ALU.add, [dTS], [dTS])
        ts(sv(l, 0, 8), TS[:, 0:8], -8.0, None, ALU.mult, None, [dTS], [dSV])
        ts(sv(l, 8, 8), TS[:, 0:8], -16.0, None, ALU.mult, None, [dTS], [dSV])
    EX = scv(256, 32, F32)
    SM = scv(384, 8, F32)
    dEX = Dep()
    for l in range(L):
        actf(EX[:, l * 8:(l + 1) * 8], pv("hg_lb", l, 0, 8), AF.Exp, [dPV], [dEX])
    tt(SM[:], EX[:, 0:8], EX[:, 8:16], ALU.add, [dEX], [dEX])
    tt(SM[:], SM[:], EX[:, 16:24], ALU.add, [dEX], [dEX])
    tt(SM[:], SM[:], EX[:, 24:32], ALU.add, [dEX], [dEX])
    recip(SM[:], SM[:], [dEX], [dEX])
    for l in range(L):
        tt(EX[:, l * 8:(l + 1) * 8], EX[:, l * 8:(l + 1) * 8], SM[:], ALU.mult, [dEX], [dEX])
    for l in range(n_layers):
        if l == 0:
            memset(sv(0, 16, 8), 0.0, [dSV])
        elif l == 1:
            cp(sv(1, 16, 8), EX[:, 8:16], [dEX], [dSV])
        else:
            tt(sv(l, 16, 8), sv(l - 1, 16, 8), EX[:, l * 8:(l + 1) * 8], ALU.add, [dEX, dSV], [dSV])
    for l in range(n_layers):
        ts(sv(l, 16, 8), sv(l, 16, 8), 0.0, 1.0, ALU.max, ALU.min, [dSV], [dSV])
        ts(sv(l, 24, 8), sv(l, 16, 8), -1.0, 1.0, ALU.mult, ALU.add, [dSV], [dSV])
    P.barrier()

    GS = P.sb("GS", [128, 64], F32)
    dGS = Dep()

    def norm_mod(src, l, sq, which):
        gname, ksh, ksc = ("n1g", 0, 1) if which == 0 else ("n2g", 3, 4)
        if l >= 0:
            for c in range(8):
                stt(GS[:, which * 8 + c: which * 8 + c + 1], mod(l, ksc, c, sq), 1.0, pv(gname, l, c),
                    ALU.add, ALU.mult, [dMOD, dPV], [dGS])
        HT = scv(0, 4096, F32).rearrange("p (c t) -> p c t", c=8)
        SQ = [scv(16384 + i * 2048, 512, F32) for i in range(2)]
        RS = scv(20480, 512, F32)
        TM = [scv(22528 + i * 2048, 512, F32) for i in range(2)]
        dHT, dSQ, dRS, dTM = Dep(), [Dep(), Dep()], Dep(), [Dep(), Dep()]
        srcv = src.rearrange("(c p) t -> p c t", p=128)
        for tq in range(4):
            P.dma(sp, HT[:, :, :], srcv[:, :, tq * 512:(tq + 1) * 512], writes=[dHT])
            ps, pd = P.psum()
            for c in range(8):
                actf(SQ[c % 2][:], HT[:, c, :], AF.Square, [dHT], [dSQ[c % 2]])
                mm(ps[:, :], ONf, SQ[c % 2][:], c == 0, c == 7, [dSQ[c % 2], dC], [pd], inc=True)
            actf(RS[:], ps[:, :], AF.Sqrt, [pd], [dRS], bias=1e-6, scale=1.0 / D)
            recip(RS[:], RS[:], [dRS], [dRS])
            for c in range(8):
                tt(TM[c % 2][:], HT[:, c, :], RS[:], ALU.mult, [dHT, dRS], [dTM[c % 2]])
                if l >= 0:
                    actf(U[:, c, tq * 512:(tq + 1) * 512], TM[c % 2][:], AF.Identity, [dTM[c % 2], dGS, dMOD], [dU],
                         bias=mod(l, ksh, c, sq), scale=GS[:, which * 8 + c: which * 8 + c + 1])
                else:
                    ts(HT[:, c, :], TM[c % 2][:], PV[:, PV_L * L + c: PV_L * L + c + 1], None, ALU.mult, None,
                       [dTM[c % 2], dPV], [dHT])
            if l < 0:
                P.dma(sp, outT[sq].rearrange("(c p) t -> p c t", p=128)[:, :, tq * 512:(tq + 1) * 512], HT[:, :, :],
                      reads=[dHT])

    def gate_merge(l, n, first):
        SG = [scv(i * 2048, 512, F32) for i in range(2)]
        TP = [scv(4096 + i * 2048, 512, F32) for i in range(2)]
        dSG, dTP = [Dep(), Dep()], [Dep(), Dep()]
        k = 0
        for cg in range(2):
            sg_, sgd = load_slab(w_in[l], 0, 8, O_GATE + n * D + cg * 512, 512)
            sb_, sbd = load_slab(w_branch[l, n], 0, 4, cg * 512, 512)
            for c in range(4):
                cc = cg * 4 + c
                for tq in range(4):
                    tsl = slice(tq * 512, (tq + 1) * 512)
                    pg, pgd = P.psum()
                    for kc in range(8):
                        mm(pg[:, :], sg_[:, kc, c * 128:(c + 1) * 128], U[:, kc, tsl], kc == 0, kc == 7, [sgd, dU], [pgd])
                    py, pyd = P.psum()
                    for kc in range(4):
                        mm(py[:, :], sb_[:, kc, c * 128:(c + 1) * 128], Y[:, kc, tsl], kc == 0, kc == 3, [sbd, dY], [pyd])
                    i = k % 2
                    k += 1
                    actf(SG[i][:], pg[:, :], AF.Sigmoid, [pgd], [dSG[i]])
                    if first:
                        tt(M[:, cc, tsl], py[:, :], SG[i][:], ALU.mult, [pyd, dSG[i]], [dM])
                    else:
                        tt(TP[i][:], py[:, :], SG[i][:], ALU.mult, [pyd, dSG[i]], [dTP[i]])
                        tt(M[:, cc, tsl], M[:, cc, tsl], TP[i][:], ALU.add, [dTP[i], dM], [dM])

    def out_proj_residual(l, sq, src, dst):
        HT = scv(0, 4096, F32).rearrange("p (c t) -> p c t", c=8)
        dHT = Dep()
        s0, s0d = load_slab(w_out[l], 0, 8, 0, 512)
        s1, s1d = load_slab(w_out[l], 0, 8, 512, 512)
        srcv = src.rearrange("(c p) t -> p c t", p=128)
        dstv = dst.rearrange("(c p) t -> p c t", p=128)
        for tq in range(4):
            tsl = slice(tq * 512, (tq + 1) * 512)
            P.dma(sp, HT[:, :, :], srcv[:, :, tsl], writes=[dHT])
            for c2 in range(8):
                sl, sd = (s0, s0d) if c2 < 4 else (s1, s1d)
                ps, pd = P.psum()
                for kc in range(8):
                    mm(ps[:, :], sl[:, kc, (c2 % 4) * 128:(c2 % 4 + 1) * 128], M[:, kc, tsl], kc == 0, kc == 7, [sd, dM], [pd])
                stt(HT[:, c2, :], ps[:, :], mod(l, 2, c2, sq), HT[:, c2, :], ALU.mult, ALU.add, [pd, dHT, dMOD], [dHT])
            P.dma(sp, dstv[:, :, tsl], HT[:, :, :], reads=[dHT])

    def ffn(l, sq, hsrc):
        ZV = scv(0, 2050, F32)
        ZG = scv(8208, 2050, F32)
        CV = scv(16416, 2048, F32)
        CG = scv(24608, 2048, F32)
        dZV, dZG, dCV, dCG = Dep(), Dep(), Dep(), Dep()
        memset(ZV[:, 0:1], 0.0, [dZV])
        memset(ZV[:, 2049:2050], 0.0, [dZV])
        memset(ZG[:, 0:1], 0.0, [dZG])
        memset(ZG[:, 2049:2050], 0.0, [dZG])
        up = ffn_up[l]
        for j in range(22):
            sl, sd = load_slab(up, 0, 8, j * 128, 128, 0)
            load_slab(up, 0, 8, DFF + j * 128, 128, 128, new=False)
            for half, Z, dZ in ((0, ZV, dZV), (1, ZG, dZG)):
                for tq in range(4):
                    ps, pd = P.psum()
                    for kc in range(8):
                        mm(ps[:, :], sl[:, kc, half * 128:(half + 1) * 128], U[:, kc, tq * 512:(tq + 1) * 512],
                           kc == 0, kc == 7, [sd, dU], [pd])
                    cp(Z[:, 1 + tq * 512: 1 + (tq + 1) * 512], ps[:, :], [pd], [dZ], eng=act)
            for half, Z, dZ, C, dCx in ((0, ZV, dZV, CV, dCV), (1, ZG, dZG, CG, dCG)):
                jj = half * 22 + j
                cw = lambda tp: pv("ffn_cw", l, tp * 44 + jj)
                ts(C[:], Z[:, 1:2049], cw(1), pv("ffn_cb", l, jj), ALU.mult, ALU.add, [dZ, dPV], [dCx])
                stt(C[:], Z[:, 0:2048], cw(0), C[:], ALU.mult, ALU.add, [dZ, dCx], [dCx])
                stt(C[:], Z[:, 2:2050], cw(2), C[:], ALU.mult, ALU.add, [dZ, dCx], [dCx])
            actf(CG[:], CG[:], AF.Silu, [dCG], [dCG])
            tt(AFF[:, j, :], CV[:], CG[:], ALU.mult, [dCV, dCG], [dAFF])
        P.barrier()
        HT = scv(0, 2048, F32).rearrange("p (c t) -> p c t", c=4)
        dHT = Dep()
        hv = hsrc.rearrange("(c p) t -> p c t", p=128)
        for cg in range(2):
            sls = []
            for pi, (k0, nk) in enumerate(((0, 8), (8, 8), (16, 6))):
                sls.append(load_slab(ffn_down[l], k0, nk, cg * 512, 512) + (k0, nk))
            for tq in range(4):
                tsl = slice(tq * 512, (tq + 1) * 512)
                P.dma(sp, HT[:, :, :], hv[:, cg * 4:(cg + 1) * 4, tsl], writes=[dHT])
                for c2 in range(4):
                    ps, pd = P.psum()
                    for sl, sd, k0, nk in sls:
                        for kc in range(nk):
                            kk = k0 + kc
                            mm(ps[:, :], sl[:, kc, c2 * 128:(c2 + 1) * 128], AFF[:, kk, tsl], kk == 0, kk == 21,
                               [sd, dAFF], [pd])
                    stt(HT[:, c2, :], ps[:, :], mod(l, 5, cg * 4 + c2, sq), HT[:, c2, :], ALU.mult, ALU.add,
                        [pd, dHT, dMOD], [dHT])
                P.dma(sp, hv[:, cg * 4:(cg + 1) * 4, tsl], HT[:, :, :], reads=[dHT])

    def tk(t0, n, d):
        if d == 0:
            return slice(t0, t0 + n)
        a = T - 1 - t0
        b = a - n
        return slice(a, None if b < 0 else b, -1)

    def load_layer_small(l):
        P.dma(pool, W2s, rw_w2[l].rearrange("d p n -> p d n"), writes=[dLW])
        P.dma(pool, A2s, rw_a2[l].rearrange("d p n -> p d n"), writes=[dLW])
        P.dma(pool, G2s, rw_g2[l], writes=[dLW])
        P.dma(pool, WAs, lru_wa[l].rearrange("d j p n -> p d j n"), writes=[dLW])
        P.dma(pool, WXs, lru_wx[l].rearrange("d j p n -> p d j n"), writes=[dLW])

    def mixer_lru(l):
        XP = scv(0, 2051, F32)
        XC = scv(8208, 2048, F32)
        XCB = scv(16400, 2048, BF16)
        Ba = scv(20496, 2048, F32)
        Bm = scv(28688, 2048, F32)
        Bx, Bh, Bg, By = (ACC[:, i, :] for i in range(4))
        dXP, dXC, dXCB, dBa, dBm, dBx, dBh, dBg, dBy = (Dep() for _ in range(9))
        memset(XP[:, 0:1], 0.0, [dXP])
        memset(XP[:, 2049:2051], 0.0, [dXP])
        for j in range(4):
            linear_fm(w_in[l], 8, O_LRU + j * 128, 128, ufn, [dU], T,
                      lambda ps, pd, col, cw, t0, tn: cp(XP[:, 1 + t0:1 + t0 + tn], ps[:, 0:tn], [pd], [dXP], eng=act))
            linear_fm(w_in[l], 8, O_LRU + 512 + j * 128, 128, ufn, [dU], T,
                      lambda ps, pd, col, cw, t0, tn: actf(Bg[:, t0:t0 + tn], ps[:, 0:tn], AF.Gelu, [pd], [dBg]))
            cwv = lambda tp: pv("lru_cw", l, tp * 4 + j)
            ts(XC[:], XP[:, 0:T], cwv(0), pv("lru_cb", l, j), ALU.mult, ALU.add, [dXP, dPV], [dXC])
            for tp in range(1, 4):
                stt(XC[:], XP[:, tp:tp + T], cwv(tp), XC[:], ALU.mult, ALU.add, [dXP, dXC], [dXC])
            cp(XCB[:], XC[:], [dXC], [dXCB])
            for d in range(2):
                for tq in range(4):
                    tsl = slice(tq * 512, (tq + 1) * 512)
                    ps, pd = P.psum()
                    mm(ps[:, :], WAs[:, d, j, :], XCB[:, tsl], True, True, [dLW, dXCB], [pd])
                    actf(Ba[:, tsl], ps[:, :], AF.Sigmoid, [pd, dPV], [dBa], bias=pv("lru_ba", l, d * 4 + j))
                    ps, pd = P.psum()
                    mm(ps[:, :], WXs[:, d, j, :], XCB[:, tsl], True, True, [dLW, dXCB], [pd])
                    actf(Bx[:, tsl], ps[:, :], AF.Sigmoid, [pd, dPV], [dBx], bias=pv("lru_bx", l, d * 4 + j))
                actf(Bm[:], Ba[:], AF.Exp, [dBa, dSV], [dBm], scale=sv(l, 8 + d * 4 + j))
                actf(Ba[:], Ba[:], AF.Exp, [dBa, dSV], [dBa], scale=sv(l, d * 4 + j))
                ts(Bm[:], Bm[:], 1.0, -1.0, ALU.min, ALU.mult, [dBm], [dBm])
                actf(Bm[:], Bm[:], AF.Sqrt, [dBm], [dBm], bias=1.0)
                tt(Bx[:], Bx[:], XC[:], ALU.mult, [dBx, dXC], [dBx])
                tt(Bm[:], Bm[:], Bx[:], ALU.mult, [dBm, dBx], [dBm])
                if d == 0:
                    scan(By[:], Ba[:], Bm[:], 0.0, [dBa, dBm], [dBy])
                else:
                    scan(Bh[:, ::-1], Ba[:, ::-1], Bm[:, ::-1], 0.0, [dBa, dBm], [dBh])
                    tt(By[:], By[:], Bh[:], ALU.add, [dBy, dBh], [dBy])
            tt(Y[:, j, :], By[:], Bg[:], ALU.mult, [dBy, dBg], [dY])

    PADB = RR[:, 40960:45056]

    HN_DEPS = (Dep(), Dep())

    def head_rmsnorm_gate(l, gname, h, gate_mul):
        SQ = PADB[:, 0:1024].bitcast(F32)
        RS = PADB[:, 1024:2048].bitcast(F32)
        dSQ, dRS = HN_DEPS
        for tq in range(4):
            tsl = slice(tq * 512, (tq + 1) * 512)
            actf(SQ[:], ACC[:, h, tsl], AF.Square, [dACC], [dSQ])
            ps, pd = P.psum()
            mm(ps[:, :], ONf, SQ[:], True, True, [dSQ, dC], [pd])
            actf(RS[:], ps[:, :], AF.Sqrt, [pd], [dRS], bias=1e-6, scale=1.0 / 128)
            recip(RS[:], RS[:], [dRS], [dRS])
            tt(SQ[:], ACC[:, h, tsl], RS[:], ALU.mult, [dACC, dRS, dSQ], [dSQ])
            stt(Y[:, h, tsl], SQ[:], pv(gname, l, h), Y[:, h, tsl], ALU.mult, ALU.mult, [dSQ, dPV, dY], [dY])

    def mixer_mlstm(l):
        QF = scv(0, 4096, BF16).rearrange("p (c t) -> p c t", c=2)
        KF = scv(8192, 4096, BF16).rearrange("p (c t) -> p c t", c=2)
        WKV = scv(16384, 6144, BF16).rearrange("p (k n) -> p k n", k=8)
        WGm = scv(28672, 128, BF16).rearrange("p (k n) -> p k n", k=8)
        o = [28928]

        def al(n, dtype):
            v = scv(o[0], n, dtype)
            o[0] += (n * (4 if dtype is F32 else 2) + 3) // 4 * 4
            return v
        KVT = al(768, BF16)
        IG, NLF, NEGC, KS = al(4, F32), al(4, F32), al(4, F32), al(4, F32)
        NLFB, DT, RD, HO = al(128, F32), al(128, F32), al(128, F32), al(128, F32)
        PW, QP = al(128, BF16), al(128, BF16)
        KTx = [al(128, BF16), al(128, BF16)]
        EBE = al(1, F32)
        Cst = al(258, F32).rearrange("p (c n) -> p c n", c=2)
        CBf = al(256, BF16).rearrange("p (c n) -> p c n", c=2)
        NBb = al(256, BF16).rearrange("p (c n) -> p c n", c=2)
        dQF, dKF, dWKV, dKVT, dG, dNLFB, dDT, dRD, dHO, dPW, dQP, dKT, dEBE, dCst, dCB = (Dep() for _ in range(15))
        linear_fm(w_in[l], 8, O_ML, 256, ufn, [dU], T,
                  lambda ps, pd, col, cw, t0, tn: ts(QF[:, col // 128, t0:t0 + tn], ps[:, 0:tn], 0.125, None, ALU.mult, None, [pd], [dQF]))
        linear_fm(w_in[l], 8, O_ML + 256, 256, ufn, [dU], T,
                  lambda ps, pd, col, cw, t0, tn: cp(KF[:, col // 128, t0:t0 + tn], ps[:, 0:tn], [pd], [dKF], eng=act))
        linear_fm(w_in[l], 8, O_ML + 1024, 512, ufn, [dU], T,
                  lambda ps, pd, col, cw, t0, tn: actf(Y[:, col // 128, t0:t0 + tn], ps[:, 0:tn], AF.Sigmoid, [pd], [dY]))
        wv = w_in[l]
        P.dma(pool, WKV[:, :, :], wv[:, O_ML + 256:O_ML + 1024].rearrange("(kc p) n -> p kc n", p=128), writes=[dWKV])
        P.dma(pool, WGm[:, :, :], wv[:, O_ML + 1536:O_ML + 1552].rearrange("(kc p) n -> p kc n", p=128), writes=[dWKV])
        memset(KTx[0][:], 0.0, [dKT])
        memset(KTx[1][:], 0.0, [dKT])
        GB = pv("ml_gb", l, 0, 16)
        URt = al(1024, BF16).rearrange("p (k n) -> p k n", k=8)
        dUR = Dep()
        TMPR = PADB[:, 0:4096].rearrange("p (c t) -> p c t", c=2)
        dTR = Dep()
        for d in range(2):
            if d == 1:
                for BUF, dB in ((QF, dQF), (KF, dKF)):
                    cp(TMPR[:, :, :], BUF[:, :, ::-1], [dB], [dTR])
                    cp(BUF[:, :, :], TMPR[:, :, :], [dTR], [dB])
            memset(Cst[:, :, :], 0.0, [dCst])
            memset(CBf[:, :, :], 0.0, [dCB])
            memset(NBb[:, :, :], 0.0, [dCB])
            for tau in range(16):
                tsl = tk(tau * 128, 128, d)
                pt = slice(tau * 128, tau * 128 + 128)
                if d == 1:
                    cp(URt[:, :, :], U[:, :, tsl], [dU], [dUR])
                    uf = lambda kc: URt[:, kc, :]
                else:
                    uf = lambda kc: U[:, kc, pt]
                for c0, cn in ((0, 512), (512, 256)):
                    ps, pd = P.psum()
                    for kc in range(8):
                        mm(ps[:, 0:cn], uf(kc), WKV[:, kc, c0:c0 + cn], kc == 0, kc == 7, [dU, dWKV, dUR], [pd])
                    cp(KVT[:, c0:c0 + cn], ps[:, 0:cn], [pd], [dKVT], eng=act)
                psg, pgd = P.psum()
                for kc in range(8):
                    mm(psg[:, 0:16], uf(kc), WGm[:, kc, :], kc == 0, kc == 7, [dU, dWKV, dUR], [pgd])
                tt(IG[:], psg[:, d * 4:d * 4 + 4], GB[:, d * 4:d * 4 + 4], ALU.add, [pgd, dPV], [dG])
                tt(NLF[:], psg[:, 8 + d * 4:12 + d * 4], GB[:, 8 + d * 4:12 + d * 4], ALU.add, [pgd, dPV], [dG])
                actf(NLF[:], NLF[:], AF.Exp, [dG], [dG], scale=-1.0)
                actf(NLF[:], NLF[:], AF.Ln, [dG], [dG], bias=1.0)
                psb, pbd = P.psum()
                mm(psb[:, 0:4], LEf, NLF[:], True, True, [dC, dG], [pbd])
                tt(NEGC[:], psb[:, 0:4], IG[:], ALU.add, [pbd, dG], [dG])
                for h in range(4):
                    hp = slice((h % 2) * 64, (h % 2) * 64 + 64)
                    hc = h // 2
                    ts(NLFB[:], ONf, NLF[:, h:h + 1], None, ALU.mult, None, [dC, dG], [dNLFB])
                    bb, bbd = P.psum()
                    mm(bb[:, 0:128], NLFB[:], LEf, True, True, [dNLFB, dC], [bbd])
                    actf(EBE[:], bb[:, 127:128], AF.Exp, [bbd], [dEBE], scale=-1.0)
                    actf(KS[:, h:h + 1], bb[:, 127:128], AF.Exp, [bbd, dG], [dG], scale=-1.0, bias=NEGC[:, h:h + 1])
                    actf(DT[:], bb[:, 0:128], AF.Exp, [bbd, dG], [dDT], scale=-1.0, bias=NEGC[:, h:h + 1])
                    tt(DT[:], DT[:], LEf, ALU.mult, [dDT, dC], [dDT])
                    st, std = P.psum()
                    mm(st[:, 0:128], KF[hp, hc, pt], QF[hp, hc, pt], True, True, [dKF, dQF], [std])
                    tt(PW[:], st[:, 0:128], DT[:], ALU.mult, [std, dDT], [dPW])
                    actf(RD[hp, :], bb[hp, 0:128], AF.Exp, [bbd], [dRD], scale=-1.0)
                    tt(QP[hp, :], QF[hp, hc, pt], RD[hp, :], ALU.mult, [dQF, dRD], [dQP])
                    nu, nud = P.psum()
                    mm(nu[:, 0:128], KVT[:, 256 + h * 128:384 + h * 128], PW[:], True, False, [dKVT, dPW], [nud])
                    mm(nu[:, 0:128], CBf[hp, hc, :], QP[hp, :], False, True, [dCB, dQP], [nud])
                    de, ded = P.psum()
                    mm(de[:, 0:128], ONb, PW[:], True, False, [dC, dPW], [ded])
                    mm(de[:, 0:128], NBb[hp, hc, :], QP[hp, :], False, True, [dCB, dQP], [ded])
                    actf(RD[:], de[:, 0:128], AF.Abs, [ded], [dRD])
                    ts(RD[:], RD[:], 1.0, None, ALU.max, None, [dRD], [dRD])
                    recip(RD[:], RD[:], [dRD], [dRD])
                    if d == 0:
                        tt(ACC[:, h, tsl], nu[:, 0:128], RD[:], ALU.mult, [nud, dRD], [dACC])
                    else:
                        tt(HO[:], nu[:, 0:128], RD[:], ALU.mult, [nud, dRD], [dHO])
                        tt(ACC[:, h, tsl], ACC[:, h, tsl], HO[:], ALU.add, [dHO, dACC], [dACC])
                    kt = KTx[h % 2]
                    ts(kt[:, hp], KVT[:, h * 64:h * 64 + 64], KS[:, h:h + 1], None, ALU.mult, None, [dKVT, dG], [dKT])
                    pc, pcd = P.psum()
                    mm(pc[:, 0:128], kt[:], KVT[:, 256 + h * 128:384 + h * 128], True, True, [dKT, dKVT], [pcd], inc=False)
                    mm(pc[:, 128:129], kt[:], ONb[:, 0:1], True, True, [dKT, dC], [pcd], inc=True)
                    stt(Cst[hp, hc, :], Cst[hp, hc, :], EBE[hp, :], pc[hp, 0:129], ALU.mult, ALU.add, [dCst, dEBE, pcd], [dCst])
                    cp(CBf[hp, hc, :], Cst[hp, hc, 0:128], [dCst], [dCB], eng=act)
                    ts(NBb[hp, hc, :], ONf[hp, :], Cst[hp, hc, 128:129], None, ALU.mult, None, [dCst, dC], [dCB])
        for h in range(4):
            head_rmsnorm_gate(l, "ml_norm", h, True)

    def mixer_hgrn2(l):
        LF = scv(0, 2048, F32)
        G = scv(8192, 2048, F32)
        Kb = scv(16384, 2048, BF16)
        QS = scv(20480, 2048, BF16)
        WI = scv(24576, 1024, BF16).rearrange("p (k n) -> p k n", k=8)
        o = [26624]

        def al(n, dtype):
            v = scv(o[0], n, dtype)
            o[0] += (n * (4 if dtype is F32 else 2) + 3) // 4 * 4
            return v
        GLt, EXt, Sst = (al(128, F32) for _ in range(3))
        QTt, QHt, ATT, VT, KHT, SBb = (al(128, BF16) for _ in range(6))
        TM = al(9 * 128, F32).rearrange("p (k n) -> p k n", k=9)
        KTLa = al(9 * 128, BF16).rearrange("p (k n) -> p k n", k=9)
        EGE = al(1, F32)
        URt = al(1024, BF16).rearrange("p (k n) -> p k n", k=8)
        dUR = Dep()
        dLF, dG, dKb, dQS, dWI, dGL, dEX, dTMP, dEXk, dS, dQT, dQH, dKH, dATT, dVT, dKHT, dSB, dKTL, dEGE = (Dep() for _ in range(19))
        dKTLs = [Dep() for _ in range(8)]
        for h in range(4):
            linear_fm(w_in[l], 8, O_HG + h * 128, 128, ufn, [dU], T,
                      lambda ps, pd, col, cw, t0, tn: actf(QS[:, t0:t0 + tn], ps[:, 0:tn], AF.Silu, [pd], [dQS]))
            linear_fm(w_in[l], 8, O_HG + 2048 + h * 128, 128, ufn, [dU], T,
                      lambda ps, pd, col, cw, t0, tn: actf(Y[:, h, t0:t0 + tn], ps[:, 0:tn], AF.Silu, [pd], [dY]))
            P.dma(pool, WI[:, :, :], w_in[l][:, O_HG + 1536 + h * 128:O_HG + 1664 + h * 128].rearrange("(kc p) n -> p kc n", p=128),
                  writes=[dWI])
            for d in range(2):
                linear_fm(w_in[l], 8, O_HG + 512 + d * 512 + h * 128, 128, ufn, [dU], T,
                          lambda ps, pd, col, cw, t0, tn: actf(LF[:, tk(t0, tn, d)], ps[:, 0:tn], AF.Sigmoid, [pd], [dLF]))
                ts(LF[:], LF[:], sv(l, 24 + d * 4 + h), sv(l, 16 + d * 4 + h), ALU.mult, ALU.add, [dLF, dSV], [dLF])
                ts(Kb[:], LF[:], -1.0, 1.0, ALU.mult, ALU.add, [dLF], [dKb])
                actf(LF[:], LF[:], AF.Ln, [dLF], [dLF])
                for tau in range(16):
                    pt = slice(tau * 128, tau * 128 + 128)
                    scan(G[:, pt], ONf, LF[:, pt], 0.0, [dLF, dC], [dG])
                memset(Sst[:], 0.0, [dS])
                memset(SBb[:], 0.0, [dSB])
                for tau in range(16):
                    t0 = tau * 128
                    pt = slice(t0, t0 + 128)
                    nt = tk(t0, 128, d)
                    Gt = G[:, pt]
                    Gt3 = Gt.rearrange("p (b i) -> p b i", b=8)
                    GL3 = GLt.rearrange("p (b i) -> p b i", b=8)
                    cp(GL3[:, 0, :], Gt3[:, 0, :], [dG], [dGL], eng=pool)
                    tt(GL3[:, 1:8, :], Gt3[:, 1:8, :], Gt3[:, 0:7, 15:16].to_broadcast([128, 7, 16]), ALU.subtract, [dG], [dGL], eng=pool)
                    actf(EXt[:], GLt[:], AF.Exp, [dGL], [dEX])
                    tt(QTt[:], QS[:, nt], EXt[:], ALU.mult, [dQS, dEX], [dQT])
                    actf(EXt[:], Gt, AF.Exp, [dG, dQT], [dEX])
                    tt(QHt[:], QS[:, nt], EXt[:], ALU.mult, [dQS, dEX], [dQH])
                    cp(TM[:, 0, :], Gt, [dG], [dTMP], eng=pool)
                    tt(TM[:, 1:9, :], Gt.unsqueeze(1).to_broadcast([128, 8, 128]),
                       Gt3[:, 0:8, 15:16].to_broadcast([128, 8, 128]), ALU.subtract, [dG], [dTMP])
                    actf(TM[:, :, :], TM[:, :, :], AF.Exp, [dTMP], [dTMP], scale=-1.0)
                    stt(KTLa[:, :, :], TM[:, :, :], 1e26, Kb[:, pt].unsqueeze(1).to_broadcast([128, 9, 128]),
                        ALU.min, ALU.mult, [dTMP, dKb], [dKTL])
                    at, atd = P.psum()
                    for I in range(8):
                        mm(at[:, 16 * I:16 * I + 16], KTLa[:, I, :], QTt[:, 16 * I:16 * I + 16], True, True, [dKTL, dQT], [atd],
                           inc=(I == 7))
                    tt(ATT[:], at[:, 0:128], LEf, ALU.mult, [atd, dC], [dATT])
                    KH = KTLa[:, 8, :]
                    actf(EGE[:], Gt[:, 127:128], AF.Exp, [dG], [dEGE])
                    vp, vpd = P.psum()
                    if d == 1:
                        cp(URt[:, :, :], U[:, :, nt], [dU], [dUR])
                    for kc in range(8):
                        mm(vp[:, 0:128], URt[:, kc, :] if d == 1 else U[:, kc, pt], WI[:, kc, :], kc == 0, kc == 7,
                           [dU, dWI, dUR], [vpd])
                    cp(VT[:], vp[:, 0:128], [vpd], [dVT], eng=act)
                    op_, opd = P.psum()
                    mm(op_[:, 0:128], VT[:], ATT[:], True, False, [dVT, dATT], [opd])
                    mm(op_[:, 0:128], SBb[:], QHt[:], False, True, [dSB, dQH], [opd])
                    if d == 0:
                        cp(ACC[:, h, nt], op_[:, 0:128], [opd], [dACC], eng=act)
                    else:
                        tt(ACC[:, h, nt], ACC[:, h, nt], op_[:, 0:128], ALU.add, [opd, dACC], [dACC])
                    tp_, tpd = P.psum()
                    mm(tp_[:, 0:128], KH, IDb, True, True, [dKTL, dC], [tpd])
                    cp(KHT[:], tp_[:, 0:128], [tpd], [dKHT], eng=act)
                    sp_, spd = P.psum()
                    mm(sp_[:, 0:128], KHT[:], VT[:], True, True, [dKHT, dVT], [spd])
                    stt(Sst[:], Sst[:], EGE[:], sp_[:, 0:128], ALU.mult, ALU.add, [dS, dEGE, spd], [dS])
                    cp(SBb[:], Sst[:], [dS], [dSB], eng=pool)
            head_rmsnorm_gate(l, "hg_norm", h, True)

    MUV = P.sb("MUV", [128, 40], F32)
    dMUV = Dep()

    def mixer_rwkv(l):
        deps = {}

        def dp(n):
            if n not in deps:
                deps[n] = Dep()
            return deps[n]
        P0 = scv(0, 2050, F32)
        PF = scv(8208, 2048, F32)
        G = scv(16400, 2048, F32)
        Bb = scv(24592, 2048, BF16)
        KT = [scv(28688, 2048, BF16), scv(32784, 2048, BF16)]
        TXW, XA, SXG, Rr, Vv, KK, Kraw, GG = (M[:, i, :] for i in range(8))
        ts(MUV[:, 0:14], pv("rw_mu", l, 0, 14), -1.0, 1.0, ALU.mult, ALU.add, [dPV], [dMUV])
        ts(MUV[:, 14:28], pv("rw_mu", l, 0, 14), 0.5, None, ALU.mult, None, [dPV], [dMUV])
        ts(MUV[:, 28:32], pv("rw_ka", l, 0, 4), -1.0, 1.0, ALU.mult, ALU.add, [dPV], [dMUV])
        ts(MUV[:, 32:36], pv("rw_rk", l, 0, 4), 0.5, None, ALU.mult, None, [dPV], [dMUV])

        def shifted(ci):
            memset(P0[:, 0:1], 0.0, [dp("P0")])
            memset(P0[:, 2049:2050], 0.0, [dp("P0")])
            linear_fm(w_in[l], 8, O_RW + ci * 128, 128, ufn, [dU], T,
                      lambda ps, pd, col, cw, t0, tn: cp(P0[:, 1 + t0:1 + t0 + tn], ps[:, 0:tn], [pd], [dp("P0")], eng=act))
            tt(PF[:], P0[:, 0:T], P0[:, 2:T + 2], ALU.add, [dp("P0")], [dp("PF")])
            ts(PF[:], PF[:], MUV[:, 14 + ci:15 + ci], None, ALU.mult, None, [dp("PF"), dMUV], [dp("PF")])
            stt(PF[:], P0[:, 1:T + 1], MUV[:, ci:ci + 1], PF[:], ALU.mult, ALU.add, [dp("P0"), dp("PF"), dMUV], [dp("PF")])

        shifted(12)
        actf(TXW[0:64, :], PF[0:64, :], AF.Tanh, [dp("PF")], [dp("TXW")])
        cp(XA[64:128, :], PF[64:128, :], [dp("PF")], [dp("XA")])
        shifted(13)
        actf(SXG[:, :], PF[:], AF.Sigmoid, [dp("PF")], [dp("SXG")])
        SQt = P0[:, 0:512]
        RSt = P0[:, 512:1024]
        T1 = P0[:, 1024:1536]
        HP = [slice(0, 64), slice(64, 128)]
        for j in range(rw_limit[0]):
            P.barrier()
            regions = [(SC, 18440, 20480), (RR, 40960, 45056)] + [(RR, k * 4096, (k + 1) * 4096) for k in range(4) if k != j]
            ri, ro = [0], [regions[0][1]]

            def al(n, dtype=BF16):
                ne = n * (2 if dtype is F32 else 1)
                ne = (ne + 1) // 2 * 2
                while ro[0] + ne > regions[ri[0]][2]:
                    ri[0] += 1
                    ro[0] = regions[ri[0]][1]
                t_, a_ = regions[ri[0]][0], ro[0]
                ro[0] += ne
                v = t_[:, a_:a_ + ne]
                return v.bitcast(F32) if dtype is F32 else v

            def mkset(sid):
                B = {"sid": sid}
                for nm in ("AT", "RT", "BT", "KTt", "BH", "KHh", "VTf", "ATk", "VTk", "BHk", "KHk", "PT"):
                    B[nm] = al(128)
                for nm in ("EXa", "EXb", "EXc", "EXd"):
                    B[nm] = al(128, F32)
                for nm in ("NT0", "WS", "AAK", "ARB", "ARK", "Xb", "Tb", "TTb", "UT"):
                    B[nm] = [al(128), al(128)]
                B["MT"] = [al(128, F32), al(128, F32)]
                B["Sbs"] = al(128)
                B["GC"] = al(64, F32)
                return B
            NSET = 3
            sets = [mkset(i) for i in range(NSET)]
            Sf = al(128, F32)
            for B in sets:
                for b_ in (B["UT"][0], B["UT"][1], B["MT"][0], B["MT"][1], B["Sbs"]):
                    memset(b_[:], 0.0, [dp("misc%d" % B["sid"])], eng=pool)
            shifted(j)
            cp(Rr[:, :], PF[:], [dp("PF")], [dp("R")], eng=act)
            shifted(8 + j)
            cp(Vv[:, :], PF[:], [dp("PF")], [dp("V")], eng=act)
            shifted(4 + j)
            cp(Kraw[:, :], PF[:], [dp("PF")], [dp("Kraw")], eng=act)
            ts(PF[:], PF[:], pv("rw_kk", l, j), None, ALU.mult, None, [dp("PF"), dPV], [dp("PF")])
            for tq in range(4):
                tsl = slice(tq * 512, (tq + 1) * 512)
                actf(SQt, PF[:, tsl], AF.Square, [dp("PF")], [dp("P0")])
                ps, pd = P.psum()
                mm(ps[:, :], BKf, SQt, True, True, [dC, dp("P0")], [pd])
                ts(RSt, ps[:, :], 1e-24, None, ALU.max, None, [pd], [dp("P0")])
                actf(RSt, RSt, AF.Sqrt, [dp("P0")], [dp("P0")])
                recip(RSt, RSt, [dp("P0")], [dp("P0")])
                tt(KK[:, tsl], PF[:, tsl], RSt, ALU.mult, [dp("PF"), dp("P0")], [dp("KK")])
                ps, pd = P.psum()
                mm(ps[:, :], G2s[:, j * 128:(j + 1) * 128], SXG[:, tsl], True, True, [dLW, dp("SXG")], [pd])
                cp(GG[:, tsl], ps[:, :], [pd], [dp("GG")], eng=act)
            AS = P0[:, 0:2048]
            for d in range(rw_limit[1]):
                for tq in range(4):
                    tsl = slice(tq * 512, (tq + 1) * 512)
                    ps, pd = P.psum()
                    mm(ps[:, :], W2s[:, d, j * 128:(j + 1) * 128], TXW[0:64, tsl], True, True, [dLW, dp("TXW")], [pd])
                    actf(PF[:, tk(tq * 512, 512, d)], ps[:, :], AF.Sigmoid, [pd, dPV], [dp("PF")], bias=pv("rw_w0", l, d * 4 + j))
                    ps, pd = P.psum()
                    mm(ps[:, :], A2s[:, d, j * 128:(j + 1) * 128], XA[64:128, tsl], True, True, [dLW, dp("XA")], [pd])
                    actf(AS[:, tk(tq * 512, 512, d)], ps[:, :], AF.Sigmoid, [pd, dPV], [dp("P0")], bias=pv("rw_a0", l, d * 4 + j))
                ts(PF[:], PF[:], -0.6065306597, None, ALU.mult, None, [dp("PF")], [dp("PF")])
                for tau in range(16):
                    pt = slice(tau * 128, tau * 128 + 128)
                    scan(G[:, pt], ONf, PF[:, pt], 0.0, [dp("PF"), dC], [dp("G")])
                rv = slice(None, None, -1) if d == 1 else slice(None)
                dKT = dp("KT%d" % d)
                ts(KT[d][:], AS, pv("rw_ka", l, j), MUV[:, 28 + j:29 + j], ALU.mult, ALU.add, [dp("P0"), dPV, dMUV], [dKT])
                tt(KT[d][:], KT[d][:], Kraw[:, rv], ALU.mult, [dKT, dp("Kraw")], [dKT])
                tt(Bb[:], KK[:, rv], AS, ALU.mult, [dp("KK"), dp("P0")], [dp("Bb")])
                memset(Sf[:], 0.0, [dp("Sf0"), dp("Sf1")])

                def D(B, nm, e=None):
                    return dp("%s%s_%d" % (nm, "" if e is None else str(e), B["sid"]))

                def prep(tau, B):
                    pt = slice(tau * 128, tau * 128 + 128)
                    nt = tk(tau * 128, 128, d)
                    Gt = G[:, pt]
                    actf(B["EXa"][:], Gt, AF.Exp, [dp("G")], [D(B, "EXa")])
                    tt(B["RT"][:], Rr[:, nt], B["EXa"][:], ALU.mult, [dp("R"), D(B, "EXa")], [D(B, "RT")])
                    cp(B["EXb"][:, 1:128], B["EXa"][:, 0:127], [D(B, "EXa")], [D(B, "EXb")], eng=pool)
                    memset(B["EXb"][:, 0:1], 1.0, [D(B, "EXb")], eng=pool)
                    stt(B["AT"][:], KK[:, nt], -1.0, B["EXb"][:], ALU.mult, ALU.mult, [dp("KK"), D(B, "EXb")], [D(B, "AT")])
                    actf(B["EXc"][:], Gt, AF.Exp, [dp("G")], [D(B, "EXc")], scale=-1.0)
                    tt(B["BT"][:], Bb[:, pt], B["EXc"][:], ALU.mult, [dp("Bb"), D(B, "EXc")], [D(B, "BT")])
                    tt(B["KTt"][:], KT[d][:, pt], B["EXc"][:], ALU.mult, [dKT, D(B, "EXc")], [D(B, "KTt")], eng=pool)
                    actf(B["EXd"][:], Gt, AF.Exp, [dp("G")], [D(B, "EXd")], scale=-1.0, bias=Gt[:, 127:128])
                    tt(B["BH"][:], Bb[:, pt], B["EXd"][:], ALU.mult, [dp("Bb"), D(B, "EXd")], [D(B, "BH")])
                    tt(B["KHh"][:], KT[d][:, pt], B["EXd"][:], ALU.mult, [dKT, D(B, "EXd")], [D(B, "KHh")], eng=pool)
                    cp(B["VTf"][:], Vv[:, nt], [dp("V")], [D(B, "VTf")], eng=pool)
                    for sn, dn_ in (("AT", "ATk"), ("VTf", "VTk"), ("BH", "BHk"), ("KHh", "KHk")):
                        ps, pd = P.psum()
                        mm(ps[:, 0:128], B[sn][:], IDb, True, True, [D(B, sn), dC], [pd])
                        cp(B[dn_][:], ps[:, 0:128], [pd], [D(B, dn_)], eng=act)

                def score(B, lh, ln, rh, rn, mask, dst, dn_):
                    ps, pd = P.psum()
                    mm(ps[:, 0:128], lh, rh, True, True, [D(B, ln), D(B, rn)], [pd])
                    tt(dst[:], ps[:, 0:128], mask, ALU.mult, [pd, dC], [dn_])

                def st_scores(tau, B, e):
                    hp = HP[e]
                    AT, BT, KTt, RT = B["AT"], B["BT"], B["KTt"], B["RT"]
                    score(B, BT[hp, :], "BT", AT[hp, :], "AT", LTf, B["NT0"][e], D(B, "NT", e))
                    score(B, AT[hp, :], "AT", BT[hp, :], "BT", GTf, B["WS"][e], D(B, "WS", e))
                    score(B, KTt[hp, :], "KTt", AT[hp, :], "AT", LTf, B["AAK"][e], D(B, "AAK", e))
                    score(B, BT[hp, :], "BT", RT[hp, :], "RT", LEf, B["ARB"][e], D(B, "ARB", e))
                    score(B, KTt[hp, :], "KTt", RT[hp, :], "RT", LEf, B["ARK"][e], D(B, "ARK", e))

                def st_x0(tau, B, e):
                    hp, oc = HP[e], HP[1 - e]
                    ps, pd = P.psum()
                    mm(ps[:, 0:64], B["AAK"][e][:], B["VTk"][:, hp], True, True, [D(B, "AAK", e), D(B, "VTk")], [pd])
                    cp(B["Xb"][e][:, oc], ps[:, 0:64], [pd], [D(B, "Xb", e)], eng=act)
                    cp(B["Xb"][e][:, hp], B["ATk"][:, hp], [D(B, "ATk")], [D(B, "Xb", e)], eng=pool)

                def st_lvl0(tau, B, e):
                    Tb, TTb = B["Tb"][e], B["TTb"][e]
                    tt(Tb[:], B["WS"][e][:], LMb[0], ALU.mult, [D(B, "WS", e), dC], [D(B, "T", e)])
                    tt(Tb[:], Tb[:], IDb, ALU.add, [D(B, "T", e), dC], [D(B, "T", e)])
                    tt(TTb[:], B["NT0"][e][:], LMTb[0], ALU.mult, [D(B, "NT", e), dC], [D(B, "TT", e)], eng=pool)
                    tt(TTb[:], TTb[:], IDb, ALU.add, [D(B, "TT", e), dC], [D(B, "TT", e)], eng=pool)

                def mk_lvlA(k):
                    def st(tau, B, e):
                        ps, pd = P.psum()
                        mm(ps[:, 0:128], B["NT0"][e][:], B["Tb"][e][:], True, True, [D(B, "NT", e), D(B, "T", e)], [pd])
                        tt(B["WS"][e][:], ps[:, 0:128], LMb[k], ALU.mult, [pd, dC], [D(B, "WS", e)])
                    return st

                def mk_lvlB(k):
                    def st(tau, B, e):
                        Tb, TTb, WS = B["Tb"][e], B["TTb"][e], B["WS"][e]
                        dT, dTT, dWS = D(B, "T", e), D(B, "TT", e), D(B, "WS", e)
                        pz, pzd = P.psum()
                        mm(pz[:, 0:128], IDb, Tb[:], True, False, [dC, dT], [pzd])
                        mm(pz[:, 0:128], TTb[:], WS[:], False, True, [dTT, dWS], [pzd])
                        pt_, ptd = P.psum()
                        mm(pt_[:, 0:128], WS[:], TTb[:], True, True, [dTT, dWS], [ptd])
                        cp(Tb[:], pz[:, 0:128], [pzd], [dT], eng=act)
                        tt(TTb[:], TTb[:], pt_[:, 0:128], ALU.add, [ptd, dTT], [dTT])
                    return st

                def st_apply(tau, B, e):
                    ps, pd = P.psum()
                    mm(ps[:, 0:128], B["TTb"][e][:], B["Xb"][e][:], True, True, [D(B, "TT", e), D(B, "Xb", e)], [pd])
                    cp(B["Xb"][e][:], ps[:, 0:128], [pd], [D(B, "Xb", e)], eng=act)

                def st_mt(tau, B, e):
                    hp = HP[e]
                    ps, pd = P.psum()
                    mm(ps[:, 0:64], B["Xb"][e][:], B["BHk"][:, hp], True, True, [D(B, "Xb", e), D(B, "BHk")], [pd])
                    stt(B["MT"][e][hp, hp], IDf[hp, hp], B["EXa"][hp, 127:128], ps[hp, 0:64], ALU.mult, ALU.add,
                        [pd, dC, D(B, "EXa")], [D(B, "MT", e)])

                def st_gc(tau, B, e):
                    hp, oc = HP[e], HP[1 - e]
                    ps, pd = P.psum()
                    mm(ps[:, 0:64], B["BHk"][:], B["Xb"][e][:, oc], True, False, [D(B, "Xb", e), D(B, "BHk")], [pd])
                    mm(ps[:, 0:64], B["KHk"][:], B["VTk"][:, hp], False, True, [D(B, "KHk"), D(B, "VTk")], [pd])
                    cp(B["GC"][hp, :], ps[hp, 0:64], [pd], [D(B, "GC", e)], eng=act)

                def st_pt(tau, B, e):
                    hp = HP[e]
                    ps, pd = P.psum()
                    mm(ps[:, 0:128], B["Xb"][e][:], IDb, True, True, [D(B, "Xb", e), dC], [pd])
                    cp(B["PT"][hp, :], ps[hp, 0:128], [pd], [D(B, "PT", e)], eng=act)

                def st_u(tau, B, e):
                    hp, oc = HP[e], HP[1 - e]
                    ps, pd = P.psum()
                    mm(ps[:, 0:64], B["PT"][hp, :], B["Sbs"][hp, hp], True, True, [D(B, "PT", e), D(B, "Sbs", e)], [pd])
                    tt(B["UT"][e][:, hp], ps[:, 0:64], B["Xb"][e][:, oc], ALU.add, [pd, D(B, "Xb", e)], [D(B, "UT", e)])

                def st_y(tau, B, e):
                    hp = HP[e]
                    nt = tk(tau * 128, 128, d)
                    ps, pd = P.psum()
                    mm(ps[:, 0:128], B["Sbs"][hp, :], B["RT"][hp, :], True, False, [D(B, "Sbs", e), D(B, "RT")], [pd])
                    mm(ps[:, 0:128], B["UT"][e][:], B["ARB"][e][:], False, False, [D(B, "UT", e), D(B, "ARB", e)], [pd])
                    mm(ps[:, 0:128], B["VTk"][:], B["ARK"][e][:], False, True, [D(B, "VTk"), D(B, "ARK", e)], [pd])
                    if d == 0:
                        cp(ACC[hp, j, nt], ps[hp, 0:128], [pd], [dACC], eng=act)
                    else:
                        tt(ACC[hp, j, nt], ACC[hp, j, nt], ps[hp, 0:128], ALU.add, [pd, dACC], [dACC])

                def st_chain(tau, B, e):
                    hp = HP[e]
                    cp(B["Sbs"][hp, hp], Sf[hp, hp], [dp("Sf%d" % e)], [D(B, "Sbs", e)], eng=act)
                    ps, pd = P.psum()
                    mm(ps[:, 0:64], B["MT"][e][hp, :], Sf[hp, hp], True, True, [D(B, "MT", e), dp("Sf%d" % e)], [pd])
                    tt(Sf[hp, hp], ps[hp, 0:64], B["GC"][hp, :], ALU.add, [pd, D(B, "GC", e), dp("Sf%d" % e)], [dp("Sf%d" % e)])

                indep = [st_scores, st_x0, st_lvl0]
                for k in range(1, 7):
                    indep += [mk_lvlA(k), mk_lvlB(k)]
                indep += [st_apply, st_mt, st_gc, st_pt]
                ntile = rw_limit[2]
                for tau0 in range(0, ntile, NSET):
                    ctx = [(tau0 + i, sets[i]) for i in range(min(NSET, ntile - tau0))]
                    for tau, B in ctx:
                        prep(tau, B)
                    for stg in indep:
                        for tau, B in ctx:
                            for e in range(2):
                                stg(tau, B, e)
                    for tau, B in ctx:
                        for e in range(2):
                            st_chain(tau, B, e)
                    for stg in (st_u, st_y):
                        for tau, B in ctx:
                            for e in range(2):
                                stg(tau, B, e)
            P.barrier()
            for tq in range(4):
                tsl = slice(tq * 512, (tq + 1) * 512)
                ps, pd = P.psum()
                mm(ps[:, :], BKf, ACC[:, j, tsl], True, True, [dC, dACC], [pd])
                stt(SQt, ps[:, :], -1.0 / 64, ACC[:, j, tsl], ALU.mult, ALU.add, [pd, dACC], [dp("P0")])
                actf(RSt, SQt, AF.Square, [dp("P0")], [dp("P0")])
                ps, pd = P.psum()
                mm(ps[:, :], BKf, RSt, True, True, [dC, dp("P0")], [pd])
                actf(RSt, ps[:, :], AF.Sqrt, [pd], [dp("P0")], bias=64e-5, scale=1.0 / 64)
                recip(RSt, RSt, [dp("P0")], [dp("P0")])
                tt(SQt, SQt, RSt, ALU.mult, [dp("P0")], [dp("P0")])
                ts(SQt, SQt, pv("rw_lnw", l, j), pv("rw_lnb", l, j), ALU.mult, ALU.add, [dp("P0"), dPV], [dp("P0")])
                tt(T1, KT[0][:, tsl], KT[1][:, tk(tq * 512, 512, 1)], ALU.add, [dp("KT0"), dp("KT1")], [dp("P0")])
                tt(T1, T1, Rr[:, tsl], ALU.mult, [dp("P0"), dp("R")], [dp("P0")])
                ts(T1, T1, MUV[:, 32 + j:33 + j], None, ALU.mult, None, [dp("P0"), dMUV], [dp("P0")])
                ps, pd = P.psum()
                mm(ps[:, :], BKf, T1, True, True, [dC, dp("P0")], [pd])
                tt(T1, ps[:, :], Vv[:, tsl], ALU.mult, [pd, dp("V")], [dp("P0")])
                tt(SQt, SQt, T1, ALU.add, [dp("P0")], [dp("P0")])
                tt(Y[:, j, tsl], SQt, GG[:, tsl], ALU.mult, [dp("P0"), dp("GG")], [dY])

    def mixer_stub(l):
        pass

    mix_fns = {0: globals().get("_mx_rwkv"), 1: mixer_mlstm, 2: mixer_lru, 3: globals().get("_mx_hg")}
    mix_fns[0] = locals().get("mixer_rwkv", mixer_stub)
    mix_fns[3] = locals().get("mixer_hgrn2", mixer_stub)

    def layer(l, sq):
        src = xT[sq] if l == 0 else hd[sq]
        load_layer_small(l)
        norm_mod(src, l, sq, 0)
        P.barrier()
        if l == 0 and sq == 0:
            tap("U", U[:, :, :], [dU], None)
        first = True
        for n in range(4):
            if n in mixers:
                mix_fns[n](l)
                P.barrier()
                if l == 0 and sq == 0:
                    tap("Y%d" % n, Y[:, :, :], [dY], None)
                gate_merge(l, n, first)
                first = False
                P.barrier()
        out_proj_residual(l, sq, src, hd[sq])
        P.barrier()
        norm_mod(hd[sq], l, sq, 1)
        P.barrier()
        ffn(l, sq, hd[sq])
        P.barrier()

    for sq in range(n_seq):
        for l in range(n_layers):
            layer(l, sq)
        norm_mod(hd[sq], -1, sq, 0)
        P.barrier()
    P.finish()
    es.close()
    return nc, P


def make_in_maps(inputs, n_cores=NCORE):
    inp = {k: np.asarray(v) for k, v in inputs.items()}
    pv = pack_pv(inp)
    cst = pack_consts()
    shared = {
        "pv": pv, "cst": cst,
        "ada_w": np.ascontiguousarray(inp["ada_w"], np.float32), "w_in": np.ascontiguousarray(inp["w_in"], np.float32),
        "rw_w2": np.ascontiguousarray(inp["rw_w2"], np.float32), "rw_a2": np.ascontiguousarray(inp["rw_a2"], np.float32),
        "rw_g2": np.ascontiguousarray(inp["rw_g2"], np.float32),
        "lru_wa": pack_lru_bd(inp["lru_wa"]), "lru_wx": pack_lru_bd(inp["lru_wx"]),
        "w_branch": np.ascontiguousarray(inp["w_branch"], np.float32), "w_out": np.ascontiguousarray(inp["w_out"], np.float32),
        "ffn_up": np.ascontiguousarray(inp["ffn_up"], np.float32), "ffn_down": np.ascontiguousarray(inp["ffn_down"], np.float32),
    }
    maps = []
    for i in range(n_cores):
        xs = inp["x"][i * NSEQ:(i + 1) * NSEQ]
        m = dict(shared)
        m["xT"] = np.ascontiguousarray(np.transpose(xs, (0, 2, 1)), np.float32)
        cs = inp["c"][i * NSEQ:(i + 1) * NSEQ].astype(np.float32)
        m["cT"] = np.ascontiguousarray(cs.T.reshape(8, 128, NSEQ).transpose(1, 0, 2))
        maps.append(m)
    return maps


def kernel(**inputs):
    nc, _ = build()
    maps = make_in_maps(inputs)
    res = run_bass_kernel_spmd(nc, maps, core_ids=list(range(NCORE)))
    outs = [np.transpose(r["outT"], (0, 2, 1)) for r in res.results]
    return np.ascontiguousarray(np.concatenate(outs, axis=0), dtype=np.float32)
```

```python
import numpy as np
import concourse.bass as bass
import concourse.mybir as mybir
from concourse.bass_utils import run_bass_kernel_spmd
from contextlib import ExitStack

F32 = mybir.dt.float32
BF16 = mybir.dt.bfloat16
ALU = mybir.AluOpType
AF = mybir.ActivationFunctionType

D = 1024
T = 2048
L = 4
NSEQ = 2
NCORE = 8
W = 512
N_IN = 11024
O_RW, O_ML, O_LRU, O_HG, O_GATE = 0, 1792, 3344, 4368, 6928
DFF = 2816
class Track:
    def __init__(self, sem, step):
        self.sem = sem
        self.val = 0
        self.step = step


class Dep:
    __slots__ = ("w", "r")

    def __init__(self):
        self.w = None
        self.r = {}


class Eng:
    def __init__(self, name, h, tr, is_pe=False):
        self.name = name
        self.h = h
        self.tr = tr
        self.is_pe = is_pe
        self.seen = {}
        self.ops = []
        self.pool = []
        self.dma_i = 0


class Prog:
    def __init__(self, nc, es):
        self.nc = nc
        self.es = es
        self.tracks = []

        def mk(name, step=1):
            t = Track(es.enter_context(nc.semaphore(name)), step)
            self.tracks.append(t)
            return t

        self.pe = Eng("pe", nc.tensor, mk("s_pe"), True)
        self.act = Eng("act", nc.scalar, mk("s_act"))
        self.dve = Eng("dve", nc.vector, mk("s_dve"))
        self.pool = Eng("pool", nc.gpsimd, mk("s_pool"))
        self.sp = Eng("sp", nc.sync, mk("s_sp"))
        self.engs = [self.pe, self.act, self.dve, self.pool, self.sp]
        for e, n in ((self.sp, 16), (self.pool, 16), (self.act, 4)):
            e.pool = [mk("d_%s%d" % (e.name, i), 16) for i in range(n)]
        self.nps = 0
        self.psums = []
        self.n_ops = 0

    def sb(self, name, shape, dt=F32):
        return self.es.enter_context(self.nc.sbuf_tensor(name, list(shape), dt))

    def make_psums(self):
        for i in range(8):
            t = self.es.enter_context(self.nc.psum_tensor("ps%d" % i, [128, 512], F32))
            self.psums.append((t, Dep()))

    def psum(self):
        t = self.psums[self.nps % 8]
        self.nps += 1
        return t

    def _waits(self, eng, reads, writes, extra=()):
        need = {}

        def add(ev):
            if ev is None:
                return
            tr, v = ev
            if need.get(tr, 0) < v:
                need[tr] = v

        for d in reads:
            add(d.w)
        for d in writes:
            add(d.w)
            for tr, v in d.r.items():
                add((tr, v))
        for ev in extra:
            add(ev)
        for tr, v in need.items():
            if tr is eng.tr and eng.is_pe:
                continue
            if eng.seen.get(tr, 0) >= v:
                continue
            eng.seen[tr] = v
            eng.ops.append(("wait", tr.sem, v))

    def op(self, eng, fn, reads=(), writes=(), inc=True):
        self._waits(eng, reads, writes)
        val = eng.tr.val + 1
        if inc:
            eng.tr.val = val
        eng.ops.append(("op", fn, inc))
        self.n_ops += 1
        for d in reads:
            if d.r.get(eng.tr, 0) < val:
                d.r[eng.tr] = val
        for d in writes:
            d.w = (eng.tr, val)
            d.r = {}

    def dma(self, eng, out, in_, reads=(), writes=(), **kw):
        tr = eng.pool[eng.dma_i % len(eng.pool)]
        eng.dma_i += 1
        extra = [(tr, tr.val)] if tr.val > 0 else []
        self._waits(eng, reads, writes, extra)
        tr.val += 16
        eng.ops.append(("dma", out, in_, tr.sem, kw))
        self.n_ops += 1
        for d in reads:
            d.r[tr] = tr.val
        for d in writes:
            d.w = (tr, tr.val)
            d.r = {}

    def barrier(self):
        for e in self.engs:
            for tr in self.tracks:
                if tr.val > e.seen.get(tr, 0):
                    if tr is e.tr and e.is_pe:
                        continue
                    e.seen[tr] = tr.val
                    e.ops.append(("wait", tr.sem, tr.val))

    def finish(self):
        self.barrier()
        nc = self.nc

        def replay(e, h):
            for o in e.ops:
                if o[0] == "wait":
                    h.wait_ge(o[1], o[2])
                elif o[0] == "op":
                    ins = o[1](h)
                    if o[2]:
                        ins.then_inc(e.tr.sem, 1)
                else:
                    h.dma_start(out=o[1], in_=o[2], **o[4]).then_inc(o[3], 16)

        with nc.Block() as block:
            @block.tensor
            def _(h):
                replay(self.pe, h)

            @block.scalar
            def _(h):
                replay(self.act, h)

            @block.vector
            def _(h):
                replay(self.dve, h)

            @block.gpsimd
            def _(h):
                replay(self.pool, h)

            @block.sync
            def _(h):
                replay(self.sp, h)

PV_SPEC = [("ada_b", 48), ("n1g", 8), ("n2g", 8), ("rw_mu", 14), ("rw_w0", 8), ("rw_a0", 8), ("rw_kk", 4),
           ("rw_ka", 4), ("rw_rk", 4), ("rw_lnw", 4), ("rw_lnb", 4), ("ml_norm", 4), ("lru_cw", 16),
           ("lru_cb", 4), ("lru_ba", 8), ("lru_bx", 8), ("lru_lam", 8), ("hg_lb", 8), ("hg_norm", 4),
           ("ffn_cw", 132), ("ffn_cb", 44), ("ml_gb", 16)]
PV_L = sum(n for _, n in PV_SPEC)
PV_OFF = {}
_o = 0
for _n, _c in PV_SPEC:
    PV_OFF[_n] = _o
    _o += _c
NPV = PV_L * L + 8


def _fm(v):
    return np.ascontiguousarray(np.asarray(v, np.float32).reshape(-1, 128).T)


def pack_pv(inp):
    pv = np.zeros((128, NPV), np.float32)
    for l in range(L):
        ent = {
            "ada_b": _fm(inp["ada_b"][l]), "n1g": _fm(inp["norm1_g"][l]), "n2g": _fm(inp["norm2_g"][l]),
            "rw_mu": _fm(inp["rw_mu"][l]), "rw_w0": _fm(inp["rw_w0"][l].reshape(-1)),
            "rw_a0": _fm(inp["rw_a0"][l].reshape(-1)), "rw_kk": _fm(inp["rw_kk"][l]),
            "rw_ka": _fm(inp["rw_ka"][l]), "rw_rk": _fm(inp["rw_rk"][l].reshape(-1)),
            "rw_lnw": _fm(inp["rw_lnw"][l]), "rw_lnb": _fm(inp["rw_lnb"][l]), "ml_norm": _fm(inp["ml_norm"][l]),
            "lru_cw": _fm(inp["lru_conv_w"][l].reshape(-1)), "lru_cb": _fm(inp["lru_conv_b"][l]),
            "lru_ba": _fm(inp["lru_ba"][l].reshape(-1)), "lru_bx": _fm(inp["lru_bx"][l].reshape(-1)),
            "lru_lam": _fm(inp["lru_lam"][l].reshape(-1)), "hg_lb": _fm(inp["hg_lb"][l].reshape(-1)),
            "hg_norm": _fm(inp["hg_norm"][l]), "ffn_cw": _fm(inp["ffn_conv_w"][l].reshape(-1)),
            "ffn_cb": _fm(inp["ffn_conv_b"][l]),
            "ml_gb": np.tile(np.concatenate([inp["ml_ibias"][l].reshape(-1), inp["ml_fbias"][l].reshape(-1)])[None, :],
                             (128, 1)).astype(np.float32),
        }
        for n, c in PV_SPEC:
            assert ent[n].shape == (128, c), (n, ent[n].shape)
            pv[:, l * PV_L + PV_OFF[n]: l * PV_L + PV_OFF[n] + c] = ent[n]
    pv[:, PV_L * L:] = _fm(inp["final_g"])
    return pv


NCST = 20


def pack_consts():
    c = np.zeros((128, NCST, 128), np.float32)
    i = np.arange(128)
    c[:, 0, :] = np.eye(128)
    c[:, 1, :] = 1.0
    c[:, 2, :] = (i[:, None] <= i[None, :])
    c[:, 3, :] = (i[:, None] < i[None, :])
    c[:, 4, :] = (i[:, None] // 64 == i[None, :] // 64)
    c[:, 5, :] = (i[:, None] > i[None, :])
    for k in range(7):
        bsz = 1 << k
        t, s_ = i[:, None], i[None, :]
        m = (t // (2 * bsz) == s_ // (2 * bsz)) & (t % (2 * bsz) >= bsz) & (s_ % (2 * bsz) < bsz)
        c[:, 6 + k, :] = m
        c[:, 13 + k, :] = m.T
    return c


def pack_lru_bd(w):
    out = np.zeros((L, 2, 4, 128, 128), np.float32)
    for j in range(4):
        out[:, :, j, 0:64, 0:64] = w[:, :, 2 * j]
        out[:, :, j, 64:128, 64:128] = w[:, :, 2 * j + 1]
    return out


def build(n_layers=L, n_seq=NSEQ, mixers=(0, 1, 2, 3), taps=(), rw_limit=(4, 2, 16)):
    nc = bass.Bass("TRN2", target_bir_lowering=False)
    dt = lambda name, shape, kind="ExternalInput": nc.dram_tensor(name, list(shape), F32, kind=kind).ap()
    xT = dt("xT", [NSEQ, D, T])
    cT = dt("cT", [128, 8, NSEQ])
    pvd = dt("pv", [128, NPV])
    cst = dt("cst", [128, NCST, 128])
    ada_w = dt("ada_w", [L, D, 6 * D])
    w_in = dt("w_in", [L, D, N_IN])
    rw_w2 = dt("rw_w2", [L, 2, 64, W])
    rw_a2 = dt("rw_a2", [L, 2, 64, W])
    rw_g2 = dt("rw_g2", [L, 128, W])
    lru_wa = dt("lru_wa", [L, 2, 4, 128, 128])
    lru_wx = dt("lru_wx", [L, 2, 4, 128, 128])
    w_branch = dt("w_branch", [L, 4, W, D])
    w_out = dt("w_out", [L, D, D])
    ffn_up = dt("ffn_up", [L, D, 2 * DFF])
    ffn_down = dt("ffn_down", [L, DFF, D])
    outT = dt("outT", [NSEQ, D, T], "ExternalOutput")
    hd = dt("hd", [NSEQ, D, T], "Internal")
    tapd = {}
    for name, shape in taps:
        tapd[name] = dt("tap_" + name, shape, "ExternalOutput")

    es = ExitStack()
    P = Prog(nc, es)
    P.make_psums()
    pe, act, dve, pool, sp = P.pe, P.act, P.dve, P.pool, P.sp

    def mm(out, lhsT, rhs, start, stop, r, w, inc=None):
        P.op(pe, lambda h: h.matmul(out, lhsT, rhs, start=start, stop=stop), r, w, inc=stop if inc is None else inc)

    def actf(out, in_, func, r, w, bias=None, scale=None):
        kw = {}
        if bias is not None:
            kw["bias"] = bias
        if scale is not None:
            kw["scale"] = scale
        P.op(act, lambda h: h.activation(out, in_, func, **kw), r, w)

    def tt(out, a, b, op, r, w, eng=None):
        P.op(eng or dve, lambda h: h.tensor_tensor(out, a, b, op), r, w)

    def ts(out, a, s1, s2, op0, op1, r, w, eng=None):
        if op1 is None:
            P.op(eng or dve, lambda h: h.tensor_scalar(out, a, s1, None, op0), r, w)
        else:
            P.op(eng or dve, lambda h: h.tensor_scalar(out, a, s1, s2, op0, op1), r, w)

    def stt(out, a, s, b, op0, op1, r, w, eng=None):
        P.op(eng or dve, lambda h: h.scalar_tensor_tensor(out, a, s, b, op0, op1), r, w)

    def cp(out, in_, r, w, eng=None):
        e = eng or dve
        if e is act:
            P.op(act, lambda h: h.copy(out, in_), r, w)
        else:
            P.op(e, lambda h: h.tensor_copy(out, in_), r, w)

    def memset(ap, v, w, eng=None):
        P.op(eng or dve, lambda h: h.memset(ap, v), (), w)

    def recip(out, in_, r, w):
        P.op(dve, lambda h: h.reciprocal(out, in_), r, w)

    def scan(out, d0, d1, init, r, w):
        P.op(dve, lambda h: h.tensor_tensor_scan(out, d0, d1, init, ALU.mult, ALU.add), r, w)

    def tap(name, src_ap, r, dst=None):
        if name in tapd:
            P.dma(pool, tapd[name] if dst is None else dst, src_ap, reads=r)

    U = P.sb("U", [128, 8, T], BF16)
    dU = Dep()
    RR = P.sb("RR", [128, 45056], BF16)
    ACC = RR[:, 0:16384].bitcast(F32).rearrange("p (c t) -> p c t", c=4)
    Y = RR[:, 16384:24576].rearrange("p (c t) -> p c t", c=4)
    M = RR[:, 24576:40960].rearrange("p (c t) -> p c t", c=8)
    AFF = RR[:, 0:45056].rearrange("p (c t) -> p c t", c=22)
    dACC, dY, dM, dAFF = Dep(), Dep(), Dep(), Dep()
    SCB = 40960
    SC = P.sb("SC", [128, SCB // 2], BF16)

    def scv(off, n, dtype):
        assert off % 4 == 0
        if dtype is F32:
            assert off + 4 * n <= SCB, (off, n)
            return SC[:, off // 2: off // 2 + 2 * n].bitcast(F32)
        assert off + 2 * n <= SCB, (off, n)
        return SC[:, off // 2: off // 2 + n]

    NSLAB = 3
    slabs = [(P.sb("slab%d" % i, [128, 8, 512], BF16), Dep()) for i in range(NSLAB)]
    slab_i = [0]
    PV = P.sb("PV", [128, NPV], F32)
    dPV = Dep()
    CF = P.sb("CF", [128, 6, 128], F32)
    CB = P.sb("CB", [128, NCST, 128], BF16)
    dC = Dep()
    MOD = P.sb("MOD", [128, L * 48 * NSEQ], F32)
    dMOD = Dep()
    SV = P.sb("SV", [128, L * 32], F32)
    dSV = Dep()
    CND = P.sb("CND", [128, 8, NSEQ], F32)
    CNDB = P.sb("CNDB", [128, 8, NSEQ], BF16)
    dCND = Dep()
    LW = P.sb("LW", [128, 3584], BF16)
    dLW = Dep()
    W2s = LW[0:64, 0:1024].rearrange("p (d n) -> p d n", d=2)
    A2s = LW[64:128, 0:1024].rearrange("p (d n) -> p d n", d=2)
    G2s = LW[:, 1024:1536]
    WAs = LW[:, 1536:2560].rearrange("p (d j n) -> p d j n", d=2, j=4)
    WXs = LW[:, 2560:3584].rearrange("p (d j n) -> p d j n", d=2, j=4)

    IDf, ONf, LEf, LTf, BKf = (CF[:, i, :] for i in range(5))
    IDb, ONb, LEb, LTb, BKb = (CB[:, i, :] for i in range(5))
    GTf = CF[:, 5, :]
    LMb = [CB[:, 6 + k, :] for k in range(7)]
    LMTb = [CB[:, 13 + k, :] for k in range(7)]

    def pv(name, l, c0=0, n=1):
        o = l * PV_L + PV_OFF[name] + c0
        return PV[:, o:o + n]

    def mod(l, k, c, sq):
        o = ((l * 6 + k) * 8 + c) * NSEQ + sq
        return MOD[:, o:o + 1]

    def load_slab(W2d, k0, nkc, col0, ncols, dst_col=0, new=True):
        if new:
            slab_i[0] += 1
        sl, sd = slabs[slab_i[0] % NSLAB]
        src = W2d[k0 * 128:(k0 + nkc) * 128, col0:col0 + ncols].rearrange("(kc p) n -> p kc n", p=128)
        P.dma(pool, sl[:, 0:nkc, dst_col:dst_col + ncols], src, writes=[sd])
        return sl, sd

    def linear_fm(W2d, nkc, col0, ncols, xfn, xdeps, ntok, evac):
        for cg in range(0, ncols, 512):
            n = min(512, ncols - cg)
            sl, sd = load_slab(W2d, 0, nkc, col0 + cg, n)
            for c in range(0, n, 128):
                cw = min(128, n - c)
                for t0 in range(0, ntok, 512):
                    tn = min(512, ntok - t0)
                    ps, pd = P.psum()
                    for kc in range(nkc):
                        mm(ps[0:cw, 0:tn], sl[:, kc, c:c + cw], xfn(kc, t0, tn), kc == 0, kc == nkc - 1,
                           [sd] + xdeps, [pd])
                    evac(ps, pd, cg + c, cw, t0, tn)

    ufn = lambda kc, t0, tn: U[:, kc, t0:t0 + tn]

    P.dma(sp, PV[:], pvd[:, :], writes=[dPV])
    P.dma(sp, CF[:], cst[:, 0:6, :], writes=[dC])
    P.dma(pool, CB[:], cst[:, :, :], writes=[dC])
    P.dma(sp, CND[:], cT[:, :, :], writes=[dCND])
    actf(CND[:], CND[:], AF.Silu, [dCND], [dCND])
    cp(CNDB[:], CND[:], [dCND], [dCND])
    for l in range(n_layers):
        def ev_mod(ps, pd, col, cw, t0, tn, l=l):
            cidx = col // 128
            o = (l * 48 + cidx) * NSEQ
            ts(MOD[:, o:o + NSEQ], ps[:, 0:NSEQ], pv("ada_b", l, cidx), None, ALU.add, None, [pd, dPV], [dMOD])
        linear_fm(ada_w[l], 8, 0, 6 * D, lambda kc, t0, tn: CNDB[:, kc, :], [dCND], NSEQ, ev_mod)
    def sv(l, o, n=1):
        return SV[:, l * 32 + o: l * 32 + o + n]
    TS = scv(0, 64, F32)
    dTS = Dep()
    for l in range(n_layers):
        lam = pv("lru_lam", l, 0, 8)
        actf(TS[:, 0:8], lam, AF.Abs, [dPV], [dTS])
        actf(TS[:, 0:8], TS[:, 0:8], AF.Exp, [dTS], [dTS], scale=-1.0)
        actf(TS[:, 0:8], TS[:, 0:8], AF.Ln, [dTS], [dTS], bias=1.0)
        ts(TS[:, 8:16], lam, -1.0, 0.0, ALU.mult, ALU.max, [dPV], [dTS])
        tt(TS[:, 0:8], TS[:, 0:8], TS[:, 8:16], ALU.add, [dTS], [dTS])
        ts(sv(l, 0, 8), TS[:, 0:8], -8.0, None, ALU.mult, None, [dTS], [dSV])
        ts(sv(l, 8, 8), TS[:, 0:8], -16.0, None, ALU.mult, None, [dTS], [dSV])
    EX = scv(256, 32, F32)
    SM = scv(384, 8, F32)
    dEX = Dep()
    for l in range(L):
        actf(EX[:, l * 8:(l + 1) * 8], pv("hg_lb", l, 0, 8), AF.Exp, [dPV], [dEX])
    tt(SM[:], EX[:, 0:8], EX[:, 8:16], ALU.add, [dEX], [dEX])
    tt(SM[:], SM[:], EX[:, 16:24], ALU.add, [dEX], [dEX])
    tt(SM[:], SM[:], EX[:, 24:32], ALU.add, [dEX], [dEX])
    recip(SM[:], SM[:], [dEX], [dEX])
    for l in range(L):
        tt(EX[:, l * 8:(l + 1) * 8], EX[:, l * 8:(l + 1) * 8], SM[:], ALU.mult, [dEX], [dEX])
    for l in range(n_layers):
        if l == 0:
            memset(sv(0, 16, 8), 0.0, [dSV])
        elif l == 1:
            cp(sv(1, 16, 8), EX[:, 8:16], [dEX], [dSV])
        else:
            tt(sv(l, 16, 8), sv(l - 1, 16, 8), EX[:, l * 8:(l + 1) * 8], ALU.add, [dEX, dSV], [dSV])
    for l in range(n_layers):
        ts(sv(l, 16, 8), sv(l, 16, 8), 0.0, 1.0, ALU.max, ALU.min, [dSV], [dSV])
        ts(sv(l, 24, 8), sv(l, 16, 8), -1.0, 1.0, ALU.mult, ALU.add, [dSV], [dSV])
    P.barrier()

    GS = P.sb("GS", [128, 64], F32)
    dGS = Dep()

    def norm_mod(src, l, sq, which):
        gname, ksh, ksc = ("n1g", 0, 1) if which == 0 else ("n2g", 3, 4)
        if l >= 0:
            for c in range(8):
                stt(GS[:, which * 8 + c: which * 8 + c + 1], mod(l, ksc, c, sq), 1.0, pv(gname, l, c),
                    ALU.add, ALU.mult, [dMOD, dPV], [dGS])
        HT = scv(0, 4096, F32).rearrange("p (c t) -> p c t", c=8)
        SQ = [scv(16384 + i * 2048, 512, F32) for i in range(2)]
        RS = scv(20480, 512, F32)
        TM = [scv(22528 + i * 2048, 512, F32) for i in range(2)]
        dHT, dSQ, dRS, dTM = Dep(), [Dep(), Dep()], Dep(), [Dep(), Dep()]
        srcv = src.rearrange("(c p) t -> p c t", p=128)
        for tq in range(4):
            P.dma(sp, HT[:, :, :], srcv[:, :, tq * 512:(tq + 1) * 512], writes=[dHT])
            ps, pd = P.psum()
            for c in range(8):
                actf(SQ[c % 2][:], HT[:, c, :], AF.Square, [dHT], [dSQ[c % 2]])
                mm(ps[:, :], ONf, SQ[c % 2][:], c == 0, c == 7, [dSQ[c % 2], dC], [pd], inc=True)
            actf(RS[:], ps[:, :], AF.Sqrt, [pd], [dRS], bias=1e-6, scale=1.0 / D)
            recip(RS[:], RS[:], [dRS], [dRS])
            for c in range(8):
                tt(TM[c % 2][:], HT[:, c, :], RS[:], ALU.mult, [dHT, dRS], [dTM[c % 2]])
                if l >= 0:
                    actf(U[:, c, tq * 512:(tq + 1) * 512], TM[c % 2][:], AF.Identity, [dTM[c % 2], dGS, dMOD], [dU],
                         bias=mod(l, ksh, c, sq), scale=GS[:, which * 8 + c: which * 8 + c + 1])
                else:
                    ts(HT[:, c, :], TM[c % 2][:], PV[:, PV_L * L + c: PV_L * L + c + 1], None, ALU.mult, None,
                       [dTM[c % 2], dPV], [dHT])
            if l < 0:
                P.dma(sp, outT[sq].rearrange("(c p) t -> p c t", p=128)[:, :, tq * 512:(tq + 1) * 512], HT[:, :, :],
                      reads=[dHT])

    def gate_merge(l, n, first):
        SG = [scv(i * 2048, 512, F32) for i in range(2)]
        TP = [scv(4096 + i * 2048, 512, F32) for i in range(2)]
        dSG, dTP = [Dep(), Dep()], [Dep(), Dep()]
        k = 0
        for cg in range(2):
            sg_, sgd = load_slab(w_in[l], 0, 8, O_GATE + n * D + cg * 512, 512)
            sb_, sbd = load_slab(w_branch[l, n], 0, 4, cg * 512, 512)
            for c in range(4):
                cc = cg * 4 + c
                for tq in range(4):
                    tsl = slice(tq * 512, (tq + 1) * 512)
                    pg, pgd = P.psum()
                    for kc in range(8):
                        mm(pg[:, :], sg_[:, kc, c * 128:(c + 1) * 128], U[:, kc, tsl], kc == 0, kc == 7, [sgd, dU], [pgd])
                    py, pyd = P.psum()
                    for kc in range(4):
                        mm(py[:, :], sb_[:, kc, c * 128:(c + 1) * 128], Y[:, kc, tsl], kc == 0, kc == 3, [sbd, dY], [pyd])
                    i = k % 2
                    k += 1
                    actf(SG[i][:], pg[:, :], AF.Sigmoid, [pgd], [dSG[i]])
                    if first:
                        tt(M[:, cc, tsl], py[:, :], SG[i][:], ALU.mult, [pyd, dSG[i]], [dM])
                    else:
                        tt(TP[i][:], py[:, :], SG[i][:], ALU.mult, [pyd, dSG[i]], [dTP[i]])
                        tt(M[:, cc, tsl], M[:, cc, tsl], TP[i][:], ALU.add, [dTP[i], dM], [dM])

    def out_proj_residual(l, sq, src, dst):
        HT = scv(0, 4096, F32).rearrange("p (c t) -> p c t", c=8)
        dHT = Dep()
        s0, s0d = load_slab(w_out[l], 0, 8, 0, 512)
        s1, s1d = load_slab(w_out[l], 0, 8, 512, 512)
        srcv = src.rearrange("(c p) t -> p c t", p=128)
        dstv = dst.rearrange("(c p) t -> p c t", p=128)
        for tq in range(4):
            tsl = slice(tq * 512, (tq + 1) * 512)
            P.dma(sp, HT[:, :, :], srcv[:, :, tsl], writes=[dHT])
            for c2 in range(8):
                sl, sd = (s0, s0d) if c2 < 4 else (s1, s1d)
                ps, pd = P.psum()
                for kc in range(8):
                    mm(ps[:, :], sl[:, kc, (c2 % 4) * 128:(c2 % 4 + 1) * 128], M[:, kc, tsl], kc == 0, kc == 7, [sd, dM], [pd])
                stt(HT[:, c2, :], ps[:, :], mod(l, 2, c2, sq), HT[:, c2, :], ALU.mult, ALU.add, [pd, dHT, dMOD], [dHT])
            P.dma(sp, dstv[:, :, tsl], HT[:, :, :], reads=[dHT])

    def ffn(l, sq, hsrc):
        ZV = scv(0, 2050, F32)
        ZG = scv(8208, 2050, F32)
        CV = scv(16416, 2048, F32)
        CG = scv(24608, 2048, F32)
        dZV, dZG, dCV, dCG = Dep(), Dep(), Dep(), Dep()
        memset(ZV[:, 0:1], 0.0, [dZV])
        memset(ZV[:, 2049:2050], 0.0, [dZV])
        memset(ZG[:, 0:1], 0.0, [dZG])
        memset(ZG[:, 2049:2050], 0.0, [dZG])
        up = ffn_up[l]
        for j in range(22):
            sl, sd = load_slab(up, 0, 8, j * 128, 128, 0)
            load_slab(up, 0, 8, DFF + j * 128, 128, 128, new=False)
            for half, Z, dZ in ((0, ZV, dZV), (1, ZG, dZG)):
                for tq in range(4):
                    ps, pd = P.psum()
                    for kc in range(8):
                        mm(ps[:, :], sl[:, kc, half * 128:(half + 1) * 128], U[:, kc, tq * 512:(tq + 1) * 512],
                           kc == 0, kc == 7, [sd, dU], [pd])
                    cp(Z[:, 1 + tq * 512: 1 + (tq + 1) * 512], ps[:, :], [pd], [dZ], eng=act)
            for half, Z, dZ, C, dCx in ((0, ZV, dZV, CV, dCV), (1, ZG, dZG, CG, dCG)):
                jj = half * 22 + j
                cw = lambda tp: pv("ffn_cw", l, tp * 44 + jj)
                ts(C[:], Z[:, 1:2049], cw(1), pv("ffn_cb", l, jj), ALU.mult, ALU.add, [dZ, dPV], [dCx])
                stt(C[:], Z[:, 0:2048], cw(0), C[:], ALU.mult, ALU.add, [dZ, dCx], [dCx])
                stt(C[:], Z[:, 2:2050], cw(2), C[:], ALU.mult, ALU.add, [dZ, dCx], [dCx])
            actf(CG[:], CG[:], AF.Silu, [dCG], [dCG])
            tt(AFF[:, j, :], CV[:], CG[:], ALU.mult, [dCV, dCG], [dAFF])
        P.barrier()
        HT = scv(0, 2048, F32).rearrange("p (c t) -> p c t", c=4)
        dHT = Dep()
        hv = hsrc.rearrange("(c p) t -> p c t", p=128)
        for cg in range(2):
            sls = []
            for pi, (k0, nk) in enumerate(((0, 8), (8, 8), (16, 6))):
                sls.append(load_slab(ffn_down[l], k0, nk, cg * 512, 512) + (k0, nk))
            for tq in range(4):
                tsl = slice(tq * 512, (tq + 1) * 512)
                P.dma(sp, HT[:, :, :], hv[:, cg * 4:(cg + 1) * 4, tsl], writes=[dHT])
                for c2 in range(4):
                    ps, pd = P.psum()
                    for sl, sd, k0, nk in sls:
                        for kc in range(nk):
                            kk = k0 + kc
                            mm(ps[:, :], sl[:, kc, c2 * 128:(c2 + 1) * 128], AFF[:, kk, tsl], kk == 0, kk == 21,
                               [sd, dAFF], [pd])
                    stt(HT[:, c2, :], ps[:, :], mod(l, 5, cg * 4 + c2, sq), HT[:, c2, :], ALU.mult, ALU.add,
                        [pd, dHT, dMOD], [dHT])
                P.dma(sp, hv[:, cg * 4:(cg + 1) * 4, tsl], HT[:, :, :], reads=[dHT])

    def tk(t0, n, d):
        if d == 0:
            return slice(t0, t0 + n)
        a = T - 1 - t0
        b = a - n
        return slice(a, None if b < 0 else b, -1)

    def load_layer_small(l):
        P.dma(pool, W2s, rw_w2[l].rearrange("d p n -> p d n"), writes=[dLW])
        P.dma(pool, A2s, rw_a2[l].rearrange("d p n -> p d n"), writes=[dLW])
        P.dma(pool, G2s, rw_g2[l], writes=[dLW])
        P.dma(pool, WAs, lru_wa[l].rearrange("d j p n -> p d j n"), writes=[dLW])
        P.dma(pool, WXs, lru_wx[l].rearrange("d j p n -> p d j n"), writes=[dLW])

    def mixer_lru(l):
        XP = scv(0, 2051, F32)
        XC = scv(8208, 2048, F32)
        XCB = scv(16400, 2048, BF16)
        Ba = scv(20496, 2048, F32)
        Bm = scv(28688, 2048, F32)
        Bx, Bh, Bg, By = (ACC[:, i, :] for i in range(4))
        dXP, dXC, dXCB, dBa, dBm, dBx, dBh, dBg, dBy = (Dep() for _ in range(9))
        memset(XP[:, 0:1], 0.0, [dXP])
        memset(XP[:, 2049:2051], 0.0, [dXP])
        for j in range(4):
            linear_fm(w_in[l], 8, O_LRU + j * 128, 128, ufn, [dU], T,
                      lambda ps, pd, col, cw, t0, tn: cp(XP[:, 1 + t0:1 + t0 + tn], ps[:, 0:tn], [pd], [dXP], eng=act))
            linear_fm(w_in[l], 8, O_LRU + 512 + j * 128, 128, ufn, [dU], T,
                      lambda ps, pd, col, cw, t0, tn: actf(Bg[:, t0:t0 + tn], ps[:, 0:tn], AF.Gelu, [pd], [dBg]))
            cwv = lambda tp: pv("lru_cw", l, tp * 4 + j)
            ts(XC[:], XP[:, 0:T], cwv(0), pv("lru_cb", l, j), ALU.mult, ALU.add, [dXP, dPV], [dXC])
            for tp in range(1, 4):
                stt(XC[:], XP[:, tp:tp + T], cwv(tp), XC[:], ALU.mult, ALU.add, [dXP, dXC], [dXC])
            cp(XCB[:], XC[:], [dXC], [dXCB])
            for d in range(2):
                for tq in range(4):
                    tsl = slice(tq * 512, (tq + 1) * 512)
                    ps, pd = P.psum()
                    mm(ps[:, :], WAs[:, d, j, :], XCB[:, tsl], True, True, [dLW, dXCB], [pd])
                    actf(Ba[:, tsl], ps[:, :], AF.Sigmoid, [pd, dPV], [dBa], bias=pv("lru_ba", l, d * 4 + j))
                    ps, pd = P.psum()
                    mm(ps[:, :], WXs[:, d, j, :], XCB[:, tsl], True, True, [dLW, dXCB], [pd])
                    actf(Bx[:, tsl], ps[:, :], AF.Sigmoid, [pd, dPV], [dBx], bias=pv("lru_bx", l, d * 4 + j))
                actf(Bm[:], Ba[:], AF.Exp, [dBa, dSV], [dBm], scale=sv(l, 8 + d * 4 + j))
                actf(Ba[:], Ba[:], AF.Exp, [dBa, dSV], [dBa], scale=sv(l, d * 4 + j))
                ts(Bm[:], Bm[:], 1.0, -1.0, ALU.min, ALU.mult, [dBm], [dBm])
                actf(Bm[:], Bm[:], AF.Sqrt, [dBm], [dBm], bias=1.0)
                tt(Bx[:], Bx[:], XC[:], ALU.mult, [dBx, dXC], [dBx])
                tt(Bm[:], Bm[:], Bx[:], ALU.mult, [dBm, dBx], [dBm])
                if d == 0:
                    scan(By[:], Ba[:], Bm[:], 0.0, [dBa, dBm], [dBy])
                else:
                    scan(Bh[:, ::-1], Ba[:, ::-1], Bm[:, ::-1], 0.0, [dBa, dBm], [dBh])
                    tt(By[:], By[:], Bh[:], ALU.add, [dBy, dBh], [dBy])
            tt(Y[:, j, :], By[:], Bg[:], ALU.mult, [dBy, dBg], [dY])

    PADB = RR[:, 40960:45056]

    HN_DEPS = (Dep(), Dep())

    def head_rmsnorm_gate(l, gname, h, gate_mul):
        SQ = PADB[:, 0:1024].bitcast(F32)
        RS = PADB[:, 1024:2048].bitcast(F32)
        dSQ, dRS = HN_DEPS
        for tq in range(4):
            tsl = slice(tq * 512, (tq + 1) * 512)
            actf(SQ[:], ACC[:, h, tsl], AF.Square, [dACC], [dSQ])
            ps, pd = P.psum()
            mm(ps[:, :], ONf, SQ[:], True, True, [dSQ, dC], [pd])
            actf(RS[:], ps[:, :], AF.Sqrt, [pd], [dRS], bias=1e-6, scale=1.0 / 128)
            recip(RS[:], RS[:], [dRS], [dRS])
            tt(SQ[:], ACC[:, h, tsl], RS[:], ALU.mult, [dACC, dRS, dSQ], [dSQ])
            stt(Y[:, h, tsl], SQ[:], pv(gname, l, h), Y[:, h, tsl], ALU.mult, ALU.mult, [dSQ, dPV, dY], [dY])

    def mixer_mlstm(l):
        QF = scv(0, 4096, BF16).rearrange("p (c t) -> p c t", c=2)
        KF = scv(8192, 4096, BF16).rearrange("p (c t) -> p c t", c=2)
        WKV = scv(16384, 6144, BF16).rearrange("p (k n) -> p k n", k=8)
        WGm = scv(28672, 128, BF16).rearrange("p (k n) -> p k n", k=8)
        o = [28928]

        def al(n, dtype):
            v = scv(o[0], n, dtype)
            o[0] += (n * (4 if dtype is F32 else 2) + 3) // 4 * 4
            return v
        KVT = al(768, BF16)
        IG, NLF, NEGC, KS = al(4, F32), al(4, F32), al(4, F32), al(4, F32)
        NLFB, DT, RD, HO = al(128, F32), al(128, F32), al(128, F32), al(128, F32)
        PW, QP = al(128, BF16), al(128, BF16)
        KTx = [al(128, BF16), al(128, BF16)]
        EBE = al(1, F32)
        Cst = al(258, F32).rearrange("p (c n) -> p c n", c=2)
        CBf = al(256, BF16).rearrange("p (c n) -> p c n", c=2)
        NBb = al(256, BF16).rearrange("p (c n) -> p c n", c=2)
        dQF, dKF, dWKV, dKVT, dG, dNLFB, dDT, dRD, dHO, dPW, dQP, dKT, dEBE, dCst, dCB = (Dep() for _ in range(15))
        linear_fm(w_in[l], 8, O_ML, 256, ufn, [dU], T,
                  lambda ps, pd, col, cw, t0, tn: ts(QF[:, col // 128, t0:t0 + tn], ps[:, 0:tn], 0.125, None, ALU.mult, None, [pd], [dQF]))
        linear_fm(w_in[l], 8, O_ML + 256, 256, ufn, [dU], T,
                  lambda ps, pd, col, cw, t0, tn: cp(KF[:, col // 128, t0:t0 + tn], ps[:, 0:tn], [pd], [dKF], eng=act))
        linear_fm(w_in[l], 8, O_ML + 1024, 512, ufn, [dU], T,
                  lambda ps, pd, col, cw, t0, tn: actf(Y[:, col // 128, t0:t0 + tn], ps[:, 0:tn], AF.Sigmoid, [pd], [dY]))
        wv = w_in[l]
        P.dma(pool, WKV[:, :, :], wv[:, O_ML + 256:O_ML + 1024].rearrange("(kc p) n -> p kc n", p=128), writes=[dWKV])
        P.dma(pool, WGm[:, :, :], wv[:, O_ML + 1536:O_ML + 1552].rearrange("(kc p) n -> p kc n", p=128), writes=[dWKV])
        memset(KTx[0][:], 0.0, [dKT])
        memset(KTx[1][:], 0.0, [dKT])
        GB = pv("ml_gb", l, 0, 16)
        URt = al(1024, BF16).rearrange("p (k n) -> p k n", k=8)
        dUR = Dep()
        TMPR = PADB[:, 0:4096].rearrange("p (c t) -> p c t", c=2)
        dTR = Dep()
        for d in range(2):
            if d == 1:
                for BUF, dB in ((QF, dQF), (KF, dKF)):
                    cp(TMPR[:, :, :], BUF[:, :, ::-1], [dB], [dTR])
                    cp(BUF[:, :, :], TMPR[:, :, :], [dTR], [dB])
            memset(Cst[:, :, :], 0.0, [dCst])
            memset(CBf[:, :, :], 0.0, [dCB])
            memset(NBb[:, :, :], 0.0, [dCB])
            for tau in range(16):
                tsl = tk(tau * 128, 128, d)
                pt = slice(tau * 128, tau * 128 + 128)
                if d == 1:
                    cp(URt[:, :, :], U[:, :, tsl], [dU], [dUR])
                    uf = lambda kc: URt[:, kc, :]
                else:
                    uf = lambda kc: U[:, kc, pt]
                for c0, cn in ((0, 512), (512, 256)):
                    ps, pd = P.psum()
                    for kc in range(8):
                        mm(ps[:, 0:cn], uf(kc), WKV[:, kc, c0:c0 + cn], kc == 0, kc == 7, [dU, dWKV, dUR], [pd])
                    cp(KVT[:, c0:c0 + cn], ps[:, 0:cn], [pd], [dKVT], eng=act)
                psg, pgd = P.psum()
                for kc in range(8):
                    mm(psg[:, 0:16], uf(kc), WGm[:, kc, :], kc == 0, kc == 7, [dU, dWKV, dUR], [pgd])
                tt(IG[:], psg[:, d * 4:d * 4 + 4], GB[:, d * 4:d * 4 + 4], ALU.add, [pgd, dPV], [dG])
                tt(NLF[:], psg[:, 8 + d * 4:12 + d * 4], GB[:, 8 + d * 4:12 + d * 4], ALU.add, [pgd, dPV], [dG])
                actf(NLF[:], NLF[:], AF.Exp, [dG], [dG], scale=-1.0)
                actf(NLF[:], NLF[:], AF.Ln, [dG], [dG], bias=1.0)
                psb, pbd = P.psum()
                mm(psb[:, 0:4], LEf, NLF[:], True, True, [dC, dG], [pbd])
                tt(NEGC[:], psb[:, 0:4], IG[:], ALU.add, [pbd, dG], [dG])
                for h in range(4):
                    hp = slice((h % 2) * 64, (h % 2) * 64 + 64)
                    hc = h // 2
                    ts(NLFB[:], ONf, NLF[:, h:h + 1], None, ALU.mult, None, [dC, dG], [dNLFB])
                    bb, bbd = P.psum()
                    mm(bb[:, 0:128], NLFB[:], LEf, True, True, [dNLFB, dC], [bbd])
                    actf(EBE[:], bb[:, 127:128], AF.Exp, [bbd], [dEBE], scale=-1.0)
                    actf(KS[:, h:h + 1], bb[:, 127:128], AF.Exp, [bbd, dG], [dG], scale=-1.0, bias=NEGC[:, h:h + 1])
                    actf(DT[:], bb[:, 0:128], AF.Exp, [bbd, dG], [dDT], scale=-1.0, bias=NEGC[:, h:h + 1])
                    tt(DT[:], DT[:], LEf, ALU.mult, [dDT, dC], [dDT])
                    st, std = P.psum()
                    mm(st[:, 0:128], KF[hp, hc, pt], QF[hp, hc, pt], True, True, [dKF, dQF], [std])
                    tt(PW[:], st[:, 0:128], DT[:], ALU.mult, [std, dDT], [dPW])
                    actf(RD[hp, :], bb[hp, 0:128], AF.Exp, [bbd], [dRD], scale=-1.0)
                    tt(QP[hp, :], QF[hp, hc, pt], RD[hp, :], ALU.mult, [dQF, dRD], [dQP])
                    nu, nud = P.psum()
                    mm(nu[:, 0:128], KVT[:, 256 + h * 128:384 + h * 128], PW[:], True, False, [dKVT, dPW], [nud])
                    mm(nu[:, 0:128], CBf[hp, hc, :], QP[hp, :], False, True, [dCB, dQP], [nud])
                    de, ded = P.psum()
                    mm(de[:, 0:128], ONb, PW[:], True, False, [dC, dPW], [ded])
                    mm(de[:, 0:128], NBb[hp, hc, :], QP[hp, :], False, True, [dCB, dQP], [ded])
                    actf(RD[:], de[:, 0:128], AF.Abs, [ded], [dRD])
                    ts(RD[:], RD[:], 1.0, None, ALU.max, None, [dRD], [dRD])
                    recip(RD[:], RD[:], [dRD], [dRD])
                    if d == 0:
                        tt(ACC[:, h, tsl], nu[:, 0:128], RD[:], ALU.mult, [nud, dRD], [dACC])
                    else:
                        tt(HO[:], nu[:, 0:128], RD[:], ALU.mult, [nud, dRD], [dHO])
                        tt(ACC[:, h, tsl], ACC[:, h, tsl], HO[:], ALU.add, [dHO, dACC], [dACC])
                    kt = KTx[h % 2]
                    ts(kt[:, hp], KVT[:, h * 64:h * 64 + 64], KS[:, h:h + 1], None, ALU.mult, None, [dKVT, dG], [dKT])
                    pc, pcd = P.psum()
                    mm(pc[:, 0:128], kt[:], KVT[:, 256 + h * 128:384 + h * 128], True, True, [dKT, dKVT], [pcd], inc=False)
                    mm(pc[:, 128:129], kt[:], ONb[:, 0:1], True, True, [dKT, dC], [pcd], inc=True)
                    stt(Cst[hp, hc, :], Cst[hp, hc, :], EBE[hp, :], pc[hp, 0:129], ALU.mult, ALU.add, [dCst, dEBE, pcd], [dCst])
                    cp(CBf[hp, hc, :], Cst[hp, hc, 0:128], [dCst], [dCB], eng=act)
                    ts(NBb[hp, hc, :], ONf[hp, :], Cst[hp, hc, 128:129], None, ALU.mult, None, [dCst, dC], [dCB])
        for h in range(4):
            head_rmsnorm_gate(l, "ml_norm", h, True)

    def mixer_hgrn2(l):
        LF = scv(0, 2048, F32)
        G = scv(8192, 2048, F32)
        Kb = scv(16384, 2048, BF16)
        QS = scv(20480, 2048, BF16)
        WI = scv(24576, 1024, BF16).rearrange("p (k n) -> p k n", k=8)
        o = [26624]

        def al(n, dtype):
            v = scv(o[0], n, dtype)
            o[0] += (n * (4 if dtype is F32 else 2) + 3) // 4 * 4
            return v
        GLt, EXt, Sst = (al(128, F32) for _ in range(3))
        QTt, QHt, ATT, VT, KHT, SBb = (al(128, BF16) for _ in range(6))
        TM = al(9 * 128, F32).rearrange("p (k n) -> p k n", k=9)
        KTLa = al(9 * 128, BF16).rearrange("p (k n) -> p k n", k=9)
        EGE = al(1, F32)
        URt = al(1024, BF16).rearrange("p (k n) -> p k n", k=8)
        dUR = Dep()
        dLF, dG, dKb, dQS, dWI, dGL, dEX, dTMP, dEXk, dS, dQT, dQH, dKH, dATT, dVT, dKHT, dSB, dKTL, dEGE = (Dep() for _ in range(19))
        dKTLs = [Dep() for _ in range(8)]
        for h in range(4):
            linear_fm(w_in[l], 8, O_HG + h * 128, 128, ufn, [dU], T,
                      lambda ps, pd, col, cw, t0, tn: actf(QS[:, t0:t0 + tn], ps[:, 0:tn], AF.Silu, [pd], [dQS]))
            linear_fm(w_in[l], 8, O_HG + 2048 + h * 128, 128, ufn, [dU], T,
                      lambda ps, pd, col, cw, t0, tn: actf(Y[:, h, t0:t0 + tn], ps[:, 0:tn], AF.Silu, [pd], [dY]))
            P.dma(pool, WI[:, :, :], w_in[l][:, O_HG + 1536 + h * 128:O_HG + 1664 + h * 128].rearrange("(kc p) n -> p kc n", p=128),
                  writes=[dWI])
            for d in range(2):
                linear_fm(w_in[l], 8, O_HG + 512 + d * 512 + h * 128, 128, ufn, [dU], T,
                          lambda ps, pd, col, cw, t0, tn: actf(LF[:, tk(t0, tn, d)], ps[:, 0:tn], AF.Sigmoid, [pd], [dLF]))
                ts(LF[:], LF[:], sv(l, 24 + d * 4 + h), sv(l, 16 + d * 4 + h), ALU.mult, ALU.add, [dLF, dSV], [dLF])
                ts(Kb[:], LF[:], -1.0, 1.0, ALU.mult, ALU.add, [dLF], [dKb])
                actf(LF[:], LF[:], AF.Ln, [dLF], [dLF])
                for tau in range(16):
                    pt = slice(tau * 128, tau * 128 + 128)
                    scan(G[:, pt], ONf, LF[:, pt], 0.0, [dLF, dC], [dG])
                memset(Sst[:], 0.0, [dS])
                memset(SBb[:], 0.0, [dSB])
                for tau in range(16):
                    t0 = tau * 128
                    pt = slice(t0, t0 + 128)
                    nt = tk(t0, 128, d)
                    Gt = G[:, pt]
                    Gt3 = Gt.rearrange("p (b i) -> p b i", b=8)
                    GL3 = GLt.rearrange("p (b i) -> p b i", b=8)
                    vp, vpd = P.psum()
                    if d == 1:
                        cp(URt[:, :, :], U[:, :, nt], [dU], [dUR])
                    for kc in range(8):
                        mm(vp[:, 0:128], URt[:, kc, :] if d == 1 else U[:, kc, pt], WI[:, kc, :], kc == 0, kc == 7,
                           [dU, dWI, dUR], [vpd])
                    cp(GL3[:, 0, :], Gt3[:, 0, :], [dG], [dGL], eng=pool)
                    tt(GL3[:, 1:8, :], Gt3[:, 1:8, :], Gt3[:, 0:7, 15:16].to_broadcast([128, 7, 16]), ALU.subtract, [dG], [dGL])
                    cp(TM[:, 0, :], Gt, [dG], [dTMP], eng=pool)
                    tt(TM[:, 1:9, :], Gt.unsqueeze(1).to_broadcast([128, 8, 128]),
                       Gt3[:, 0:8, 15:16].to_broadcast([128, 8, 128]), ALU.subtract, [dG], [dTMP])
                    actf(TM[:, :, :], TM[:, :, :], AF.Exp, [dTMP], [dTMP], scale=-1.0)
                    actf(EXt[:], GLt[:], AF.Exp, [dGL], [dEX])
                    stt(KTLa[:, :, :], TM[:, :, :], 1e26, Kb[:, pt].unsqueeze(1).to_broadcast([128, 9, 128]),
                        ALU.min, ALU.mult, [dTMP, dKb], [dKTL])
                    tt(QTt[:], QS[:, nt], EXt[:], ALU.mult, [dQS, dEX], [dQT])
                    actf(EXt[:], Gt, AF.Exp, [dG, dQT], [dEX])
                    actf(EGE[:], Gt[:, 127:128], AF.Exp, [dG], [dEGE])
                    cp(VT[:], vp[:, 0:128], [vpd], [dVT], eng=act)
                    tt(QHt[:], QS[:, nt], EXt[:], ALU.mult, [dQS, dEX], [dQH])
                    at, atd = P.psum()
                    for I in range(8):
                        mm(at[:, 16 * I:16 * I + 16], KTLa[:, I, :], QTt[:, 16 * I:16 * I + 16], True, True, [dKTL, dQT], [atd],
                           inc=(I == 7))
                    KH = KTLa[:, 8, :]
                    tp_, tpd = P.psum()
                    mm(tp_[:, 0:128], KH, IDb, True, True, [dKTL, dC], [tpd])
                    tt(ATT[:], at[:, 0:128], LEf, ALU.mult, [atd, dC], [dATT])
                    cp(KHT[:], tp_[:, 0:128], [tpd], [dKHT], eng=act)
                    op_, opd = P.psum()
                    mm(op_[:, 0:128], VT[:], ATT[:], True, False, [dVT, dATT], [opd])
                    mm(op_[:, 0:128], SBb[:], QHt[:], False, True, [dSB, dQH], [opd])
                    sp_, spd = P.psum()
                    mm(sp_[:, 0:128], KHT[:], VT[:], True, True, [dKHT, dVT], [spd])
                    stt(Sst[:], Sst[:], EGE[:], sp_[:, 0:128], ALU.mult, ALU.add, [dS, dEGE, spd], [dS])
                    cp(SBb[:], Sst[:], [dS], [dSB], eng=act)
                    if d == 0:
                        cp(ACC[:, h, nt], op_[:, 0:128], [opd], [dACC], eng=act)
                    else:
                        tt(ACC[:, h, nt], ACC[:, h, nt], op_[:, 0:128], ALU.add, [opd, dACC], [dACC])
            head_rmsnorm_gate(l, "hg_norm", h, True)

    MUV = P.sb("MUV", [128, 40], F32)
    dMUV = Dep()

    def mixer_rwkv(l):
        deps = {}

        def dp(n):
            if n not in deps:
                deps[n] = Dep()
            return deps[n]
        P0 = scv(0, 2050, F32)
        PF = scv(8208, 2048, F32)
        G = scv(16400, 2048, F32)
        Bb = scv(24592, 2048, BF16)
        KT = [scv(28688, 2048, BF16), scv(32784, 2048, BF16)]
        TXW, XA, SXG, Rr, Vv, KK, Kraw, GG = (M[:, i, :] for i in range(8))
        ts(MUV[:, 0:14], pv("rw_mu", l, 0, 14), -1.0, 1.0, ALU.mult, ALU.add, [dPV], [dMUV])
        ts(MUV[:, 14:28], pv("rw_mu", l, 0, 14), 0.5, None, ALU.mult, None, [dPV], [dMUV])
        ts(MUV[:, 28:32], pv("rw_ka", l, 0, 4), -1.0, 1.0, ALU.mult, ALU.add, [dPV], [dMUV])
        ts(MUV[:, 32:36], pv("rw_rk", l, 0, 4), 0.5, None, ALU.mult, None, [dPV], [dMUV])

        def shifted(ci):
            memset(P0[:, 0:1], 0.0, [dp("P0")])
            memset(P0[:, 2049:2050], 0.0, [dp("P0")])
            linear_fm(w_in[l], 8, O_RW + ci * 128, 128, ufn, [dU], T,
                      lambda ps, pd, col, cw, t0, tn: cp(P0[:, 1 + t0:1 + t0 + tn], ps[:, 0:tn], [pd], [dp("P0")], eng=act))
            tt(PF[:], P0[:, 0:T], P0[:, 2:T + 2], ALU.add, [dp("P0")], [dp("PF")])
            ts(PF[:], PF[:], MUV[:, 14 + ci:15 + ci], None, ALU.mult, None, [dp("PF"), dMUV], [dp("PF")])
            stt(PF[:], P0[:, 1:T + 1], MUV[:, ci:ci + 1], PF[:], ALU.mult, ALU.add, [dp("P0"), dp("PF"), dMUV], [dp("PF")])

        shifted(12)
        actf(TXW[0:64, :], PF[0:64, :], AF.Tanh, [dp("PF")], [dp("TXW")])
        cp(XA[64:128, :], PF[64:128, :], [dp("PF")], [dp("XA")])
        shifted(13)
        actf(SXG[:, :], PF[:], AF.Sigmoid, [dp("PF")], [dp("SXG")])
        SQt = P0[:, 0:512]
        RSt = P0[:, 512:1024]
        T1 = P0[:, 1024:1536]
        HP = [slice(0, 64), slice(64, 128)]
        for j in range(rw_limit[0]):
            P.barrier()
            regions = [(SC, 18440, 20480), (RR, 40960, 45056)] + [(RR, k * 4096, (k + 1) * 4096) for k in range(4) if k != j]
            ri, ro = [0], [regions[0][1]]

            def al(n, dtype=BF16):
                ne = n * (2 if dtype is F32 else 1)
                ne = (ne + 1) // 2 * 2
                while ro[0] + ne > regions[ri[0]][2]:
                    ri[0] += 1
                    ro[0] = regions[ri[0]][1]
                t_, a_ = regions[ri[0]][0], ro[0]
                ro[0] += ne
                v = t_[:, a_:a_ + ne]
                return v.bitcast(F32) if dtype is F32 else v

            def mkset(sid):
                B = {"sid": sid}
                for nm in ("AT", "RT", "BT", "KTt", "BH", "KHh", "VTf", "ATk", "VTk", "BHk", "KHk", "PT"):
                    B[nm] = al(128)
                for nm in ("EXa", "EXb", "EXc", "EXd"):
                    B[nm] = al(128, F32)
                for nm in ("NT0", "WS", "AAK", "ARB", "ARK", "Xb", "Tb", "TTb", "UT"):
                    B[nm] = [al(128), al(128)]
                B["MT"] = [al(128, F32), al(128, F32)]
                B["Sbs"] = al(128)
                B["GC"] = al(64, F32)
                return B
            NSET = 3
            sets = [mkset(i) for i in range(NSET)]
            Sf = al(128, F32)
            for B in sets:
                for b_ in (B["UT"][0], B["UT"][1], B["MT"][0], B["MT"][1], B["Sbs"]):
                    memset(b_[:], 0.0, [dp("misc%d" % B["sid"])], eng=pool)
            shifted(j)
            cp(Rr[:, :], PF[:], [dp("PF")], [dp("R")], eng=act)
            shifted(8 + j)
            cp(Vv[:, :], PF[:], [dp("PF")], [dp("V")], eng=act)
            shifted(4 + j)
            cp(Kraw[:, :], PF[:], [dp("PF")], [dp("Kraw")], eng=act)
            ts(PF[:], PF[:], pv("rw_kk", l, j), None, ALU.mult, None, [dp("PF"), dPV], [dp("PF")])
            for tq in range(4):
                tsl = slice(tq * 512, (tq + 1) * 512)
                actf(SQt, PF[:, tsl], AF.Square, [dp("PF")], [dp("P0")])
                ps, pd = P.psum()
                mm(ps[:, :], BKf, SQt, True, True, [dC, dp("P0")], [pd])
                ts(RSt, ps[:, :], 1e-24, None, ALU.max, None, [pd], [dp("P0")])
                actf(RSt, RSt, AF.Sqrt, [dp("P0")], [dp("P0")])
                recip(RSt, RSt, [dp("P0")], [dp("P0")])
                tt(KK[:, tsl], PF[:, tsl], RSt, ALU.mult, [dp("PF"), dp("P0")], [dp("KK")])
                ps, pd = P.psum()
                mm(ps[:, :], G2s[:, j * 128:(j + 1) * 128], SXG[:, tsl], True, True, [dLW, dp("SXG")], [pd])
                cp(GG[:, tsl], ps[:, :], [pd], [dp("GG")], eng=act)
            AS = P0[:, 0:2048]
            for d in range(rw_limit[1]):
                for tq in range(4):
                    tsl = slice(tq * 512, (tq + 1) * 512)
                    ps, pd = P.psum()
                    mm(ps[:, :], W2s[:, d, j * 128:(j + 1) * 128], TXW[0:64, tsl], True, True, [dLW, dp("TXW")], [pd])
                    actf(PF[:, tk(tq * 512, 512, d)], ps[:, :], AF.Sigmoid, [pd, dPV], [dp("PF")], bias=pv("rw_w0", l, d * 4 + j))
                    ps, pd = P.psum()
                    mm(ps[:, :], A2s[:, d, j * 128:(j + 1) * 128], XA[64:128, tsl], True, True, [dLW, dp("XA")], [pd])
                    actf(AS[:, tk(tq * 512, 512, d)], ps[:, :], AF.Sigmoid, [pd, dPV], [dp("P0")], bias=pv("rw_a0", l, d * 4 + j))
                ts(PF[:], PF[:], -0.6065306597, None, ALU.mult, None, [dp("PF")], [dp("PF")])
                for tau in range(16):
                    pt = slice(tau * 128, tau * 128 + 128)
                    scan(G[:, pt], ONf, PF[:, pt], 0.0, [dp("PF"), dC], [dp("G")])
                rv = slice(None, None, -1) if d == 1 else slice(None)
                dKT = dp("KT%d" % d)
                ts(KT[d][:], AS, pv("rw_ka", l, j), MUV[:, 28 + j:29 + j], ALU.mult, ALU.add, [dp("P0"), dPV, dMUV], [dKT])
                tt(KT[d][:], KT[d][:], Kraw[:, rv], ALU.mult, [dKT, dp("Kraw")], [dKT])
                tt(Bb[:], KK[:, rv], AS, ALU.mult, [dp("KK"), dp("P0")], [dp("Bb")])
                memset(Sf[:], 0.0, [dp("Sf0"), dp("Sf1")])

                def D(B, nm, e=None):
                    return dp("%s%s_%d" % (nm, "" if e is None else str(e), B["sid"]))

                def prep(tau, B):
                    pt = slice(tau * 128, tau * 128 + 128)
                    nt = tk(tau * 128, 128, d)
                    Gt = G[:, pt]
                    actf(B["EXa"][:], Gt, AF.Exp, [dp("G")], [D(B, "EXa")])
                    tt(B["RT"][:], Rr[:, nt], B["EXa"][:], ALU.mult, [dp("R"), D(B, "EXa")], [D(B, "RT")])
                    cp(B["EXb"][:, 1:128], B["EXa"][:, 0:127], [D(B, "EXa")], [D(B, "EXb")], eng=pool)
                    memset(B["EXb"][:, 0:1], 1.0, [D(B, "EXb")], eng=pool)
                    stt(B["AT"][:], KK[:, nt], -1.0, B["EXb"][:], ALU.mult, ALU.mult, [dp("KK"), D(B, "EXb")], [D(B, "AT")])
                    actf(B["EXc"][:], Gt, AF.Exp, [dp("G")], [D(B, "EXc")], scale=-1.0)
                    tt(B["BT"][:], Bb[:, pt], B["EXc"][:], ALU.mult, [dp("Bb"), D(B, "EXc")], [D(B, "BT")])
                    tt(B["KTt"][:], KT[d][:, pt], B["EXc"][:], ALU.mult, [dKT, D(B, "EXc")], [D(B, "KTt")], eng=pool)
                    actf(B["EXd"][:], Gt, AF.Exp, [dp("G")], [D(B, "EXd")], scale=-1.0, bias=Gt[:, 127:128])
                    tt(B["BH"][:], Bb[:, pt], B["EXd"][:], ALU.mult, [dp("Bb"), D(B, "EXd")], [D(B, "BH")])
                    tt(B["KHh"][:], KT[d][:, pt], B["EXd"][:], ALU.mult, [dKT, D(B, "EXd")], [D(B, "KHh")], eng=pool)
                    cp(B["VTf"][:], Vv[:, nt], [dp("V")], [D(B, "VTf")], eng=pool)
                    for sn, dn_ in (("AT", "ATk"), ("VTf", "VTk"), ("BH", "BHk"), ("KHh", "KHk")):
                        ps, pd = P.psum()
                        mm(ps[:, 0:128], B[sn][:], IDb, True, True, [D(B, sn), dC], [pd])
                        cp(B[dn_][:], ps[:, 0:128], [pd], [D(B, dn_)], eng=act)

                def score(B, lh, ln, rh, rn, mask, dst, dn_):
                    ps, pd = P.psum()
                    mm(ps[:, 0:128], lh, rh, True, True, [D(B, ln), D(B, rn)], [pd])
                    tt(dst[:], ps[:, 0:128], mask, ALU.mult, [pd, dC], [dn_])

                def st_scores(tau, B, e):
                    hp = HP[e]
                    AT, BT, KTt, RT = B["AT"], B["BT"], B["KTt"], B["RT"]
                    score(B, BT[hp, :], "BT", AT[hp, :], "AT", LTf, B["NT0"][e], D(B, "NT", e))
                    score(B, AT[hp, :], "AT", BT[hp, :], "BT", GTf, B["WS"][e], D(B, "WS", e))
                    score(B, KTt[hp, :], "KTt", AT[hp, :], "AT", LTf, B["AAK"][e], D(B, "AAK", e))
                    score(B, BT[hp, :], "BT", RT[hp, :], "RT", LEf, B["ARB"][e], D(B, "ARB", e))
                    score(B, KTt[hp, :], "KTt", RT[hp, :], "RT", LEf, B["ARK"][e], D(B, "ARK", e))

                def st_x0(tau, B, e):
                    hp, oc = HP[e], HP[1 - e]
                    ps, pd = P.psum()
                    mm(ps[:, 0:64], B["AAK"][e][:], B["VTk"][:, hp], True, True, [D(B, "AAK", e), D(B, "VTk")], [pd])
                    cp(B["Xb"][e][:, oc], ps[:, 0:64], [pd], [D(B, "Xb", e)], eng=act)
                    cp(B["Xb"][e][:, hp], B["ATk"][:, hp], [D(B, "ATk")], [D(B, "Xb", e)], eng=pool)

                def st_lvl0(tau, B, e):
                    Tb, TTb = B["Tb"][e], B["TTb"][e]
                    tt(Tb[:], B["WS"][e][:], LMb[0], ALU.mult, [D(B, "WS", e), dC], [D(B, "T", e)])
                    tt(Tb[:], Tb[:], IDb, ALU.add, [D(B, "T", e), dC], [D(B, "T", e)])
                    tt(TTb[:], B["NT0"][e][:], LMTb[0], ALU.mult, [D(B, "NT", e), dC], [D(B, "TT", e)], eng=pool)
                    tt(TTb[:], TTb[:], IDb, ALU.add, [D(B, "TT", e), dC], [D(B, "TT", e)], eng=pool)

                def mk_lvlA(k):
                    def st(tau, B, e):
                        ps, pd = P.psum()
                        mm(ps[:, 0:128], B["NT0"][e][:], B["Tb"][e][:], True, True, [D(B, "NT", e), D(B, "T", e)], [pd])
                        tt(B["WS"][e][:], ps[:, 0:128], LMb[k], ALU.mult, [pd, dC], [D(B, "WS", e)])
                    return st

                def mk_lvlB(k):
                    def st(tau, B, e):
                        Tb, TTb, WS = B["Tb"][e], B["TTb"][e], B["WS"][e]
                        dT, dTT, dWS = D(B, "T", e), D(B, "TT", e), D(B, "WS", e)
                        pz, pzd = P.psum()
                        mm(pz[:, 0:128], IDb, Tb[:], True, False, [dC, dT], [pzd])
                        mm(pz[:, 0:128], TTb[:], WS[:], False, True, [dTT, dWS], [pzd])
                        pt_, ptd = P.psum()
                        mm(pt_[:, 0:128], WS[:], TTb[:], True, True, [dTT, dWS], [ptd])
                        cp(Tb[:], pz[:, 0:128], [pzd], [dT], eng=act)
                        tt(TTb[:], TTb[:], pt_[:, 0:128], ALU.add, [ptd, dTT], [dTT])
                    return st

                def st_apply(tau, B, e):
                    ps, pd = P.psum()
                    mm(ps[:, 0:128], B["TTb"][e][:], B["Xb"][e][:], True, True, [D(B, "TT", e), D(B, "Xb", e)], [pd])
                    cp(B["Xb"][e][:], ps[:, 0:128], [pd], [D(B, "Xb", e)], eng=act)

                def st_mt(tau, B, e):
                    hp = HP[e]
                    ps, pd = P.psum()
                    mm(ps[:, 0:64], B["Xb"][e][:], B["BHk"][:, hp], True, True, [D(B, "Xb", e), D(B, "BHk")], [pd])
                    stt(B["MT"][e][hp, hp], IDf[hp, hp], B["EXa"][hp, 127:128], ps[hp, 0:64], ALU.mult, ALU.add,
                        [pd, dC, D(B, "EXa")], [D(B, "MT", e)])

                def st_gc(tau, B, e):
                    hp, oc = HP[e], HP[1 - e]
                    ps, pd = P.psum()
                    mm(ps[:, 0:64], B["BHk"][:], B["Xb"][e][:, oc], True, False, [D(B, "Xb", e), D(B, "BHk")], [pd])
                    mm(ps[:, 0:64], B["KHk"][:], B["VTk"][:, hp], False, True, [D(B, "KHk"), D(B, "VTk")], [pd])
                    cp(B["GC"][hp, :], ps[hp, 0:64], [pd], [D(B, "GC", e)], eng=act)

                def st_pt(tau, B, e):
                    hp = HP[e]
                    ps, pd = P.psum()
                    mm(ps[:, 0:128], B["Xb"][e][:], IDb, True, True, [D(B, "Xb", e), dC], [pd])
                    cp(B["PT"][hp, :], ps[hp, 0:128], [pd], [D(B, "PT", e)], eng=act)

                def st_u(tau, B, e):
                    hp, oc = HP[e], HP[1 - e]
                    ps, pd = P.psum()
                    mm(ps[:, 0:64], B["PT"][hp, :], B["Sbs"][hp, hp], True, True, [D(B, "PT", e), D(B, "Sbs", e)], [pd])
                    tt(B["UT"][e][:, hp], ps[:, 0:64], B["Xb"][e][:, oc], ALU.add, [pd, D(B, "Xb", e)], [D(B, "UT", e)])

                def st_y(tau, B, e):
                    hp = HP[e]
                    nt = tk(tau * 128, 128, d)
                    ps, pd = P.psum()
                    mm(ps[:, 0:128], B["Sbs"][hp, :], B["RT"][hp, :], True, False, [D(B, "Sbs", e), D(B, "RT")], [pd])
                    mm(ps[:, 0:128], B["UT"][e][:], B["ARB"][e][:], False, False, [D(B, "UT", e), D(B, "ARB", e)], [pd])
                    mm(ps[:, 0:128], B["VTk"][:], B["ARK"][e][:], False, True, [D(B, "VTk"), D(B, "ARK", e)], [pd])
                    if d == 0:
                        cp(ACC[hp, j, nt], ps[hp, 0:128], [pd], [dACC], eng=act)
                    else:
                        tt(ACC[hp, j, nt], ACC[hp, j, nt], ps[hp, 0:128], ALU.add, [pd, dACC], [dACC])

                def st_chain(tau, B, e):
                    hp = HP[e]
                    cp(B["Sbs"][hp, hp], Sf[hp, hp], [dp("Sf%d" % e)], [D(B, "Sbs", e)], eng=act)
                    ps, pd = P.psum()
                    mm(ps[:, 0:64], B["MT"][e][hp, :], Sf[hp, hp], True, True, [D(B, "MT", e), dp("Sf%d" % e)], [pd])
                    tt(Sf[hp, hp], ps[hp, 0:64], B["GC"][hp, :], ALU.add, [pd, D(B, "GC", e), dp("Sf%d" % e)], [dp("Sf%d" % e)])

                indep = [st_scores, st_x0, st_lvl0]
                for k in range(1, 7):
                    indep += [mk_lvlA(k), mk_lvlB(k)]
                indep += [st_apply, st_mt, st_gc, st_pt]
                ntile = rw_limit[2]
                for tau0 in range(0, ntile, NSET):
                    ctx = [(tau0 + i, sets[i]) for i in range(min(NSET, ntile - tau0))]
                    for tau, B in ctx:
                        prep(tau, B)
                    for stg in indep:
                        for tau, B in ctx:
                            for e in range(2):
                                stg(tau, B, e)
                    for tau, B in ctx:
                        for e in range(2):
                            st_chain(tau, B, e)
                    for stg in (st_u, st_y):
                        for tau, B in ctx:
                            for e in range(2):
                                stg(tau, B, e)
            P.barrier()
            for tq in range(4):
                tsl = slice(tq * 512, (tq + 1) * 512)
                ps, pd = P.psum()
                mm(ps[:, :], BKf, ACC[:, j, tsl], True, True, [dC, dACC], [pd])
                stt(SQt, ps[:, :], -1.0 / 64, ACC[:, j, tsl], ALU.mult, ALU.add, [pd, dACC], [dp("P0")])
                actf(RSt, SQt, AF.Square, [dp("P0")], [dp("P0")])
                ps, pd = P.psum()
                mm(ps[:, :], BKf, RSt, True, True, [dC, dp("P0")], [pd])
                actf(RSt, ps[:, :], AF.Sqrt, [pd], [dp("P0")], bias=64e-5, scale=1.0 / 64)
                recip(RSt, RSt, [dp("P0")], [dp("P0")])
                tt(SQt, SQt, RSt, ALU.mult, [dp("P0")], [dp("P0")])
                ts(SQt, SQt, pv("rw_lnw", l, j), pv("rw_lnb", l, j), ALU.mult, ALU.add, [dp("P0"), dPV], [dp("P0")])
                tt(T1, KT[0][:, tsl], KT[1][:, tk(tq * 512, 512, 1)], ALU.add, [dp("KT0"), dp("KT1")], [dp("P0")])
                tt(T1, T1, Rr[:, tsl], ALU.mult, [dp("P0"), dp("R")], [dp("P0")])
                ts(T1, T1, MUV[:, 32 + j:33 + j], None, ALU.mult, None, [dp("P0"), dMUV], [dp("P0")])
                ps, pd = P.psum()
                mm(ps[:, :], BKf, T1, True, True, [dC, dp("P0")], [pd])
                tt(T1, ps[:, :], Vv[:, tsl], ALU.mult, [pd, dp("V")], [dp("P0")])
                tt(SQt, SQt, T1, ALU.add, [dp("P0")], [dp("P0")])
                tt(Y[:, j, tsl], SQt, GG[:, tsl], ALU.mult, [dp("P0"), dp("GG")], [dY])

    def mixer_stub(l):
        pass

    mix_fns = {0: globals().get("_mx_rwkv"), 1: mixer_mlstm, 2: mixer_lru, 3: globals().get("_mx_hg")}
    mix_fns[0] = locals().get("mixer_rwkv", mixer_stub)
    mix_fns[3] = locals().get("mixer_hgrn2", mixer_stub)

    def layer(l, sq):
        src = xT[sq] if l == 0 else hd[sq]
        load_layer_small(l)
        norm_mod(src, l, sq, 0)
        P.barrier()
        if l == 0 and sq == 0:
            tap("U", U[:, :, :], [dU], None)
        first = True
        for n in range(4):
            if n in mixers:
                mix_fns[n](l)
                P.barrier()
                if l == 0 and sq == 0:
                    tap("Y%d" % n, Y[:, :, :], [dY], None)
                gate_merge(l, n, first)
                first = False
                P.barrier()
        out_proj_residual(l, sq, src, hd[sq])
        P.barrier()
        norm_mod(hd[sq], l, sq, 1)
        P.barrier()
        ffn(l, sq, hd[sq])
        P.barrier()

    for sq in range(n_seq):
        for l in range(n_layers):
            layer(l, sq)
        norm_mod(hd[sq], -1, sq, 0)
        P.barrier()
    P.finish()
    es.close()
    return nc, P


def make_in_maps(inputs, n_cores=NCORE):
    inp = {k: np.asarray(v) for k, v in inputs.items()}
    pv = pack_pv(inp)
    cst = pack_consts()
    shared = {
        "pv": pv, "cst": cst,
        "ada_w": np.ascontiguousarray(inp["ada_w"], np.float32), "w_in": np.ascontiguousarray(inp["w_in"], np.float32),
        "rw_w2": np.ascontiguousarray(inp["rw_w2"], np.float32), "rw_a2": np.ascontiguousarray(inp["rw_a2"], np.float32),
        "rw_g2": np.ascontiguousarray(inp["rw_g2"], np.float32),
        "lru_wa": pack_lru_bd(inp["lru_wa"]), "lru_wx": pack_lru_bd(inp["lru_wx"]),
        "w_branch": np.ascontiguousarray(inp["w_branch"], np.float32), "w_out": np.ascontiguousarray(inp["w_out"], np.float32),
        "ffn_up": np.ascontiguousarray(inp["ffn_up"], np.float32), "ffn_down": np.ascontiguousarray(inp["ffn_down"], np.float32),
    }
    maps = []
    for i in range(n_cores):
        xs = inp["x"][i * NSEQ:(i + 1) * NSEQ]
        m = dict(shared)
        m["xT"] = np.ascontiguousarray(np.transpose(xs, (0, 2, 1)), np.float32)
        cs = inp["c"][i * NSEQ:(i + 1) * NSEQ].astype(np.float32)
        m["cT"] = np.ascontiguousarray(cs.T.reshape(8, 128, NSEQ).transpose(1, 0, 2))
        maps.append(m)
    return maps


def kernel(**inputs):
    nc, _ = build()
    maps = make_in_maps(inputs)
    res = run_bass_kernel_spmd(nc, maps, core_ids=list(range(NCORE)))
    outs = [np.transpose(r["outT"], (0, 2, 1)) for r in res.results]
    return np.ascontiguousarray(np.concatenate(outs, axis=0), dtype=np.float32)
```

```python
import numpy as np
import concourse.bass as bass
import concourse.mybir as mybir
from concourse.bass_utils import run_bass_kernel_spmd
from contextlib import ExitStack

F32 = mybir.dt.float32
BF16 = mybir.dt.bfloat16
ALU = mybir.AluOpType
AF = mybir.ActivationFunctionType

D = 1024
T = 2048
L = 4
NSEQ = 2
NCORE = 8
W = 512
N_IN = 11024
O_RW, O_ML, O_LRU, O_HG, O_GATE = 0, 1792, 3344, 4368, 6928
DFF = 2816
class Track:
    def __init__(self, sem, step):
        self.sem = sem
        self.val = 0
        self.step = step


class Dep:
    __slots__ = ("w", "r")

    def __init__(self):
        self.w = None
        self.r = {}


class Eng:
    def __init__(self, name, h, tr, is_pe=False):
        self.name = name
        self.h = h
        self.tr = tr
        self.is_pe = is_pe
        self.seen = {}
        self.ops = []
        self.pool = []
        self.dma_i = 0


class Prog:
    def __init__(self, nc, es):
        self.nc = nc
        self.es = es
        self.tracks = []

        def mk(name, step=1):
            t = Track(es.enter_context(nc.semaphore(name)), step)
            self.tracks.append(t)
            return t

        self.pe = Eng("pe", nc.tensor, mk("s_pe"), True)
        self.act = Eng("act", nc.scalar, mk("s_act"))
        self.dve = Eng("dve", nc.vector, mk("s_dve"))
        self.pool = Eng("pool", nc.gpsimd, mk("s_pool"))
        self.sp = Eng("sp", nc.sync, mk("s_sp"))
        self.engs = [self.pe, self.act, self.dve, self.pool, self.sp]
        for e, n in ((self.sp, 16), (self.pool, 16), (self.act, 4)):
            e.pool = [mk("d_%s%d" % (e.name, i), 16) for i in range(n)]
        self.nps = 0
        self.psums = []
        self.n_ops = 0

    def sb(self, name, shape, dt=F32):
        return self.es.enter_context(self.nc.sbuf_tensor(name, list(shape), dt))

    def make_psums(self):
        for i in range(8):
            t = self.es.enter_context(self.nc.psum_tensor("ps%d" % i, [128, 512], F32))
            self.psums.append((t, Dep()))

    def psum(self):
        t = self.psums[self.nps % 8]
        self.nps += 1
        return t

    def _waits(self, eng, reads, writes, extra=()):
        need = {}

        def add(ev):
            if ev is None:
                return
            tr, v = ev
            if need.get(tr, 0) < v:
                need[tr] = v

        for d in reads:
            add(d.w)
        for d in writes:
            add(d.w)
            for tr, v in d.r.items():
                add((tr, v))
        for ev in extra:
            add(ev)
        for tr, v in need.items():
            if tr is eng.tr and eng.is_pe:
                continue
            if eng.seen.get(tr, 0) >= v:
                continue
            eng.seen[tr] = v
            eng.ops.append(("wait", tr.sem, v))

    def op(self, eng, fn, reads=(), writes=(), inc=True):
        self._waits(eng, reads, writes)
        val = eng.tr.val + 1
        if inc:
            eng.tr.val = val
        eng.ops.append(("op", fn, inc))
        self.n_ops += 1
        for d in reads:
            if d.r.get(eng.tr, 0) < val:
                d.r[eng.tr] = val
        for d in writes:
            d.w = (eng.tr, val)
            d.r = {}

    def dma(self, eng, out, in_, reads=(), writes=(), **kw):
        tr = eng.pool[eng.dma_i % len(eng.pool)]
        eng.dma_i += 1
        extra = [(tr, tr.val)] if tr.val > 0 else []
        self._waits(eng, reads, writes, extra)
        tr.val += 16
        eng.ops.append(("dma", out, in_, tr.sem, kw))
        self.n_ops += 1
        for d in reads:
            d.r[tr] = tr.val
        for d in writes:
            d.w = (tr, tr.val)
            d.r = {}

    def barrier(self):
        for e in self.engs:
            for tr in self.tracks:
                if tr.val > e.seen.get(tr, 0):
                    if tr is e.tr and e.is_pe:
                        continue
                    e.seen[tr] = tr.val
                    e.ops.append(("wait", tr.sem, tr.val))

    def finish(self):
        self.barrier()
        nc = self.nc

        def replay(e, h):
            for o in e.ops:
                if o[0] == "wait":
                    h.wait_ge(o[1], o[2])
                elif o[0] == "op":
                    ins = o[1](h)
                    if o[2]:
                        ins.then_inc(e.tr.sem, 1)
                else:
                    h.dma_start(out=o[1], in_=o[2], **o[4]).then_inc(o[3], 16)

        with nc.Block() as block:
            @block.tensor
            def _(h):
                replay(self.pe, h)

            @block.scalar
            def _(h):
                replay(self.act, h)

            @block.vector
            def _(h):
                replay(self.dve, h)

            @block.gpsimd
            def _(h):
                replay(self.pool, h)

            @block.sync
            def _(h):
                replay(self.sp, h)

PV_SPEC = [("ada_b", 48), ("n1g", 8), ("n2g", 8), ("rw_mu", 14), ("rw_w0", 8), ("rw_a0", 8), ("rw_kk", 4),
           ("rw_ka", 4), ("rw_rk", 4), ("rw_lnw", 4), ("rw_lnb", 4), ("ml_norm", 4), ("lru_cw", 16),
           ("lru_cb", 4), ("lru_ba", 8), ("lru_bx", 8), ("lru_lam", 8), ("hg_lb", 8), ("hg_norm", 4),
           ("ffn_cw", 132), ("ffn_cb", 44), ("ml_gb", 16)]
PV_L = sum(n for _, n in PV_SPEC)
PV_OFF = {}
_o = 0
for _n, _c in PV_SPEC:
    PV_OFF[_n] = _o
    _o += _c
NPV = PV_L * L + 8


def _fm(v):
    return np.ascontiguousarray(np.asarray(v, np.float32).reshape(-1, 128).T)


def pack_pv(inp):
    pv = np.zeros((128, NPV), np.float32)
    for l in range(L):
        ent = {
            "ada_b": _fm(inp["ada_b"][l]), "n1g": _fm(inp["norm1_g"][l]), "n2g": _fm(inp["norm2_g"][l]),
            "rw_mu": _fm(inp["rw_mu"][l]), "rw_w0": _fm(inp["rw_w0"][l].reshape(-1)),
            "rw_a0": _fm(inp["rw_a0"][l].reshape(-1)), "rw_kk": _fm(inp["rw_kk"][l]),
            "rw_ka": _fm(inp["rw_ka"][l]), "rw_rk": _fm(inp["rw_rk"][l].reshape(-1)),
            "rw_lnw": _fm(inp["rw_lnw"][l]), "rw_lnb": _fm(inp["rw_lnb"][l]), "ml_norm": _fm(inp["ml_norm"][l]),
            "lru_cw": _fm(inp["lru_conv_w"][l].reshape(-1)), "lru_cb": _fm(inp["lru_conv_b"][l]),
            "lru_ba": _fm(inp["lru_ba"][l].reshape(-1)), "lru_bx": _fm(inp["lru_bx"][l].reshape(-1)),
            "lru_lam": _fm(inp["lru_lam"][l].reshape(-1)), "hg_lb": _fm(inp["hg_lb"][l].reshape(-1)),
            "hg_norm": _fm(inp["hg_norm"][l]), "ffn_cw": _fm(inp["ffn_conv_w"][l].reshape(-1)),
            "ffn_cb": _fm(inp["ffn_conv_b"][l]),
            "ml_gb": np.tile(np.concatenate([inp["ml_ibias"][l].reshape(-1), inp["ml_fbias"][l].reshape(-1)])[None, :],
                             (128, 1)).astype(np.float32),
        }
        for n, c in PV_SPEC:
            assert ent[n].shape == (128, c), (n, ent[n].shape)
            pv[:, l * PV_L + PV_OFF[n]: l * PV_L + PV_OFF[n] + c] = ent[n]
    pv[:, PV_L * L:] = _fm(inp["final_g"])
    return pv


NCST = 20


def pack_consts():
    c = np.zeros((128, NCST, 128), np.float32)
    i = np.arange(128)
    c[:, 0, :] = np.eye(128)
    c[:, 1, :] = 1.0
    c[:, 2, :] = (i[:, None] <= i[None, :])
    c[:, 3, :] = (i[:, None] < i[None, :])
    c[:, 4, :] = (i[:, None] // 64 == i[None, :] // 64)
    c[:, 5, :] = (i[:, None] > i[None, :])
    for k in range(7):
        bsz = 1 << k
        t, s_ = i[:, None], i[None, :]
        m = (t // (2 * bsz) == s_ // (2 * bsz)) & (t % (2 * bsz) >= bsz) & (s_ % (2 * bsz) < bsz)
        c[:, 6 + k, :] = m
        c[:, 13 + k, :] = m.T
    return c


def pack_lru_bd(w):
    out = np.zeros((L, 2, 4, 128, 128), np.float32)
    for j in range(4):
        out[:, :, j, 0:64, 0:64] = w[:, :, 2 * j]
        out[:, :, j, 64:128, 64:128] = w[:, :, 2 * j + 1]
    return out


def build(n_layers=L, n_seq=NSEQ, mixers=(0, 1, 2, 3), taps=(), rw_limit=(4, 2, 16)):
    nc = bass.Bass("TRN2", target_bir_lowering=False)
    dt = lambda name, shape, kind="ExternalInput": nc.dram_tensor(name, list(shape), F32, kind=kind).ap()
    xT = dt("xT", [NSEQ, D, T])
    cT = dt("cT", [128, 8, NSEQ])
    pvd = dt("pv", [128, NPV])
    cst = dt("cst", [128, NCST, 128])
    ada_w = dt("ada_w", [L, D, 6 * D])
    w_in = dt("w_in", [L, D, N_IN])
    rw_w2 = dt("rw_w2", [L, 2, 64, W])
    rw_a2 = dt("rw_a2", [L, 2, 64, W])
    rw_g2 = dt("rw_g2", [L, 128, W])
    lru_wa = dt("lru_wa", [L, 2, 4, 128, 128])
    lru_wx = dt("lru_wx", [L, 2, 4, 128, 128])
    w_branch = dt("w_branch", [L, 4, W, D])
    w_out = dt("w_out", [L, D, D])
    ffn_up = dt("ffn_up", [L, D, 2 * DFF])
    ffn_down = dt("ffn_down", [L, DFF, D])
    outT = dt("outT", [NSEQ, D, T], "ExternalOutput")
    hd = dt("hd", [NSEQ, D, T], "Internal")
    tapd = {}
    for name, shape in taps:
        tapd[name] = dt("tap_" + name, shape, "ExternalOutput")

    es = ExitStack()
    P = Prog(nc, es)
    P.make_psums()
    pe, act, dve, pool, sp = P.pe, P.act, P.dve, P.pool, P.sp

    def mm(out, lhsT, rhs, start, stop, r, w, inc=None):
        P.op(pe, lambda h: h.matmul(out, lhsT, rhs, start=start, stop=stop), r, w, inc=stop if inc is None else inc)

    def actf(out, in_, func, r, w, bias=None, scale=None):
        kw = {}
        if bias is not None:
            kw["bias"] = bias
        if scale is not None:
            kw["scale"] = scale
        P.op(act, lambda h: h.activation(out, in_, func, **kw), r, w)

    def tt(out, a, b, op, r, w, eng=None):
        P.op(eng or dve, lambda h: h.tensor_tensor(out, a, b, op), r, w)

    def ts(out, a, s1, s2, op0, op1, r, w, eng=None):
        if op1 is None:
            P.op(eng or dve, lambda h: h.tensor_scalar(out, a, s1, None, op0), r, w)
        else:
            P.op(eng or dve, lambda h: h.tensor_scalar(out, a, s1, s2, op0, op1), r, w)

    def stt(out, a, s, b, op0, op1, r, w, eng=None):
        P.op(eng or dve, lambda h: h.scalar_tensor_tensor(out, a, s, b, op0, op1), r, w)

    def cp(out, in_, r, w, eng=None):
        e = eng or dve
        if e is act:
            P.op(act, lambda h: h.copy(out, in_), r, w)
        else:
            P.op(e, lambda h: h.tensor_copy(out, in_), r, w)

    def memset(ap, v, w, eng=None):
        P.op(eng or dve, lambda h: h.memset(ap, v), (), w)

    def recip(out, in_, r, w):
        P.op(dve, lambda h: h.reciprocal(out, in_), r, w)

    def scan(out, d0, d1, init, r, w):
        P.op(dve, lambda h: h.tensor_tensor_scan(out, d0, d1, init, ALU.mult, ALU.add), r, w)

    def tap(name, src_ap, r, dst=None):
        if name in tapd:
            P.dma(pool, tapd[name] if dst is None else dst, src_ap, reads=r)

    U = P.sb("U", [128, 8, T], BF16)
    dU = Dep()
    RR = P.sb("RR", [128, 45056], BF16)
    ACC = RR[:, 0:16384].bitcast(F32).rearrange("p (c t) -> p c t", c=4)
    Y = RR[:, 16384:24576].rearrange("p (c t) -> p c t", c=4)
    M = RR[:, 24576:40960].rearrange("p (c t) -> p c t", c=8)
    AFF = RR[:, 0:45056].rearrange("p (c t) -> p c t", c=22)
    dACC, dY, dM, dAFF = Dep(), Dep(), Dep(), Dep()
    SCB = 40960
    SC = P.sb("SC", [128, SCB // 2], BF16)

    def scv(off, n, dtype):
        assert off % 4 == 0
        if dtype is F32:
            assert off + 4 * n <= SCB, (off, n)
            return SC[:, off // 2: off // 2 + 2 * n].bitcast(F32)
        assert off + 2 * n <= SCB, (off, n)
        return SC[:, off // 2: off // 2 + n]

    NSLAB = 3
    slabs = [(P.sb("slab%d" % i, [128, 8, 512], BF16), Dep()) for i in range(NSLAB)]
    slab_i = [0]
    PV = P.sb("PV", [128, NPV], F32)
    dPV = Dep()
    CF = P.sb("CF", [128, 6, 128], F32)
    CB = P.sb("CB", [128, NCST, 128], BF16)
    dC = Dep()
    MOD = P.sb("MOD", [128, L * 48 * NSEQ], F32)
    dMOD = Dep()
    SV = P.sb("SV", [128, L * 32], F32)
    dSV = Dep()
    CND = P.sb("CND", [128, 8, NSEQ], F32)
    CNDB = P.sb("CNDB", [128, 8, NSEQ], BF16)
    dCND = Dep()
    LW = P.sb("LW", [128, 3584], BF16)
    dLW = Dep()
    W2s = LW[0:64, 0:1024].rearrange("p (d n) -> p d n", d=2)
    A2s = LW[64:128, 0:1024].rearrange("p (d n) -> p d n", d=2)
    G2s = LW[:, 1024:1536]
    WAs = LW[:, 1536:2560].rearrange("p (d j n) -> p d j n", d=2, j=4)
    WXs = LW[:, 2560:3584].rearrange("p (d j n) -> p d j n", d=2, j=4)

    IDf, ONf, LEf, LTf, BKf = (CF[:, i, :] for i in range(5))
    IDb, ONb, LEb, LTb, BKb = (CB[:, i, :] for i in range(5))
    GTf = CF[:, 5, :]
    LMb = [CB[:, 6 + k, :] for k in range(7)]
    LMTb = [CB[:, 13 + k, :] for k in range(7)]

    def pv(name, l, c0=0, n=1):
        o = l * PV_L + PV_OFF[name] + c0
        return PV[:, o:o + n]

    def mod(l, k, c, sq):
        o = ((l * 6 + k) * 8 + c) * NSEQ + sq
        return MOD[:, o:o + 1]

    def load_slab(W2d, k0, nkc, col0, ncols, dst_col=0, new=True):
        if new:
            slab_i[0] += 1
        sl, sd = slabs[slab_i[0] % NSLAB]
        src = W2d[k0 * 128:(k0 + nkc) * 128, col0:col0 + ncols].rearrange("(kc p) n -> p kc n", p=128)
        P.dma(pool, sl[:, 0:nkc, dst_col:dst_col + ncols], src, writes=[sd])
        return sl, sd

    def linear_fm(W2d, nkc, col0, ncols, xfn, xdeps, ntok, evac):
        for cg in range(0, ncols, 512):
            n = min(512, ncols - cg)
            sl, sd = load_slab(W2d, 0, nkc, col0 + cg, n)
            for c in range(0, n, 128):
                cw = min(128, n - c)
                for t0 in range(0, ntok, 512):
                    tn = min(512, ntok - t0)
                    ps, pd = P.psum()
                    for kc in range(nkc):
                        mm(ps[0:cw, 0:tn], sl[:, kc, c:c + cw], xfn(kc, t0, tn), kc == 0, kc == nkc - 1,
                           [sd] + xdeps, [pd])
                    evac(ps, pd, cg + c, cw, t0, tn)

    ufn = lambda kc, t0, tn: U[:, kc, t0:t0 + tn]

    P.dma(sp, PV[:], pvd[:, :], writes=[dPV])
    P.dma(sp, CF[:], cst[:, 0:6, :], writes=[dC])
    P.dma(pool, CB[:], cst[:, :, :], writes=[dC])
    P.dma(sp, CND[:], cT[:, :, :], writes=[dCND])
    actf(CND[:], CND[:], AF.Silu, [dCND], [dCND])
    cp(CNDB[:], CND[:], [dCND], [dCND])
    for l in range(n_layers):
        def ev_mod(ps, pd, col, cw, t0, tn, l=l):
            cidx = col // 128
            o = (l * 48 + cidx) * NSEQ
            ts(MOD[:, o:o + NSEQ], ps[:, 0:NSEQ], pv("ada_b", l, cidx), None, ALU.add, None, [pd, dPV], [dMOD])
        linear_fm(ada_w[l], 8, 0, 6 * D, lambda kc, t0, tn: CNDB[:, kc, :], [dCND], NSEQ, ev_mod)
    def sv(l, o, n=1):
        return SV[:, l * 32 + o: l * 32 + o + n]
    TS = scv(0, 64, F32)
    dTS = Dep()
    for l in range(n_layers):
        lam = pv("lru_lam", l, 0, 8)
        actf(TS[:, 0:8], lam, AF.Abs, [dPV], [dTS])
        actf(TS[:, 0:8], TS[:, 0:8], AF.Exp, [dTS], [dTS], scale=-1.0)
        actf(TS[:, 0:8], TS[:, 0:8], AF.Ln, [dTS], [dTS], bias=1.0)
        ts(TS[:, 8:16], lam, -1.0, 0.0, ALU.mult, ALU.max, [dPV], [dTS])
        tt(TS[:, 0:8], TS[:, 0:8], TS[:, 8:16], ALU.add, [dTS], [dTS])
        ts(sv(l, 0, 8), TS[:, 0:8], -8.0, None, ALU.mult, None, [dTS], [dSV])
        ts(sv(l, 8, 8), TS[:, 0:8], -16.0, None, ALU.mult, None, [dTS], [dSV])
    EX = scv(256, 32, F32)
    SM = scv(384, 8, F32)
    dEX = Dep()
    for l in range(L):
        actf(EX[:, l * 8:(l + 1) * 8], pv("hg_lb", l, 0, 8), AF.Exp, [dPV], [dEX])
    tt(SM[:], EX[:, 0:8], EX[:, 8:16], ALU.add, [dEX], [dEX])
    tt(SM[:], SM[:], EX[:, 16:24], ALU.add, [dEX], [dEX])
    tt(SM[:], SM[:], EX[:, 24:32], ALU.add, [dEX], [dEX])
    recip(SM[:], SM[:], [dEX], [dEX])
    for l in range(L):
        tt(EX[:, l * 8:(l + 1) * 8], EX[:, l * 8:(l + 1) * 8], SM[:], ALU.mult, [dEX], [dEX])
    for l in range(n_layers):
        if l == 0:
            memset(sv(0, 16, 8), 0.0, [dSV])
        elif l == 1:
            cp(sv(1, 16, 8), EX[:, 8:16], [dEX], [dSV])
        else:
            tt(sv(l, 16, 8), sv(l - 1, 16, 8), EX[:, l * 8:(l + 1) * 8], ALU.add, [dEX, dSV], [dSV])
    for l in range(n_layers):
        ts(sv(l, 16, 8), sv(l, 16, 8), 0.0, 1.0, ALU.max, ALU.min, [dSV], [dSV])
        ts(sv(l, 24, 8), sv(l, 16, 8), -1.0, 1.0, ALU.mult, ALU.add, [dSV], [dSV])
    P.barrier()

    GS = P.sb("GS", [128, 64], F32)
    dGS = Dep()

    def norm_mod(src, l, sq, which):
        gname, ksh, ksc = ("n1g", 0, 1) if which == 0 else ("n2g", 3, 4)
        if l >= 0:
            for c in range(8):
                stt(GS[:, which * 8 + c: which * 8 + c + 1], mod(l, ksc, c, sq), 1.0, pv(gname, l, c),
                    ALU.add, ALU.mult, [dMOD, dPV], [dGS])
        HT = scv(0, 4096, F32).rearrange("p (c t) -> p c t", c=8)
        SQ = [scv(16384 + i * 2048, 512, F32) for i in range(2)]
        RS = scv(20480, 512, F32)
        TM = [scv(22528 + i * 2048, 512, F32) for i in range(2)]
        dHT, dSQ, dRS, dTM = Dep(), [Dep(), Dep()], Dep(), [Dep(), Dep()]
        srcv = src.rearrange("(c p) t -> p c t", p=128)
        for tq in range(4):
            P.dma(sp, HT[:, :, :], srcv[:, :, tq * 512:(tq + 1) * 512], writes=[dHT])
            ps, pd = P.psum()
            for c in range(8):
                actf(SQ[c % 2][:], HT[:, c, :], AF.Square, [dHT], [dSQ[c % 2]])
                mm(ps[:, :], ONf, SQ[c % 2][:], c == 0, c == 7, [dSQ[c % 2], dC], [pd], inc=True)
            actf(RS[:], ps[:, :], AF.Sqrt, [pd], [dRS], bias=1e-6, scale=1.0 / D)
            recip(RS[:], RS[:], [dRS], [dRS])
            for c in range(8):
                tt(TM[c % 2][:], HT[:, c, :], RS[:], ALU.mult, [dHT, dRS], [dTM[c % 2]])
                if l >= 0:
                    actf(U[:, c, tq * 512:(tq + 1) * 512], TM[c % 2][:], AF.Identity, [dTM[c % 2], dGS, dMOD], [dU],
                         bias=mod(l, ksh, c, sq), scale=GS[:, which * 8 + c: which * 8 + c + 1])
                else:
                    ts(HT[:, c, :], TM[c % 2][:], PV[:, PV_L * L + c: PV_L * L + c + 1], None, ALU.mult, None,
                       [dTM[c % 2], dPV], [dHT])
            if l < 0:
                P.dma(sp, outT[sq].rearrange("(c p) t -> p c t", p=128)[:, :, tq * 512:(tq + 1) * 512], HT[:, :, :],
                      reads=[dHT])

    def gate_merge(l, n, first):
        SG = [scv(i * 2048, 512, F32) for i in range(2)]
        TP = [scv(4096 + i * 2048, 512, F32) for i in range(2)]
        dSG, dTP = [Dep(), Dep()], [Dep(), Dep()]
        k = 0
        for cg in range(2):
            sg_, sgd = load_slab(w_in[l], 0, 8, O_GATE + n * D + cg * 512, 512)
            sb_, sbd = load_slab(w_branch[l, n], 0, 4, cg * 512, 512)
            for c in range(4):
                cc = cg * 4 + c
                for tq in range(4):
                    tsl = slice(tq * 512, (tq + 1) * 512)
                    pg, pgd = P.psum()
                    for kc in range(8):
                        mm(pg[:, :], sg_[:, kc, c * 128:(c + 1) * 128], U[:, kc, tsl], kc == 0, kc == 7, [sgd, dU], [pgd])
                    py, pyd = P.psum()
                    for kc in range(4):
                        mm(py[:, :], sb_[:, kc, c * 128:(c + 1) * 128], Y[:, kc, tsl], kc == 0, kc == 3, [sbd, dY], [pyd])
                    i = k % 2
                    k += 1
                    actf(SG[i][:], pg[:, :], AF.Sigmoid, [pgd], [dSG[i]])
                    if first:
                        tt(M[:, cc, tsl], py[:, :], SG[i][:], ALU.mult, [pyd, dSG[i]], [dM])
                    else:
                        tt(TP[i][:], py[:, :], SG[i][:], ALU.mult, [pyd, dSG[i]], [dTP[i]])
                        tt(M[:, cc, tsl], M[:, cc, tsl], TP[i][:], ALU.add, [dTP[i], dM], [dM])

    def out_proj_residual(l, sq, src, dst):
        HT = scv(0, 4096, F32).rearrange("p (c t) -> p c t", c=8)
        dHT = Dep()
        s0, s0d = load_slab(w_out[l], 0, 8, 0, 512)
        s1, s1d = load_slab(w_out[l], 0, 8, 512, 512)
        srcv = src.rearrange("(c p) t -> p c t", p=128)
        dstv = dst.rearrange("(c p) t -> p c t", p=128)
        for tq in range(4):
            tsl = slice(tq * 512, (tq + 1) * 512)
            P.dma(sp, HT[:, :, :], srcv[:, :, tsl], writes=[dHT])
            for c2 in range(8):
                sl, sd = (s0, s0d) if c2 < 4 else (s1, s1d)
                ps, pd = P.psum()
                for kc in range(8):
                    mm(ps[:, :], sl[:, kc, (c2 % 4) * 128:(c2 % 4 + 1) * 128], M[:, kc, tsl], kc == 0, kc == 7, [sd, dM], [pd])
                stt(HT[:, c2, :], ps[:, :], mod(l, 2, c2, sq), HT[:, c2, :], ALU.mult, ALU.add, [pd, dHT, dMOD], [dHT])
            P.dma(sp, dstv[:, :, tsl], HT[:, :, :], reads=[dHT])

    def ffn(l, sq, hsrc):
        ZV = scv(0, 2050, F32)
        ZG = scv(8208, 2050, F32)
        CV = scv(16416, 2048, F32)
        CG = scv(24608, 2048, F32)
        dZV, dZG, dCV, dCG = Dep(), Dep(), Dep(), Dep()
        memset(ZV[:, 0:1], 0.0, [dZV])
        memset(ZV[:, 2049:2050], 0.0, [dZV])
        memset(ZG[:, 0:1], 0.0, [dZG])
        memset(ZG[:, 2049:2050], 0.0, [dZG])
        up = ffn_up[l]
        for j in range(22):
            sl, sd = load_slab(up, 0, 8, j * 128, 128, 0)
            load_slab(up, 0, 8, DFF + j * 128, 128, 128, new=False)
            for half, Z, dZ in ((0, ZV, dZV), (1, ZG, dZG)):
                for tq in range(4):
                    ps, pd = P.psum()
                    for kc in range(8):
                        mm(ps[:, :], sl[:, kc, half * 128:(half + 1) * 128], U[:, kc, tq * 512:(tq + 1) * 512],
                           kc == 0, kc == 7, [sd, dU], [pd])
                    cp(Z[:, 1 + tq * 512: 1 + (tq + 1) * 512], ps[:, :], [pd], [dZ], eng=act)
            for half, Z, dZ, C, dCx in ((0, ZV, dZV, CV, dCV), (1, ZG, dZG, CG, dCG)):
                jj = half * 22 + j
                cw = lambda tp: pv("ffn_cw", l, tp * 44 + jj)
                ts(C[:], Z[:, 1:2049], cw(1), pv("ffn_cb", l, jj), ALU.mult, ALU.add, [dZ, dPV], [dCx])
                stt(C[:], Z[:, 0:2048], cw(0), C[:], ALU.mult, ALU.add, [dZ, dCx], [dCx])
                stt(C[:], Z[:, 2:2050], cw(2), C[:], ALU.mult, ALU.add, [dZ, dCx], [dCx])
            actf(CG[:], CG[:], AF.Silu, [dCG], [dCG])
            tt(AFF[:, j, :], CV[:], CG[:], ALU.mult, [dCV, dCG], [dAFF])
        P.barrier()
        HT = scv(0, 2048, F32).rearrange("p (c t) -> p c t", c=4)
        dHT = Dep()
        hv = hsrc.rearrange("(c p) t -> p c t", p=128)
        for cg in range(2):
            sls = []
            for pi, (k0, nk) in enumerate(((0, 8), (8, 8), (16, 6))):
                sls.append(load_slab(ffn_down[l], k0, nk, cg * 512, 512) + (k0, nk))
            for tq in range(4):
                tsl = slice(tq * 512, (tq + 1) * 512)
                P.dma(sp, HT[:, :, :], hv[:, cg * 4:(cg + 1) * 4, tsl], writes=[dHT])
                for c2 in range(4):
                    ps, pd = P.psum()
                    for sl, sd, k0, nk in sls:
                        for kc in range(nk):
                            kk = k0 + kc
                            mm(ps[:, :], sl[:, kc, c2 * 128:(c2 + 1) * 128], AFF[:, kk, tsl], kk == 0, kk == 21,
                               [sd, dAFF], [pd])
                    stt(HT[:, c2, :], ps[:, :], mod(l, 5, cg * 4 + c2, sq), HT[:, c2, :], ALU.mult, ALU.add,
                        [pd, dHT, dMOD], [dHT])
                P.dma(sp, hv[:, cg * 4:(cg + 1) * 4, tsl], HT[:, :, :], reads=[dHT])

    def tk(t0, n, d):
        if d == 0:
            return slice(t0, t0 + n)
        a = T - 1 - t0
        b = a - n
        return slice(a, None if b < 0 else b, -1)

    def load_layer_small(l):
        P.dma(pool, W2s, rw_w2[l].rearrange("d p n -> p d n"), writes=[dLW])
        P.dma(pool, A2s, rw_a2[l].rearrange("d p n -> p d n"), writes=[dLW])
        P.dma(pool, G2s, rw_g2[l], writes=[dLW])
        P.dma(pool, WAs, lru_wa[l].rearrange("d j p n -> p d j n"), writes=[dLW])
        P.dma(pool, WXs, lru_wx[l].rearrange("d j p n -> p d j n"), writes=[dLW])

    def mixer_lru(l):
        XP = scv(0, 2051, F32)
        XC = scv(8208, 2048, F32)
        XCB = scv(16400, 2048, BF16)
        Ba = scv(20496, 2048, F32)
        Bm = scv(28688, 2048, F32)
        Bx, Bh, Bg, By = (ACC[:, i, :] for i in range(4))
        dXP, dXC, dXCB, dBa, dBm, dBx, dBh, dBg, dBy = (Dep() for _ in range(9))
        memset(XP[:, 0:1], 0.0, [dXP])
        memset(XP[:, 2049:2051], 0.0, [dXP])
        for j in range(4):
            linear_fm(w_in[l], 8, O_LRU + j * 128, 128, ufn, [dU], T,
                      lambda ps, pd, col, cw, t0, tn: cp(XP[:, 1 + t0:1 + t0 + tn], ps[:, 0:tn], [pd], [dXP], eng=act))
            linear_fm(w_in[l], 8, O_LRU + 512 + j * 128, 128, ufn, [dU], T,
                      lambda ps, pd, col, cw, t0, tn: actf(Bg[:, t0:t0 + tn], ps[:, 0:tn], AF.Gelu, [pd], [dBg]))
            cwv = lambda tp: pv("lru_cw", l, tp * 4 + j)
            ts(XC[:], XP[:, 0:T], cwv(0), pv("lru_cb", l, j), ALU.mult, ALU.add, [dXP, dPV], [dXC])
            for tp in range(1, 4):
                stt(XC[:], XP[:, tp:tp + T], cwv(tp), XC[:], ALU.mult, ALU.add, [dXP, dXC], [dXC])
            cp(XCB[:], XC[:], [dXC], [dXCB])
            for d in range(2):
                for tq in range(4):
                    tsl = slice(tq * 512, (tq + 1) * 512)
                    ps, pd = P.psum()
                    mm(ps[:, :], WAs[:, d, j, :], XCB[:, tsl], True, True, [dLW, dXCB], [pd])
                    actf(Ba[:, tsl], ps[:, :], AF.Sigmoid, [pd, dPV], [dBa], bias=pv("lru_ba", l, d * 4 + j))
                    ps, pd = P.psum()
                    mm(ps[:, :], WXs[:, d, j, :], XCB[:, tsl], True, True, [dLW, dXCB], [pd])
                    actf(Bx[:, tsl], ps[:, :], AF.Sigmoid, [pd, dPV], [dBx], bias=pv("lru_bx", l, d * 4 + j))
                actf(Bm[:], Ba[:], AF.Exp, [dBa, dSV], [dBm], scale=sv(l, 8 + d * 4 + j))
                actf(Ba[:], Ba[:], AF.Exp, [dBa, dSV], [dBa], scale=sv(l, d * 4 + j))
                ts(Bm[:], Bm[:], 1.0, -1.0, ALU.min, ALU.mult, [dBm], [dBm])
                actf(Bm[:], Bm[:], AF.Sqrt, [dBm], [dBm], bias=1.0)
                tt(Bx[:], Bx[:], XC[:], ALU.mult, [dBx, dXC], [dBx])
                tt(Bm[:], Bm[:], Bx[:], ALU.mult, [dBm, dBx], [dBm])
                if d == 0:
                    scan(By[:], Ba[:], Bm[:], 0.0, [dBa, dBm], [dBy])
                else:
                    scan(Bh[:, ::-1], Ba[:, ::-1], Bm[:, ::-1], 0.0, [dBa, dBm], [dBh])
                    tt(By[:], By[:], Bh[:], ALU.add, [dBy, dBh], [dBy])
            tt(Y[:, j, :], By[:], Bg[:], ALU.mult, [dBy, dBg], [dY])

    PADB = RR[:, 40960:45056]

    HN_DEPS = (Dep(), Dep())

    def head_rmsnorm_gate(l, gname, h, gate_mul):
        SQ = PADB[:, 0:1024].bitcast(F32)
        RS = PADB[:, 1024:2048].bitcast(F32)
        dSQ, dRS = HN_DEPS
        for tq in range(4):
            tsl = slice(tq * 512, (tq + 1) * 512)
            actf(SQ[:], ACC[:, h, tsl], AF.Square, [dACC], [dSQ])
            ps, pd = P.psum()
            mm(ps[:, :], ONf, SQ[:], True, True, [dSQ, dC], [pd])
            actf(RS[:], ps[:, :], AF.Sqrt, [pd], [dRS], bias=1e-6, scale=1.0 / 128)
            recip(RS[:], RS[:], [dRS], [dRS])
            tt(SQ[:], ACC[:, h, tsl], RS[:], ALU.mult, [dACC, dRS, dSQ], [dSQ])
            stt(Y[:, h, tsl], SQ[:], pv(gname, l, h), Y[:, h, tsl], ALU.mult, ALU.mult, [dSQ, dPV, dY], [dY])

    def mixer_mlstm(l):
        QF = scv(0, 4096, BF16).rearrange("p (c t) -> p c t", c=2)
        KF = scv(8192, 4096, BF16).rearrange("p (c t) -> p c t", c=2)
        WKV = scv(16384, 6144, BF16).rearrange("p (k n) -> p k n", k=8)
        WGm = scv(28672, 128, BF16).rearrange("p (k n) -> p k n", k=8)
        o = [28928]

        def al(n, dtype):
            v = scv(o[0], n, dtype)
            o[0] += (n * (4 if dtype is F32 else 2) + 3) // 4 * 4
            return v
        KVT = al(768, BF16)
        IG, NLF, NEGC, KS = al(4, F32), al(4, F32), al(4, F32), al(4, F32)
        NLFB, DT, RD, HO = al(128, F32), al(128, F32), al(128, F32), al(128, F32)
        PW, QP = al(128, BF16), al(128, BF16)
        KTx = [al(128, BF16), al(128, BF16)]
        EBE = al(1, F32)
        Cst = al(258, F32).rearrange("p (c n) -> p c n", c=2)
        CBf = al(256, BF16).rearrange("p (c n) -> p c n", c=2)
        NBb = al(256, BF16).rearrange("p (c n) -> p c n", c=2)
        dQF, dKF, dWKV, dKVT, dG, dNLFB, dDT, dRD, dHO, dPW, dQP, dKT, dEBE, dCst, dCB = (Dep() for _ in range(15))
        linear_fm(w_in[l], 8, O_ML, 256, ufn, [dU], T,
                  lambda ps, pd, col, cw, t0, tn: ts(QF[:, col // 128, t0:t0 + tn], ps[:, 0:tn], 0.125, None, ALU.mult, None, [pd], [dQF]))
        linear_fm(w_in[l], 8, O_ML + 256, 256, ufn, [dU], T,
                  lambda ps, pd, col, cw, t0, tn: cp(KF[:, col // 128, t0:t0 + tn], ps[:, 0:tn], [pd], [dKF], eng=act))
        linear_fm(w_in[l], 8, O_ML + 1024, 512, ufn, [dU], T,
                  lambda ps, pd, col, cw, t0, tn: actf(Y[:, col // 128, t0:t0 + tn], ps[:, 0:tn], AF.Sigmoid, [pd], [dY]))
        wv = w_in[l]
        P.dma(pool, WKV[:, :, :], wv[:, O_ML + 256:O_ML + 1024].rearrange("(kc p) n -> p kc n", p=128), writes=[dWKV])
        P.dma(pool, WGm[:, :, :], wv[:, O_ML + 1536:O_ML + 1552].rearrange("(kc p) n -> p kc n", p=128), writes=[dWKV])
        memset(KTx[0][:], 0.0, [dKT])
        memset(KTx[1][:], 0.0, [dKT])
        GB = pv("ml_gb", l, 0, 16)
        URt = al(1024, BF16).rearrange("p (k n) -> p k n", k=8)
        dUR = Dep()
        TMPR = PADB[:, 0:4096].rearrange("p (c t) -> p c t", c=2)
        dTR = Dep()
        for d in range(2):
            if d == 1:
                for BUF, dB in ((QF, dQF), (KF, dKF)):
                    cp(TMPR[:, :, :], BUF[:, :, ::-1], [dB], [dTR])
                    cp(BUF[:, :, :], TMPR[:, :, :], [dTR], [dB])
            memset(Cst[:, :, :], 0.0, [dCst])
            memset(CBf[:, :, :], 0.0, [dCB])
            memset(NBb[:, :, :], 0.0, [dCB])
            for tau in range(16):
                tsl = tk(tau * 128, 128, d)
                pt = slice(tau * 128, tau * 128 + 128)
                if d == 1:
                    cp(URt[:, :, :], U[:, :, tsl], [dU], [dUR])
                    uf = lambda kc: URt[:, kc, :]
                else:
                    uf = lambda kc: U[:, kc, pt]
                for c0, cn in ((0, 512), (512, 256)):
                    ps, pd = P.psum()
                    for kc in range(8):
                        mm(ps[:, 0:cn], uf(kc), WKV[:, kc, c0:c0 + cn], kc == 0, kc == 7, [dU, dWKV, dUR], [pd])
                    cp(KVT[:, c0:c0 + cn], ps[:, 0:cn], [pd], [dKVT], eng=act)
                psg, pgd = P.psum()
                for kc in range(8):
                    mm(psg[:, 0:16], uf(kc), WGm[:, kc, :], kc == 0, kc == 7, [dU, dWKV, dUR], [pgd])
                tt(IG[:], psg[:, d * 4:d * 4 + 4], GB[:, d * 4:d * 4 + 4], ALU.add, [pgd, dPV], [dG])
                tt(NLF[:], psg[:, 8 + d * 4:12 + d * 4], GB[:, 8 + d * 4:12 + d * 4], ALU.add, [pgd, dPV], [dG])
                actf(NLF[:], NLF[:], AF.Exp, [dG], [dG], scale=-1.0)
                actf(NLF[:], NLF[:], AF.Ln, [dG], [dG], bias=1.0)
                psb, pbd = P.psum()
                mm(psb[:, 0:4], LEf, NLF[:], True, True, [dC, dG], [pbd])
                tt(NEGC[:], psb[:, 0:4], IG[:], ALU.add, [pbd, dG], [dG])
                for h in range(4):
                    hp = slice((h % 2) * 64, (h % 2) * 64 + 64)
                    hc = h // 2
                    ts(NLFB[:], ONf, NLF[:, h:h + 1], None, ALU.mult, None, [dC, dG], [dNLFB])
                    bb, bbd = P.psum()
                    mm(bb[:, 0:128], NLFB[:], LEf, True, True, [dNLFB, dC], [bbd])
                    actf(EBE[:], bb[:, 127:128], AF.Exp, [bbd], [dEBE], scale=-1.0)
                    actf(KS[:, h:h + 1], bb[:, 127:128], AF.Exp, [bbd, dG], [dG], scale=-1.0, bias=NEGC[:, h:h + 1])
                    actf(DT[:], bb[:, 0:128], AF.Exp, [bbd, dG], [dDT], scale=-1.0, bias=NEGC[:, h:h + 1])
                    tt(DT[:], DT[:], LEf, ALU.mult, [dDT, dC], [dDT])
                    st, std = P.psum()
                    mm(st[:, 0:128], KF[hp, hc, pt], QF[hp, hc, pt], True, True, [dKF, dQF], [std])
                    tt(PW[:], st[:, 0:128], DT[:], ALU.mult, [std, dDT], [dPW])
                    actf(RD[hp, :], bb[hp, 0:128], AF.Exp, [bbd], [dRD], scale=-1.0)
                    tt(QP[hp, :], QF[hp, hc, pt], RD[hp, :], ALU.mult, [dQF, dRD], [dQP])
                    nu, nud = P.psum()
                    mm(nu[:, 0:128], KVT[:, 256 + h * 128:384 + h * 128], PW[:], True, False, [dKVT, dPW], [nud])
                    mm(nu[:, 0:128], CBf[hp, hc, :], QP[hp, :], False, True, [dCB, dQP], [nud])
                    de, ded = P.psum()
                    mm(de[:, 0:128], ONb, PW[:], True, False, [dC, dPW], [ded])
                    mm(de[:, 0:128], NBb[hp, hc, :], QP[hp, :], False, True, [dCB, dQP], [ded])
                    actf(RD[:], de[:, 0:128], AF.Abs, [ded], [dRD])
                    ts(RD[:], RD[:], 1.0, None, ALU.max, None, [dRD], [dRD])
                    recip(RD[:], RD[:], [dRD], [dRD])
                    if d == 0:
                        tt(ACC[:, h, tsl], nu[:, 0:128], RD[:], ALU.mult, [nud, dRD], [dACC])
                    else:
                        tt(HO[:], nu[:, 0:128], RD[:], ALU.mult, [nud, dRD], [dHO])
                        tt(ACC[:, h, tsl], ACC[:, h, tsl], HO[:], ALU.add, [dHO, dACC], [dACC])
                    kt = KTx[h % 2]
                    ts(kt[:, hp], KVT[:, h * 64:h * 64 + 64], KS[:, h:h + 1], None, ALU.mult, None, [dKVT, dG], [dKT])
                    pc, pcd = P.psum()
                    mm(pc[:, 0:128], kt[:], KVT[:, 256 + h * 128:384 + h * 128], True, True, [dKT, dKVT], [pcd], inc=False)
                    mm(pc[:, 128:129], kt[:], ONb[:, 0:1], True, True, [dKT, dC], [pcd], inc=True)
                    stt(Cst[hp, hc, :], Cst[hp, hc, :], EBE[hp, :], pc[hp, 0:129], ALU.mult, ALU.add, [dCst, dEBE, pcd], [dCst])
                    cp(CBf[hp, hc, :], Cst[hp, hc, 0:128], [dCst], [dCB], eng=act)
                    ts(NBb[hp, hc, :], ONf[hp, :], Cst[hp, hc, 128:129], None, ALU.mult, None, [dCst, dC], [dCB])
        for h in range(4):
            head_rmsnorm_gate(l, "ml_norm", h, True)

    def mixer_hgrn2(l):
        LF = scv(0, 2048, F32)
        G = scv(8192, 2048, F32)
        Kb = scv(16384, 2048, BF16)
        QS = scv(20480, 2048, BF16)
        WI = scv(24576, 1024, BF16).rearrange("p (k n) -> p k n", k=8)
        o = [26624]

        def al(n, dtype):
            v = scv(o[0], n, dtype)
            o[0] += (n * (4 if dtype is F32 else 2) + 3) // 4 * 4
            return v
        GLt, EXt, Sst = (al(128, F32) for _ in range(3))
        QTt, QHt, ATT, VT, KHT, SBb = (al(128, BF16) for _ in range(6))
        TM = al(9 * 128, F32).rearrange("p (k n) -> p k n", k=9)
        KTLa = al(9 * 128, BF16).rearrange("p (k n) -> p k n", k=9)
        EGE = al(1, F32)
        URt = al(1024, BF16).rearrange("p (k n) -> p k n", k=8)
        dUR = Dep()
        dLF, dG, dKb, dQS, dWI, dGL, dEX, dTMP, dEXk, dS, dQT, dQH, dKH, dATT, dVT, dKHT, dSB, dKTL, dEGE = (Dep() for _ in range(19))
        dKTLs = [Dep() for _ in range(8)]
        for h in range(4):
            linear_fm(w_in[l], 8, O_HG + h * 128, 128, ufn, [dU], T,
                      lambda ps, pd, col, cw, t0, tn: actf(QS[:, t0:t0 + tn], ps[:, 0:tn], AF.Silu, [pd], [dQS]))
            linear_fm(w_in[l], 8, O_HG + 2048 + h * 128, 128, ufn, [dU], T,
                      lambda ps, pd, col, cw, t0, tn: actf(Y[:, h, t0:t0 + tn], ps[:, 0:tn], AF.Silu, [pd], [dY]))
            P.dma(pool, WI[:, :, :], w_in[l][:, O_HG + 1536 + h * 128:O_HG + 1664 + h * 128].rearrange("(kc p) n -> p kc n", p=128),
                  writes=[dWI])
            for d in range(2):
                linear_fm(w_in[l], 8, O_HG + 512 + d * 512 + h * 128, 128, ufn, [dU], T,
                          lambda ps, pd, col, cw, t0, tn: actf(LF[:, tk(t0, tn, d)], ps[:, 0:tn], AF.Sigmoid, [pd], [dLF]))
                ts(LF[:], LF[:], sv(l, 24 + d * 4 + h), sv(l, 16 + d * 4 + h), ALU.mult, ALU.add, [dLF, dSV], [dLF])
                ts(Kb[:], LF[:], -1.0, 1.0, ALU.mult, ALU.add, [dLF], [dKb])
                actf(LF[:], LF[:], AF.Ln, [dLF], [dLF])
                for tau in range(16):
                    pt = slice(tau * 128, tau * 128 + 128)
                    scan(G[:, pt], ONf, LF[:, pt], 0.0, [dLF, dC], [dG])
                memset(Sst[:], 0.0, [dS])
                memset(SBb[:], 0.0, [dSB])
                for tau in range(16):
                    t0 = tau * 128
                    pt = slice(t0, t0 + 128)
                    nt = tk(t0, 128, d)
                    Gt = G[:, pt]
                    Gt3 = Gt.rearrange("p (b i) -> p b i", b=8)
                    GL3 = GLt.rearrange("p (b i) -> p b i", b=8)
                    vp, vpd = P.psum()
                    if d == 1:
                        cp(URt[:, :, :], U[:, :, nt], [dU], [dUR])
                    for kc in range(8):
                        mm(vp[:, 0:128], URt[:, kc, :] if d == 1 else U[:, kc, pt], WI[:, kc, :], kc == 0, kc == 7,
                           [dU, dWI, dUR], [vpd])
                    cp(GL3[:, 0, :], Gt3[:, 0, :], [dG], [dGL], eng=pool)
                    tt(GL3[:, 1:8, :], Gt3[:, 1:8, :], Gt3[:, 0:7, 15:16].to_broadcast([128, 7, 16]), ALU.subtract, [dG], [dGL])
                    cp(TM[:, 0, :], Gt, [dG], [dTMP], eng=pool)
                    tt(TM[:, 1:9, :], Gt.unsqueeze(1).to_broadcast([128, 8, 128]),
                       Gt3[:, 0:8, 15:16].to_broadcast([128, 8, 128]), ALU.subtract, [dG], [dTMP])
                    actf(TM[:, :, :], TM[:, :, :], AF.Exp, [dTMP], [dTMP], scale=-1.0)
                    actf(EXt[:], GLt[:], AF.Exp, [dGL], [dEX])
                    stt(KTLa[:, :, :], TM[:, :, :], 1e26, Kb[:, pt].unsqueeze(1).to_broadcast([128, 9, 128]),
                        ALU.min, ALU.mult, [dTMP, dKb], [dKTL])
                    tt(QTt[:], QS[:, nt], EXt[:], ALU.mult, [dQS, dEX], [dQT])
                    actf(EXt[:], Gt, AF.Exp, [dG, dQT], [dEX])
                    actf(EGE[:], Gt[:, 127:128], AF.Exp, [dG], [dEGE])
                    cp(VT[:], vp[:, 0:128], [vpd], [dVT], eng=act)
                    tt(QHt[:], QS[:, nt], EXt[:], ALU.mult, [dQS, dEX], [dQH])
                    at, atd = P.psum()
                    for I in range(8):
                        mm(at[:, 16 * I:16 * I + 16], KTLa[:, I, :], QTt[:, 16 * I:16 * I + 16], True, True, [dKTL, dQT], [atd],
                           inc=(I == 7))
                    KH = KTLa[:, 8, :]
                    tp_, tpd = P.psum()
                    mm(tp_[:, 0:128], KH, IDb, True, True, [dKTL, dC], [tpd])
                    tt(ATT[:], at[:, 0:128], LEf, ALU.mult, [atd, dC], [dATT])
                    cp(KHT[:], tp_[:, 0:128], [tpd], [dKHT], eng=act)
                    op_, opd = P.psum()
                    mm(op_[:, 0:128], VT[:], ATT[:], True, False, [dVT, dATT], [opd])
                    mm(op_[:, 0:128], SBb[:], QHt[:], False, True, [dSB, dQH], [opd])
                    sp_, spd = P.psum()
                    mm(sp_[:, 0:128], KHT[:], VT[:], True, True, [dKHT, dVT], [spd])
                    stt(Sst[:], Sst[:], EGE[:], sp_[:, 0:128], ALU.mult, ALU.add, [dS, dEGE, spd], [dS])
                    cp(SBb[:], Sst[:], [dS], [dSB], eng=act)
                    if d == 0:
                        cp(ACC[:, h, nt], op_[:, 0:128], [opd], [dACC], eng=act)
                    else:
                        tt(ACC[:, h, nt], ACC[:, h, nt], op_[:, 0:128], ALU.add, [opd, dACC], [dACC])
            head_rmsnorm_gate(l, "hg_norm", h, True)

    MUV = P.sb("MUV", [128, 40], F32)
    dMUV = Dep()

    def mixer_rwkv(l):
        deps = {}

        def dp(n):
            if n not in deps:
                deps[n] = Dep()
            return deps[n]
        P0 = scv(0, 2050, F32)
        PF = scv(8208, 2048, F32)
        G = scv(16400, 2048, F32)
        Bb = scv(24592, 2048, BF16)
        KT = [scv(28688, 2048, BF16), scv(32784, 2048, BF16)]
        TXW, XA, SXG, Rr, Vv, KK, Kraw, GG = (M[:, i, :] for i in range(8))
        ts(MUV[:, 0:14], pv("rw_mu", l, 0, 14), -1.0, 1.0, ALU.mult, ALU.add, [dPV], [dMUV])
        ts(MUV[:, 14:28], pv("rw_mu", l, 0, 14), 0.5, None, ALU.mult, None, [dPV], [dMUV])
        ts(MUV[:, 28:32], pv("rw_ka", l, 0, 4), -1.0, 1.0, ALU.mult, ALU.add, [dPV], [dMUV])
        ts(MUV[:, 32:36], pv("rw_rk", l, 0, 4), 0.5, None, ALU.mult, None, [dPV], [dMUV])

        def shifted(ci):
            memset(P0[:, 0:1], 0.0, [dp("P0")])
            memset(P0[:, 2049:2050], 0.0, [dp("P0")])
            linear_fm(w_in[l], 8, O_RW + ci * 128, 128, ufn, [dU], T,
                      lambda ps, pd, col, cw, t0, tn: cp(P0[:, 1 + t0:1 + t0 + tn], ps[:, 0:tn], [pd], [dp("P0")], eng=act))
            tt(PF[:], P0[:, 0:T], P0[:, 2:T + 2], ALU.add, [dp("P0")], [dp("PF")])
            ts(PF[:], PF[:], MUV[:, 14 + ci:15 + ci], None, ALU.mult, None, [dp("PF"), dMUV], [dp("PF")])
            stt(PF[:], P0[:, 1:T + 1], MUV[:, ci:ci + 1], PF[:], ALU.mult, ALU.add, [dp("P0"), dp("PF"), dMUV], [dp("PF")])

        shifted(12)
        actf(TXW[0:64, :], PF[0:64, :], AF.Tanh, [dp("PF")], [dp("TXW")])
        cp(XA[64:128, :], PF[64:128, :], [dp("PF")], [dp("XA")])
        shifted(13)
        actf(SXG[:, :], PF[:], AF.Sigmoid, [dp("PF")], [dp("SXG")])
        SQt = P0[:, 0:512]
        RSt = P0[:, 512:1024]
        T1 = P0[:, 1024:1536]
        HP = [slice(0, 64), slice(64, 128)]
        for j in range(rw_limit[0]):
            P.barrier()
            regions = [(SC, 18440, 20480), (RR, 40960, 45056)] + [(RR, k * 4096, (k + 1) * 4096) for k in range(4) if k != j]
            ri, ro = [0], [regions[0][1]]

            def al(n, dtype=BF16):
                ne = n * (2 if dtype is F32 else 1)
                ne = (ne + 1) // 2 * 2
                while ro[0] + ne > regions[ri[0]][2]:
                    ri[0] += 1
                    ro[0] = regions[ri[0]][1]
                t_, a_ = regions[ri[0]][0], ro[0]
                ro[0] += ne
                v = t_[:, a_:a_ + ne]
                return v.bitcast(F32) if dtype is F32 else v

            def mkset(sid):
                B = {"sid": sid}
                for nm in ("AT", "RT", "BT", "KTt", "BH", "KHh", "VTf", "ATk", "VTk", "BHk", "KHk", "PT"):
                    B[nm] = al(128)
                for nm in ("EXa", "EXb", "EXc", "EXd"):
                    B[nm] = al(128, F32)
                for nm in ("NT0", "WS", "AAK", "ARB", "ARK", "Xb", "Tb", "TTb", "UT"):
                    B[nm] = [al(128), al(128)]
                B["MT"] = [al(128, F32), al(128, F32)]
                B["Sbs"] = al(128)
                B["GC"] = al(64, F32)
                return B
            NSET = 3
            sets = [mkset(i) for i in range(NSET)]
            Sf = al(128, F32)
            for B in sets:
                for b_ in (B["UT"][0], B["UT"][1], B["MT"][0], B["MT"][1], B["Sbs"]):
                    memset(b_[:], 0.0, [dp("misc%d" % B["sid"])], eng=pool)
            shifted(j)
            cp(Rr[:, :], PF[:], [dp("PF")], [dp("R")], eng=act)
            shifted(8 + j)
            cp(Vv[:, :], PF[:], [dp("PF")], [dp("V")], eng=act)
            shifted(4 + j)
            cp(Kraw[:, :], PF[:], [dp("PF")], [dp("Kraw")], eng=act)
            ts(PF[:], PF[:], pv("rw_kk", l, j), None, ALU.mult, None, [dp("PF"), dPV], [dp("PF")])
            for tq in range(4):
                tsl = slice(tq * 512, (tq + 1) * 512)
                actf(SQt, PF[:, tsl], AF.Square, [dp("PF")], [dp("P0")])
                ps, pd = P.psum()
                mm(ps[:, :], BKf, SQt, True, True, [dC, dp("P0")], [pd])
                ts(RSt, ps[:, :], 1e-24, None, ALU.max, None, [pd], [dp("P0")])
                actf(RSt, RSt, AF.Sqrt, [dp("P0")], [dp("P0")])
                recip(RSt, RSt, [dp("P0")], [dp("P0")])
                tt(KK[:, tsl], PF[:, tsl], RSt, ALU.mult, [dp("PF"), dp("P0")], [dp("KK")])
                ps, pd = P.psum()
                mm(ps[:, :], G2s[:, j * 128:(j + 1) * 128], SXG[:, tsl], True, True, [dLW, dp("SXG")], [pd])
                cp(GG[:, tsl], ps[:, :], [pd], [dp("GG")], eng=act)
            AS = P0[:, 0:2048]
            for d in range(rw_limit[1]):
                for tq in range(4):
                    tsl = slice(tq * 512, (tq + 1) * 512)
                    ps, pd = P.psum()
                    mm(ps[:, :], W2s[:, d, j * 128:(j + 1) * 128], TXW[0:64, tsl], True, True, [dLW, dp("TXW")], [pd])
                    actf(PF[:, tk(tq * 512, 512, d)], ps[:, :], AF.Sigmoid, [pd, dPV], [dp("PF")], bias=pv("rw_w0", l, d * 4 + j))
                    ps, pd = P.psum()
                    mm(ps[:, :], A2s[:, d, j * 128:(j + 1) * 128], XA[64:128, tsl], True, True, [dLW, dp("XA")], [pd])
                    actf(AS[:, tk(tq * 512, 512, d)], ps[:, :], AF.Sigmoid, [pd, dPV], [dp("P0")], bias=pv("rw_a0", l, d * 4 + j))
                ts(PF[:], PF[:], -0.6065306597, None, ALU.mult, None, [dp("PF")], [dp("PF")])
                for tau in range(16):
                    pt = slice(tau * 128, tau * 128 + 128)
                    scan(G[:, pt], ONf, PF[:, pt], 0.0, [dp("PF"), dC], [dp("G")])
                rv = slice(None, None, -1) if d == 1 else slice(None)
                dKT = dp("KT%d" % d)
                ts(KT[d][:], AS, pv("rw_ka", l, j), MUV[:, 28 + j:29 + j], ALU.mult, ALU.add, [dp("P0"), dPV, dMUV], [dKT])
                tt(KT[d][:], KT[d][:], Kraw[:, rv], ALU.mult, [dKT, dp("Kraw")], [dKT])
                tt(Bb[:], KK[:, rv], AS, ALU.mult, [dp("KK"), dp("P0")], [dp("Bb")])
                memset(Sf[:], 0.0, [dp("Sf0"), dp("Sf1")])

                def D(B, nm, e=None):
                    return dp("%s%s_%d" % (nm, "" if e is None else str(e), B["sid"]))

                def prep(tau, B):
                    pt = slice(tau * 128, tau * 128 + 128)
                    nt = tk(tau * 128, 128, d)
                    Gt = G[:, pt]
                    actf(B["EXa"][:], Gt, AF.Exp, [dp("G")], [D(B, "EXa")])
                    tt(B["RT"][:], Rr[:, nt], B["EXa"][:], ALU.mult, [dp("R"), D(B, "EXa")], [D(B, "RT")])
                    actf(B["EXb"][:, 1:128], Gt[:, 0:127], AF.Exp, [dp("G")], [D(B, "EXb")])
                    actf(B["EXb"][:, 0:1], Gt[:, 0:1], AF.Exp, [dp("G")], [D(B, "EXb")], scale=0.0)
                    stt(B["AT"][:], KK[:, nt], -1.0, B["EXb"][:], ALU.mult, ALU.mult, [dp("KK"), D(B, "EXb")], [D(B, "AT")])
                    actf(B["EXc"][:], Gt, AF.Exp, [dp("G")], [D(B, "EXc")], scale=-1.0)
                    tt(B["BT"][:], Bb[:, pt], B["EXc"][:], ALU.mult, [dp("Bb"), D(B, "EXc")], [D(B, "BT")])
                    tt(B["KTt"][:], KT[d][:, pt], B["EXc"][:], ALU.mult, [dKT, D(B, "EXc")], [D(B, "KTt")])
                    actf(B["EXd"][:], Gt, AF.Exp, [dp("G")], [D(B, "EXd")], scale=-1.0, bias=Gt[:, 127:128])
                    tt(B["BH"][:], Bb[:, pt], B["EXd"][:], ALU.mult, [dp("Bb"), D(B, "EXd")], [D(B, "BH")])
                    tt(B["KHh"][:], KT[d][:, pt], B["EXd"][:], ALU.mult, [dKT, D(B, "EXd")], [D(B, "KHh")])
                    if d == 1:
                        cp(B["VTf"][:], Vv[:, nt], [dp("V")], [D(B, "VTf")])
                    for sn, dn_ in (("AT", "ATk"), ("VTf", "VTk"), ("BH", "BHk"), ("KHh", "KHk")):
                        ps, pd = P.psum()
                        if sn == "VTf" and d == 0:
                            mm(ps[:, 0:128], Vv[:, pt], IDb, True, True, [dp("V"), dC], [pd])
                        else:
                            mm(ps[:, 0:128], B[sn][:], IDb, True, True, [D(B, sn), dC], [pd])
                        cp(B[dn_][:], ps[:, 0:128], [pd], [D(B, dn_)], eng=act)

                def score(B, lh, ln, rh, rn, mask, dst, dn_):
                    ps, pd = P.psum()
                    mm(ps[:, 0:128], lh, rh, True, True, [D(B, ln), D(B, rn)], [pd])
                    tt(dst[:], ps[:, 0:128], mask, ALU.mult, [pd, dC], [dn_])

                def st_scores(tau, B, e):
                    hp = HP[e]
                    AT, BT, KTt, RT = B["AT"], B["BT"], B["KTt"], B["RT"]
                    score(B, BT[hp, :], "BT", AT[hp, :], "AT", LTf, B["NT0"][e], D(B, "NT", e))
                    score(B, AT[hp, :], "AT", BT[hp, :], "BT", GTf, B["WS"][e], D(B, "WS", e))
                    score(B, KTt[hp, :], "KTt", AT[hp, :], "AT", LTf, B["AAK"][e], D(B, "AAK", e))
                    score(B, BT[hp, :], "BT", RT[hp, :], "RT", LEf, B["ARB"][e], D(B, "ARB", e))
                    score(B, KTt[hp, :], "KTt", RT[hp, :], "RT", LEf, B["ARK"][e], D(B, "ARK", e))

                def st_x0(tau, B, e):
                    hp, oc = HP[e], HP[1 - e]
                    ps, pd = P.psum()
                    mm(ps[:, 0:64], B["AAK"][e][:], B["VTk"][:, hp], True, True, [D(B, "AAK", e), D(B, "VTk")], [pd])
                    cp(B["Xb"][e][:, oc], ps[:, 0:64], [pd], [D(B, "Xb", e)], eng=act)
                    cp(B["Xb"][e][:, hp], B["ATk"][:, hp], [D(B, "ATk")], [D(B, "Xb", e)])

                def st_lvl0(tau, B, e):
                    Tb, TTb = B["Tb"][e], B["TTb"][e]
                    tt(Tb[:], B["WS"][e][:], LMb[0], ALU.mult, [D(B, "WS", e), dC], [D(B, "T", e)])
                    tt(Tb[:], Tb[:], IDb, ALU.add, [D(B, "T", e), dC], [D(B, "T", e)])
                    tt(TTb[:], B["NT0"][e][:], LMTb[0], ALU.mult, [D(B, "NT", e), dC], [D(B, "TT", e)], eng=pool)
                    tt(TTb[:], TTb[:], IDb, ALU.add, [D(B, "TT", e), dC], [D(B, "TT", e)])

                def mk_lvlA(k):
                    def st(tau, B, e):
                        ps, pd = P.psum()
                        mm(ps[:, 0:128], B["NT0"][e][:], B["Tb"][e][:], True, True, [D(B, "NT", e), D(B, "T", e)], [pd])
                        tt(B["WS"][e][:], ps[:, 0:128], LMb[k], ALU.mult, [pd, dC], [D(B, "WS", e)])
                    return st

                def mk_lvlB(k):
                    def st(tau, B, e):
                        Tb, TTb, WS = B["Tb"][e], B["TTb"][e], B["WS"][e]
                        dT, dTT, dWS = D(B, "T", e), D(B, "TT", e), D(B, "WS", e)
                        pz, pzd = P.psum()
                        mm(pz[:, 0:128], IDb, Tb[:], True, False, [dC, dT], [pzd])
                        mm(pz[:, 0:128], TTb[:], WS[:], False, True, [dTT, dWS], [pzd])
                        pt_, ptd = P.psum()
                        mm(pt_[:, 0:128], WS[:], TTb[:], True, True, [dTT, dWS], [ptd])
                        cp(Tb[:], pz[:, 0:128], [pzd], [dT], eng=act)
                        tt(TTb[:], TTb[:], pt_[:, 0:128], ALU.add, [ptd, dTT], [dTT])
                    return st

                def st_apply(tau, B, e):
                    ps, pd = P.psum()
                    mm(ps[:, 0:128], B["TTb"][e][:], B["Xb"][e][:], True, True, [D(B, "TT", e), D(B, "Xb", e)], [pd])
                    cp(B["Xb"][e][:], ps[:, 0:128], [pd], [D(B, "Xb", e)], eng=act)

                def st_mt(tau, B, e):
                    hp = HP[e]
                    ps, pd = P.psum()
                    mm(ps[:, 0:64], B["Xb"][e][:], B["BHk"][:, hp], True, True, [D(B, "Xb", e), D(B, "BHk")], [pd])
                    stt(B["MT"][e][hp, hp], IDf[hp, hp], B["EXa"][hp, 127:128], ps[hp, 0:64], ALU.mult, ALU.add,
                        [pd, dC, D(B, "EXa")], [D(B, "MT", e)])

                def st_gc(tau, B, e):
                    hp, oc = HP[e], HP[1 - e]
                    ps, pd = P.psum()
                    mm(ps[:, 0:64], B["BHk"][:], B["Xb"][e][:, oc], True, False, [D(B, "Xb", e), D(B, "BHk")], [pd])
                    mm(ps[:, 0:64], B["KHk"][:], B["VTk"][:, hp], False, True, [D(B, "KHk"), D(B, "VTk")], [pd])
                    cp(B["GC"][hp, :], ps[hp, 0:64], [pd], [D(B, "GC", e)], eng=act)

                def st_pt(tau, B, e):
                    hp = HP[e]
                    ps, pd = P.psum()
                    mm(ps[:, 0:128], B["Xb"][e][:], IDb, True, True, [D(B, "Xb", e), dC], [pd])
                    cp(B["PT"][hp, :], ps[hp, 0:128], [pd], [D(B, "PT", e)], eng=act)

                def st_u(tau, B, e):
                    hp, oc = HP[e], HP[1 - e]
                    ps, pd = P.psum()
                    mm(ps[:, 0:64], B["PT"][hp, :], B["Sbs"][hp, hp], True, True, [D(B, "PT", e), D(B, "Sbs", e)], [pd])
                    tt(B["UT"][e][:, hp], ps[:, 0:64], B["Xb"][e][:, oc], ALU.add, [pd, D(B, "Xb", e)], [D(B, "UT", e)])

                def st_y(tau, B, e):
                    hp = HP[e]
                    nt = tk(tau * 128, 128, d)
                    ps, pd = P.psum()
                    mm(ps[:, 0:128], B["Sbs"][hp, :], B["RT"][hp, :], True, False, [D(B, "Sbs", e), D(B, "RT")], [pd])
                    mm(ps[:, 0:128], B["UT"][e][:], B["ARB"][e][:], False, False, [D(B, "UT", e), D(B, "ARB", e)], [pd])
                    mm(ps[:, 0:128], B["VTk"][:], B["ARK"][e][:], False, True, [D(B, "VTk"), D(B, "ARK", e)], [pd])
                    if d == 0:
                        cp(ACC[hp, j, nt], ps[hp, 0:128], [pd], [dACC], eng=act)
                    else:
                        tt(ACC[hp, j, nt], ACC[hp, j, nt], ps[hp, 0:128], ALU.add, [pd, dACC], [dACC])

                def st_chain(tau, B, e):
                    hp = HP[e]
                    cp(B["Sbs"][hp, hp], Sf[hp, hp], [dp("Sf%d" % e)], [D(B, "Sbs", e)], eng=act)
                    ps, pd = P.psum()
                    mm(ps[:, 0:64], B["MT"][e][hp, :], Sf[hp, hp], True, True, [D(B, "MT", e), dp("Sf%d" % e)], [pd])
                    tt(Sf[hp, hp], ps[hp, 0:64], B["GC"][hp, :], ALU.add, [pd, D(B, "GC", e), dp("Sf%d" % e)], [dp("Sf%d" % e)])

                indep = [st_scores, st_x0, st_lvl0]
                for k in range(1, 7):
                    indep += [mk_lvlA(k), mk_lvlB(k)]
                indep += [st_apply, st_mt, st_gc, st_pt]
                ntile = rw_limit[2]
                for tau0 in range(0, ntile, NSET):
                    ctx = [(tau0 + i, sets[i]) for i in range(min(NSET, ntile - tau0))]
                    for tau, B in ctx:
                        prep(tau, B)
                    for stg in indep:
                        for tau, B in ctx:
                            for e in range(2):
                                stg(tau, B, e)
                    for tau, B in ctx:
                        for e in range(2):
                            st_chain(tau, B, e)
                    for stg in (st_u, st_y):
                        for tau, B in ctx:
                            for e in range(2):
                                stg(tau, B, e)
            P.barrier()
            for tq in range(4):
                tsl = slice(tq * 512, (tq + 1) * 512)
                ps, pd = P.psum()
                mm(ps[:, :], BKf, ACC[:, j, tsl], True, True, [dC, dACC], [pd])
                stt(SQt, ps[:, :], -1.0 / 64, ACC[:, j, tsl], ALU.mult, ALU.add, [pd, dACC], [dp("P0")])
                actf(RSt, SQt, AF.Square, [dp("P0")], [dp("P0")])
                ps, pd = P.psum()
                mm(ps[:, :], BKf, RSt, True, True, [dC, dp("P0")], [pd])
                actf(RSt, ps[:, :], AF.Sqrt, [pd], [dp("P0")], bias=64e-5, scale=1.0 / 64)
                recip(RSt, RSt, [dp("P0")], [dp("P0")])
                tt(SQt, SQt, RSt, ALU.mult, [dp("P0")], [dp("P0")])
                ts(SQt, SQt, pv("rw_lnw", l, j), pv("rw_lnb", l, j), ALU.mult, ALU.add, [dp("P0"), dPV], [dp("P0")])
                tt(T1, KT[0][:, tsl], KT[1][:, tk(tq * 512, 512, 1)], ALU.add, [dp("KT0"), dp("KT1")], [dp("P0")])
                tt(T1, T1, Rr[:, tsl], ALU.mult, [dp("P0"), dp("R")], [dp("P0")])
                ts(T1, T1, MUV[:, 32 + j:33 + j], None, ALU.mult, None, [dp("P0"), dMUV], [dp("P0")])
                ps, pd = P.psum()
                mm(ps[:, :], BKf, T1, True, True, [dC, dp("P0")], [pd])
                tt(T1, ps[:, :], Vv[:, tsl], ALU.mult, [pd, dp("V")], [dp("P0")])
                tt(SQt, SQt, T1, ALU.add, [dp("P0")], [dp("P0")])
                tt(Y[:, j, tsl], SQt, GG[:, tsl], ALU.mult, [dp("P0"), dp("GG")], [dY])

    def mixer_stub(l):
        pass

    mix_fns = {0: globals().get("_mx_rwkv"), 1: mixer_mlstm, 2: mixer_lru, 3: globals().get("_mx_hg")}
    mix_fns[0] = locals().get("mixer_rwkv", mixer_stub)
    mix_fns[3] = locals().get("mixer_hgrn2", mixer_stub)

    def layer(l, sq):
        src = xT[sq] if l == 0 else hd[sq]
        load_layer_small(l)
        norm_mod(src, l, sq, 0)
        P.barrier()
        if l == 0 and sq == 0:
            tap("U", U[:, :, :], [dU], None)
        first = True
        for n in range(4):
            if n in mixers:
                mix_fns[n](l)
                P.barrier()
                if l == 0 and sq == 0:
                    tap("Y%d" % n, Y[:, :, :], [dY], None)
                gate_merge(l, n, first)
                first = False
                P.barrier()
        out_proj_residual(l, sq, src, hd[sq])
        P.barrier()
        norm_mod(hd[sq], l, sq, 1)
        P.barrier()
        ffn(l, sq, hd[sq])
        P.barrier()

    for sq in range(n_seq):
        for l in range(n_layers):
            layer(l, sq)
        norm_mod(hd[sq], -1, sq, 0)
        P.barrier()
    P.finish()
    es.close()
    return nc, P


def make_in_maps(inputs, n_cores=NCORE):
    inp = {k: np.asarray(v) for k, v in inputs.items()}
    pv = pack_pv(inp)
    cst = pack_consts()
    shared = {
        "pv": pv, "cst": cst,
        "ada_w": np.ascontiguousarray(inp["ada_w"], np.float32), "w_in": np.ascontiguousarray(inp["w_in"], np.float32),
        "rw_w2": np.ascontiguousarray(inp["rw_w2"], np.float32), "rw_a2": np.ascontiguousarray(inp["rw_a2"], np.float32),
        "rw_g2": np.ascontiguousarray(inp["rw_g2"], np.float32),
        "lru_wa": pack_lru_bd(inp["lru_wa"]), "lru_wx": pack_lru_bd(inp["lru_wx"]),
        "w_branch": np.ascontiguousarray(inp["w_branch"], np.float32), "w_out": np.ascontiguousarray(inp["w_out"], np.float32),
        "ffn_up": np.ascontiguousarray(inp["ffn_up"], np.float32), "ffn_down": np.ascontiguousarray(inp["ffn_down"], np.float32),
    }
    maps = []
    for i in range(n_cores):
        xs = inp["x"][i * NSEQ:(i + 1) * NSEQ]
        m = dict(shared)
        m["xT"] = np.ascontiguousarray(np.transpose(xs, (0, 2, 1)), np.float32)
        cs = inp["c"][i * NSEQ:(i + 1) * NSEQ].astype(np.float32)
        m["cT"] = np.ascontiguousarray(cs.T.reshape(8, 128, NSEQ).transpose(1, 0, 2))
        maps.append(m)
    return maps


def kernel(**inputs):
    nc, _ = build()
    maps = make_in_maps(inputs)
    res = run_bass_kernel_spmd(nc, maps, core_ids=list(range(NCORE)))
    outs = [np.transpose(r["outT"], (0, 2, 1)) for r in res.results]
    return np.ascontiguousarray(np.concatenate(outs, axis=0), dtype=np.float32)
```
